# Optimizing a Trainium2 kernel written in Bass

```python
import jax, jax.numpy as jnp
from jax import lax
import numpy as np

D_MODEL = 1024
BATCH = 16
SEQ = 2048
DEPTH = 2
DEC_BATCH = 8
DEC_SEQ = 32
PAST_LEN = 2048

CHUNK = 64
W_ATTN = D_MODEL // 4
W_CONV = D_MODEL // 4
W_GMLP = D_MODEL // 4
W_POOL = D_MODEL // 4
MIX_WIDTH = W_ATTN + W_CONV + W_GMLP + W_POOL
HEAD_DIM = 64
N_Q_HEADS = W_ATTN // HEAD_DIM
N_KV_HEADS = 2
Q_PER_KV = N_Q_HEADS // N_KV_HEADS
KV_WIDTH = N_KV_HEADS * HEAD_DIM
WINDOW = 128
WINDOW_CHUNKS = WINDOW // CHUNK
ROPE_THETA = 500000.0
ROT_DIM = HEAD_DIM // 4
CONV_WIDTH = 3
GMLP_CHUNK = 128
GMLP_GROUPS = 4
GMLP_GROUP_DIM = W_GMLP // GMLP_GROUPS
POOL_WINDOWS = (2, 4, 8, 16)
POOL_GROUP_DIM = W_POOL // len(POOL_WINDOWS)
POOL_HIST = max(POOL_WINDOWS) - 1
D_FF = 2816
FFN_CONV_WIDTH = 3
NORM_EPS = 1e-6
PROJ_WIDTH = W_ATTN + 2 * KV_WIDTH + 3 * W_CONV + 2 * W_GMLP + W_POOL

kernel_name = 'hybrid_streaming_encoder_step'


def split_points():
    sizes = (W_ATTN, KV_WIDTH, KV_WIDTH, W_CONV, W_CONV, W_CONV, W_GMLP, W_GMLP, W_POOL)
    pts, acc = [], 0
    for s in sizes[:-1]:
        acc += s
        pts.append(acc)
    return pts


def rmsnorm(x, g):
    xf = x.astype(jnp.float32)
    y = xf * lax.rsqrt(jnp.mean(xf * xf, axis=-1, keepdims=True) + NORM_EPS)
    return (y * g.astype(jnp.float32)).astype(x.dtype)


def layernorm(x, g, b):
    xf = x.astype(jnp.float32)
    mu = jnp.mean(xf, axis=-1, keepdims=True)
    var = jnp.mean(jnp.square(xf - mu), axis=-1, keepdims=True)
    y = (xf - mu) * lax.rsqrt(var + NORM_EPS) * g.astype(jnp.float32) + b.astype(jnp.float32)
    return y.astype(x.dtype)


def partial_rope(x, pos):
    half = ROT_DIM // 2
    inv_freq = jnp.power(jnp.float32(ROPE_THETA), -jnp.arange(half, dtype=jnp.float32) * (2.0 / ROT_DIM))
    ang = pos.astype(jnp.float32)[:, None] * inv_freq[None, :]
    cos = jnp.cos(ang)[None, :, None, :]
    sin = jnp.sin(ang)[None, :, None, :]
    xf = x.astype(jnp.float32)
    x1 = xf[..., :half]
    x2 = xf[..., half:ROT_DIM]
    out = jnp.concatenate([x1 * cos - x2 * sin, x2 * cos + x1 * sin, xf[..., ROT_DIM:]], axis=-1)
    return out.astype(x.dtype)


def sink_attention(q, k, v, sink, mask):
    s = jnp.einsum('...qkrd,...skd->...krqs', q, k).astype(jnp.float32) * (HEAD_DIM ** -0.5)
    if mask is not None:
        s = jnp.where(mask, s, -jnp.inf)
    sk = sink.astype(jnp.float32)[:, :, None, None]
    m = jnp.maximum(jnp.max(s, axis=-1, keepdims=True), sk)
    p = jnp.exp(s - m)
    w = p / (jnp.sum(p, axis=-1, keepdims=True) + jnp.exp(sk - m))
    return jnp.einsum('...krqs,...skd->...qkrd', w.astype(v.dtype), v)


def window_attention_prompt(q, k, v, sink):
    bsz, t_len = q.shape[:2]
    nc = t_len // CHUNK
    pad = WINDOW_CHUNKS * CHUNK
    kp = jnp.pad(k, ((0, 0), (pad, 0), (0, 0), (0, 0)))
    vp = jnp.pad(v, ((0, 0), (pad, 0), (0, 0), (0, 0)))
    band = [(j * CHUNK, j * CHUNK + t_len) for j in range(WINDOW_CHUNKS + 1)]
    kb = jnp.concatenate([kp[:, a:b].reshape(bsz, nc, CHUNK, N_KV_HEADS, HEAD_DIM) for a, b in band], axis=2)
    vb = jnp.concatenate([vp[:, a:b].reshape(bsz, nc, CHUNK, N_KV_HEADS, HEAD_DIM) for a, b in band], axis=2)
    key_pos = (jnp.arange(nc)[:, None] - WINDOW_CHUNKS) * CHUNK + jnp.arange((WINDOW_CHUNKS + 1) * CHUNK)[None, :]
    mask = (key_pos >= 0)[:, None, None, None, :]
    qb = q.reshape(bsz, nc, CHUNK, N_KV_HEADS, Q_PER_KV, HEAD_DIM)
    o = sink_attention(qb, kb, vb, sink.reshape(N_KV_HEADS, Q_PER_KV), mask)
    return o.reshape(bsz, t_len, W_ATTN)


def window_attention_sample(q, k, v, k_past, v_past, sink):
    bsz, t_len = q.shape[:2]
    kk = jnp.concatenate([k_past, k], axis=1)
    vv = jnp.concatenate([v_past, v], axis=1)
    qb = q.reshape(bsz, t_len, N_KV_HEADS, Q_PER_KV, HEAD_DIM)
    o = sink_attention(qb, kk, vv, sink.reshape(N_KV_HEADS, Q_PER_KV), None)
    return o.reshape(bsz, t_len, W_ATTN)


def causal_dwconv(x, past, w):
    width = w.shape[0]
    t_len = x.shape[1]
    ext = jnp.concatenate([past, x], axis=1)
    y = ext[:, 0:t_len] * w[0]
    for j in range(1, width):
        y = y + ext[:, j:j + t_len] * w[j]
    return y, ext[:, -(width - 1):]


def spatial_gating(u, v, ln_g, ln_b, w_s, b_s, first_chunk):
    u = jax.nn.gelu(u)
    v = layernorm(jax.nn.gelu(v), ln_g, ln_b)
    bsz, t_len, _ = v.shape
    ii = jnp.arange(GMLP_CHUNK)
    mask = (ii[None, :] // CHUNK) <= (ii[:, None] // CHUNK)
    wm = jnp.where(mask[None], w_s, jnp.zeros_like(w_s))
    if first_chunk:
        vb = v.reshape(bsz, t_len, GMLP_GROUPS, GMLP_GROUP_DIM)
        s = jnp.einsum('gij,bjgd->bigd', wm[:, :t_len, :t_len], vb) + b_s[:, :t_len].T[None, :, :, None]
    else:
        nc = t_len // GMLP_CHUNK
        vb = v.reshape(bsz, nc, GMLP_CHUNK, GMLP_GROUPS, GMLP_GROUP_DIM)
        s = jnp.einsum('gij,bcjgd->bcigd', wm, vb) + b_s.T[None, None, :, :, None]
    return u * s.reshape(bsz, t_len, W_GMLP), v


def multiscale_pool(p, past, pos0, w_pool, scale):
    bsz, t_len, _ = p.shape
    ext_in = jnp.concatenate([past, p], axis=1)
    ext = ext_in.astype(jnp.float32)
    cs = jnp.concatenate([jnp.zeros((bsz, 1, W_POOL), jnp.float32), jnp.cumsum(ext, axis=1)], axis=1)
    pos = pos0 + jnp.arange(t_len)
    outs = []
    for g, win in enumerate(POOL_WINDOWS):
        sl = slice(g * POOL_GROUP_DIM, (g + 1) * POOL_GROUP_DIM)
        wsum = cs[:, POOL_HIST + 1:POOL_HIST + 1 + t_len, sl] - cs[:, POOL_HIST + 1 - win:POOL_HIST + 1 - win + t_len, sl]
        cnt = jnp.minimum(pos + 1, win).astype(jnp.float32)[None, :, None]
        outs.append(wsum / cnt - ext[:, POOL_HIST:, sl])
    pooled = jnp.stack(outs, axis=2).astype(p.dtype)
    mixed = jnp.einsum('btgc,gcd->btgd', pooled, w_pool).reshape(bsz, t_len, W_POOL)
    return mixed * scale, ext_in[:, -POOL_HIST:]


def trunk_layer(x, lp, pos0, past):
    bsz, t_len, _ = x.shape
    h = rmsnorm(x, lp['g_mix'])
    z = h @ lp['w_in']
    q, k, v, c_b, c_c, c_h, g_u, g_v, p_in = jnp.split(z, split_points(), axis=-1)
    pos = pos0 + jnp.arange(t_len)
    q = partial_rope(q.reshape(bsz, t_len, N_Q_HEADS, HEAD_DIM), pos)
    k = partial_rope(k.reshape(bsz, t_len, N_KV_HEADS, HEAD_DIM), pos)
    v = v.reshape(bsz, t_len, N_KV_HEADS, HEAD_DIM)
    if past is None:
        o_attn = window_attention_prompt(q, k, v, lp['sink'])
        k_state, v_state = k[:, -WINDOW:], v[:, -WINDOW:]
        conv_past = jnp.zeros((bsz, CONV_WIDTH - 1, W_CONV), x.dtype)
        pool_past = jnp.zeros((bsz, POOL_HIST, W_POOL), x.dtype)
        ffn_past = jnp.zeros((bsz, FFN_CONV_WIDTH - 1, 2 * D_FF), x.dtype)
    else:
        k_past, v_past, conv_past, pool_past, ffn_past = past
        o_attn = window_attention_sample(q, k, v, k_past, v_past, lp['sink'])
        k_state, v_state = k, v
    zc, conv_state = causal_dwconv(c_c * c_h, conv_past, lp['conv_w'])
    o_conv = c_b * zc
    o_gmlp, gmlp_rows = spatial_gating(g_u, g_v, lp['gmlp_ln_g'], lp['gmlp_ln_b'], lp['gmlp_w'], lp['gmlp_b'], past is not None)
    o_pool, pool_state = multiscale_pool(p_in, pool_past, pos0, lp['pool_w'], lp['pool_scale'])
    x = x + jnp.concatenate([o_attn, o_conv, o_gmlp, o_pool], axis=-1) @ lp['w_out']
    up = rmsnorm(x, lp['g_ffn']) @ lp['w_up']
    up, ffn_state = causal_dwconv(up, ffn_past, lp['ffn_conv_w'])
    gate, val = jnp.split(up, 2, axis=-1)
    x = x + (jax.nn.silu(gate) * val) @ lp['w_down']
    return x, (k_state, v_state, conv_state, pool_state, ffn_state, gmlp_rows)


def setup_inputs(seed: int = 0) -> dict:
    key = jax.random.key(seed)
    ks = jax.random.split(key, 24)
    f32 = jnp.float32

    def nrm(k, shape, s):
        return jax.random.normal(k, shape, f32) * s

    return {
        'x_prompt': nrm(ks[0], (BATCH, SEQ, D_MODEL), 1.0),
        'x_sample': nrm(ks[1], (DEC_BATCH, DEC_SEQ, D_MODEL), 1.0),
        'cache_attn_k': nrm(ks[2], (DEC_BATCH, DEPTH, WINDOW, N_KV_HEADS, HEAD_DIM), 1.0),
        'cache_attn_v': nrm(ks[3], (DEC_BATCH, DEPTH, WINDOW, N_KV_HEADS, HEAD_DIM), 1.0),
        'state_conv': nrm(ks[4], (DEC_BATCH, DEPTH, CONV_WIDTH - 1, W_CONV), 1.0),
        'state_pool': nrm(ks[5], (DEC_BATCH, DEPTH, POOL_HIST, W_POOL), 1.0),
        'state_ffn_conv': nrm(ks[6], (DEC_BATCH, DEPTH, FFN_CONV_WIDTH - 1, 2 * D_FF), 1.0),
        'g_mix': 1.0 + nrm(ks[7], (DEPTH, D_MODEL), 0.01),
        'w_in': nrm(ks[8], (DEPTH, D_MODEL, PROJ_WIDTH), D_MODEL ** -0.5),
        'attn_sink': nrm(ks[9], (DEPTH, N_Q_HEADS), 0.5),
        'conv_w': nrm(ks[10], (DEPTH, CONV_WIDTH, W_CONV), CONV_WIDTH ** -0.5),
        'gmlp_ln_g': 1.0 + nrm(ks[11], (DEPTH, W_GMLP), 0.01),
        'gmlp_ln_b': nrm(ks[12], (DEPTH, W_GMLP), 0.01),
        'gmlp_w': nrm(ks[13], (DEPTH, GMLP_GROUPS, GMLP_CHUNK, GMLP_CHUNK), GMLP_CHUNK ** -0.5),
        'gmlp_b': nrm(ks[14], (DEPTH, GMLP_GROUPS, GMLP_CHUNK), 0.01),
        'pool_w': nrm(ks[15], (DEPTH, len(POOL_WINDOWS), POOL_GROUP_DIM, POOL_GROUP_DIM), POOL_GROUP_DIM ** -0.5),
        'pool_scale': 1.0 + nrm(ks[16], (DEPTH, W_POOL), 0.01),
        'w_out': nrm(ks[17], (DEPTH, MIX_WIDTH, D_MODEL), MIX_WIDTH ** -0.5),
        'g_ffn': 1.0 + nrm(ks[18], (DEPTH, D_MODEL), 0.01),
        'w_up': nrm(ks[19], (DEPTH, D_MODEL, 2 * D_FF), D_MODEL ** -0.5),
        'ffn_conv_w': nrm(ks[20], (DEPTH, FFN_CONV_WIDTH, 2 * D_FF), FFN_CONV_WIDTH ** -0.5),
        'w_down': nrm(ks[21], (DEPTH, D_FF, D_MODEL), D_FF ** -0.5),
        'g_final': 1.0 + nrm(ks[22], (D_MODEL,), 0.01),
    }


def reference(x_prompt, x_sample, cache_attn_k, cache_attn_v, state_conv, state_pool, state_ffn_conv,
              g_mix, w_in, attn_sink, conv_w, gmlp_ln_g, gmlp_ln_b, gmlp_w, gmlp_b, pool_w, pool_scale,
              w_out, g_ffn, w_up, ffn_conv_w, w_down, g_final):
    def run(x, pos0, pasts):
        per_layer = []
        for l in range(DEPTH):
            lp = {'g_mix': g_mix[l], 'w_in': w_in[l], 'sink': attn_sink[l], 'conv_w': conv_w[l],
                  'gmlp_ln_g': gmlp_ln_g[l], 'gmlp_ln_b': gmlp_ln_b[l], 'gmlp_w': gmlp_w[l], 'gmlp_b': gmlp_b[l],
                  'pool_w': pool_w[l], 'pool_scale': pool_scale[l], 'w_out': w_out[l], 'g_ffn': g_ffn[l],
                  'w_up': w_up[l], 'ffn_conv_w': ffn_conv_w[l], 'w_down': w_down[l]}
            past = None if pasts is None else tuple(a[:, l] for a in pasts)
            x, st = trunk_layer(x, lp, pos0, past)
            per_layer.append(st)
        states = [jnp.stack([st[i] for st in per_layer], axis=1) for i in range(6)]
        return rmsnorm(x, g_final), states

    y_prompt, sp = run(x_prompt, 0, None)
    y_sample, ss = run(x_sample, PAST_LEN, (cache_attn_k, cache_attn_v, state_conv, state_pool, state_ffn_conv))
    return (y_prompt, y_sample, sp[0], sp[1], sp[2], sp[3], sp[4], ss[0], ss[1], ss[2], ss[3], ss[4], ss[5])
```

```python
import numpy as np
from contextlib import ExitStack
import concourse.bass as bass
import concourse.mybir as mybir
from concourse.bass_utils import run_bass_kernel_spmd

F32 = mybir.dt.float32
BF16 = mybir.dt.bfloat16
AF = mybir.ActivationFunctionType
ALU = mybir.AluOpType

NCORES = 8
D = 1024
SEQ = 2048
DEPTH = 2
DEC_SEQ = 32
PAST = 2048
DFF = 2816
NJ = DFF // 128
U = 512
NUNIT = SEQ // U
EPS = 1e-6
NCOLW = 19 * 128
WSLOT = 3072
NSMALL = 6
NBIG = 0

C_GMIX, C_GFFN, C_CONV, C_FCW, C_PSC, C_SINK = 0, 8, 16, 22, 22 + 132, 22 + 132 + 2
C_LAYER = 22 + 132 + 4
C_INVW = 2 * C_LAYER
C_RC = C_INVW + 2
C_EPS = C_RC + 32
C_EPSN = C_EPS + 1
NCOLS = C_EPSN + 1


class _Stop(Exception):
    pass


_DBG = {"stop": None, "n": 0, "log": []}


def chk(name):
    _DBG["n"] += 1
    _DBG["log"].append((_DBG["n"], name))
    if _DBG["stop"] is not None and _DBG["n"] >= _DBG["stop"]:
        raise _Stop(name)


class Sched:
    ENGS = ("pe", "act", "dve", "pool", "sp")
    LAT = 0.10
    DMA_FIXED = 2.0
    DMA_BW = 280e3

    def __init__(self):
        self.ops = []
        self.prog = {e: [] for e in self.ENGS}
        self.reorder = True

    def op(self, eng, fn, reads=(), writes=(), n=512, dur=None, tset=None):
        if dur is None:
            if eng == "pe":
                dur = 0.03 + max(64, n) / 2400.0
            elif eng == "act":
                dur = 0.17 + n / 1200.0
            elif eng == "dve":
                dur = 0.13 + n / 960.0
            else:
                dur = 0.3 + n / 500.0
        import sys as _sys
        self.ops.append(dict(kind="op", eng=eng, fn=fn, reads=list(reads), writes=list(writes), dur=dur, tset=tset,
                             line=_sys._getframe(1).f_lineno))

    def dma(self, eng, fn, semkey, reads=(), writes=(), is_output=False, nbytes=65536):
        issue = 1.1 if eng == "pool" else 0.12
        self.ops.append(dict(kind="dma", eng=eng, fn=fn, semkey=semkey, reads=list(reads), writes=list(writes),
                             dur=issue, nbytes=nbytes, is_output=is_output))

    def _analyse(self):
        ops = self.ops
        last_w = {}
        readers = {}
        for i, o in enumerate(ops):
            preds = {}

            def add(j, typ):
                if j is None or j == i:
                    return
                if typ == "raw" or j not in preds:
                    preds[j] = typ if preds.get(j) != "raw" else "raw"
            for b in o["reads"]:
                add(last_w.get(b), "raw")
                if isinstance(b, tuple) and b[0] == "ps":
                    for r in readers.get(b, ()):
                        if ops[r]["eng"] != o["eng"]:
                            add(r, "excl")
            for b in o["writes"]:
                add(last_w.get(b), "waw")
                for r in readers.get(b, ()):
                    add(r, "war")
            for b in o["reads"]:
                readers.setdefault(b, []).append(i)
            for b in o["writes"]:
                last_w[b] = i
                readers[b] = []
            o["preds"] = preds

    def _schedule(self):
        import heapq
        ops = self.ops
        n = len(ops)
        succs = [[] for _ in range(n)]
        indeg = [0] * n
        for i, o in enumerate(ops):
            indeg[i] = len(o["preds"])
            for j in o["preds"]:
                succs[j].append(i)
        tail = [0.0] * n
        for i in range(n - 1, -1, -1):
            o = ops[i]
            d = o["dur"] + (self.DMA_FIXED + o["nbytes"] / self.DMA_BW if o["kind"] == "dma" else 0.0)
            m = 0.0
            for k in succs[i]:
                if tail[k] > m:
                    m = tail[k]
            tail[i] = d + m
        ready = {e: [] for e in self.ENGS}
        for i in range(n):
            if indeg[i] == 0:
                heapq.heappush(ready[ops[i]["eng"]], i)
        t_eng = {e: 0.0 for e in self.ENGS}
        finish = [0.0] * n
        order = {e: [] for e in self.ENGS}
        dma_free = 0.0
        cur_set = None
        done = 0
        K = 16 if self.reorder else 1
        DELTA = 0.25
        while done < n:
            best = None
            for e in self.ENGS:
                h = ready[e]
                if not h:
                    continue
                cands = heapq.nsmallest(K, h)
                for i in cands:
                    est = t_eng[e]
                    for j in ops[i]["preds"]:
                        f = finish[j] + (self.LAT if ops[j]["eng"] != e or ops[j]["kind"] == "dma" else 0.0)
                        if f > est:
                            est = f
                    ts_ = ops[i].get("tset")
                    if e == "act" and ts_ is not None and ts_ != cur_set:
                        est += 1.3
                    if self.reorder:
                        key = (est if est > t_eng[e] + DELTA else t_eng[e], -tail[i], i)
                    else:
                        key = (i, est)
                    if best is None or key < best[0]:
                        best = (key, e, i, est)
            _, e, i, est = best
            ready[e].remove(i)
            heapq.heapify(ready[e])
            o = ops[i]
            if o["kind"] == "dma":
                t_eng[e] = est + o["dur"]
                st = max(t_eng[e], dma_free)
                dma_free = st + o["nbytes"] / self.DMA_BW
                finish[i] = dma_free + self.DMA_FIXED
            else:
                t_eng[e] = est + o["dur"]
                finish[i] = t_eng[e]
                if e == "act" and o.get("tset") is not None:
                    cur_set = o["tset"]
            order[e].append(i)
            o["t0"] = est
            done += 1
            for k in succs[i]:
                indeg[k] -= 1
                if indeg[k] == 0:
                    heapq.heappush(ready[ops[k]["eng"]], k)
        self.order = order
        self.est_total = max(finish) if n else 0.0

    def finish(self):
        self._analyse()
        self._schedule()
        ops = self.ops
        tok = [None] * len(ops)
        pos = [0] * len(ops)
        dcnt = {}
        for e in self.ENGS:
            c = 0
            for i in self.order[e]:
                o = ops[i]
                if o["kind"] == "dma":
                    dcnt[o["semkey"]] = dcnt.get(o["semkey"], 0) + 16
                    tok[i] = (o["semkey"], dcnt[o["semkey"]])
                else:
                    c += 1
                    tok[i] = ("E_" + e, c)
                    pos[i] = c
        out_tokens = {}
        for e in self.ENGS:
            seen = {}
            own = "E_" + e
            c = 0
            prog = self.prog[e]
            for i in self.order[e]:
                o = ops[i]
                for j, typ in o["preds"].items():
                    pj = ops[j]
                    t = tok[j]
                    if pj["kind"] != "dma" and pj["eng"] == e:
                        if not (e in ("act", "dve", "pool") and o["kind"] == "op"):
                            continue
                    if seen.get(t[0], 0) < t[1]:
                        prog.append(("wait", t[0], t[1]))
                        seen[t[0]] = t[1]
                if o["kind"] == "dma":
                    prog.append(("ins", o["fn"], o["semkey"], 16))
                    if o["is_output"]:
                        out_tokens[o["semkey"]] = max(out_tokens.get(o["semkey"], 0), tok[i][1])
                else:
                    c += 1
                    prog.append(("ins", o["fn"], own, 1))
            if e == "sp":
                self._sp_seen = seen
        for k, v in out_tokens.items():
            if self._sp_seen.get(k, 0) < v:
                self.prog["sp"].append(("wait", k, v))


def _rope_tables(pos):
    half = 8
    inv_freq = np.power(np.float32(500000.0), -np.arange(half, dtype=np.float32) * np.float32(2.0 / 16)).astype(np.float32)
    ang = pos.astype(np.float32)[:, None] * inv_freq[None, :]
    cos = np.cos(ang).astype(np.float32).T
    sin = np.sin(ang).astype(np.float32).T
    T = pos.shape[0]
    ct = np.ones((128, T), np.float32)
    st = np.zeros((128, T), np.float32)
    for hh in range(2):
        b = hh * 64
        ct[b:b + 8] = cos
        ct[b + 8:b + 16] = cos
        st[b:b + 8] = -sin
        st[b + 8:b + 16] = sin
    return np.stack([ct, st], 0)


def _head_cols(base, h, swap):
    c = np.arange(64) + base + 64 * h
    if swap:
        c = np.concatenate([c[8:16], c[0:8], c[16:]])
    return c


def _prep_shared(inp):
    w_in = inp["w_in"]
    qb, kb = 0, 256
    cols = []
    for sw in (False, True):
        cols += [_head_cols(qb, 0, sw), _head_cols(qb, 2, sw)]
        cols += [_head_cols(qb, 1, sw), _head_cols(qb, 3, sw)]
        cols += [_head_cols(kb, 0, sw), _head_cols(kb, 1, sw)]
    cols.append(np.arange(512, 1280))
    cols.append(np.arange(1280, 1536))
    cols.append(np.arange(1792, 2048))
    cols.append(np.arange(384, 512))
    cols.append(np.arange(1536, 1792))
    cols = np.concatenate(cols)
    assert cols.shape[0] == NCOLW
    w_in_p = np.ascontiguousarray(w_in[:, :, cols])
    rperm = np.concatenate([np.arange(0, 64), np.arange(128, 192), np.arange(64, 128), np.arange(192, 256), np.arange(256, 1024)])
    w_out_p = np.ascontiguousarray(inp["w_out"][:, rperm, :])

    colt = np.zeros((128, NCOLS), np.float32)
    for l in range(DEPTH):
        o = l * C_LAYER
        colt[:, o + C_GMIX:o + C_GMIX + 8] = inp["g_mix"][l].reshape(8, 128).T
        colt[:, o + C_GFFN:o + C_GFFN + 8] = inp["g_ffn"][l].reshape(8, 128).T
        cw = inp["conv_w"][l]
        for hf in range(2):
            colt[:, o + C_CONV + hf * 3:o + C_CONV + hf * 3 + 3] = cw[:, hf * 128:(hf + 1) * 128].T
        fw = inp["ffn_conv_w"][l]
        colt[:, o + C_FCW:o + C_FCW + 132] = fw.reshape(3, 44, 128).transpose(2, 1, 0).reshape(128, 132)
        colt[:, o + C_PSC:o + C_PSC + 2] = inp["pool_scale"][l].reshape(2, 128).T
        sk = inp["attn_sink"][l]
        colt[0:64, o + C_SINK] = sk[0]
        colt[64:128, o + C_SINK] = sk[2]
        colt[0:64, o + C_SINK + 1] = sk[1]
        colt[64:128, o + C_SINK + 1] = sk[3]
    wins = (2.0, 4.0, 8.0, 16.0)
    for hf in range(2):
        for q in range(2):
            win = wins[hf * 2 + q]
            colt[q * 64:(q + 1) * 64, C_INVW + hf] = np.float32(1.0) / np.float32(win)
            cnt = np.minimum(np.arange(16) + 1, win).astype(np.float32)
            colt[q * 64:(q + 1) * 64, C_RC + hf * 16:C_RC + hf * 16 + 16] = (np.float32(1.0) / cnt)[None, :]
    colt[:, C_EPS] = EPS
    colt[:, C_EPSN] = EPS * 1024

    bc = np.zeros((128, 2 * 2 * 256 + 1024), np.float32)
    for l in range(DEPTH):
        bc[:, (l * 2 + 0) * 256:(l * 2 + 1) * 256] = inp["gmlp_ln_g"][l][None, :]
        bc[:, (l * 2 + 1) * 256:(l * 2 + 2) * 256] = inp["gmlp_ln_b"][l][None, :]
    bc[:, 1024:] = inp["g_final"][None, :]

    wmT = np.ascontiguousarray(inp["gmlp_w"].transpose(3, 0, 1, 2))
    bsr = np.ascontiguousarray(inp["gmlp_b"].reshape(1, DEPTH * 4 * 128))
    wpb = np.zeros((128, DEPTH, 2, 128), np.float32)
    for l in range(DEPTH):
        for g in range(4):
            hf, q = g // 2, g % 2
            wpb[q * 64:(q + 1) * 64, l, hf, q * 64:(q + 1) * 64] = inp["pool_w"][l, g]
    return {
        "w_in_p": w_in_p, "w_out_p": w_out_p, "w_up": np.ascontiguousarray(inp["w_up"]),
        "w_down": np.ascontiguousarray(inp["w_down"]),
        "colt": colt, "bc": bc, "wmT": wmT.reshape(128, DEPTH * 4 * 128), "bsr": bsr,
        "wpb": wpb.reshape(128, DEPTH * 2 * 128),
        "ident": np.eye(128, dtype=np.float32),
        "tab_p": _rope_tables(np.arange(SEQ)), "tab_s": _rope_tables(PAST + np.arange(DEC_SEQ)),
    }


def build_program():
    nc = bass.Bass("TRN2", target_bir_lowering=False)
    S = Sched()

    def din(name, shape):
        return nc.dram_tensor(name, list(shape), F32, kind="ExternalInput").ap()

    def dout(name, shape):
        return nc.dram_tensor(name, list(shape), F32, kind="ExternalOutput").ap()

    x_p = din("x_p", (2, SEQ, D))
    x_s = din("x_s", (1, DEC_SEQ, D))
    ck = din("ck", (DEPTH, 128, 128))
    cv = din("cv", (DEPTH, 128, 128))
    st_conv = din("st_conv", (DEPTH, 2, 256))
    st_pool = din("st_pool", (DEPTH, 15, 256))
    st_ffn = din("st_ffn", (DEPTH, 2, 2 * DFF))
    w_in_p = din("w_in_p", (DEPTH, D, NCOLW))
    w_out_p = din("w_out_p", (DEPTH, D, D))
    w_up = din("w_up", (DEPTH, D, 2 * DFF))
    w_down = din("w_down", (DEPTH, DFF, D))
    colt_d = din("colt", (128, NCOLS))
    bc_d = din("bc", (128, 2048))
    wmT_d = din("wmT", (128, DEPTH * 4 * 128))
    bsr_d = din("bsr", (1, DEPTH * 4 * 128))
    wpb_d = din("wpb", (128, DEPTH * 2 * 128))
    ident_d = din("ident", (128, 128))
    tab_p = din("tab_p", (2, 128, SEQ))
    tab_s = din("tab_s", (2, 128, DEC_SEQ))

    NIMG_L = 52
    wimg = nc.dram_tensor("wimg", [DEPTH * NIMG_L, 128, 2048], BF16, kind="Internal").ap()

    y_p = dout("y_p", (2, SEQ, D))
    y_s = dout("y_s", (1, DEC_SEQ, D))
    o_pk = dout("o_pk", (2, DEPTH, 128, 128))
    o_pv = dout("o_pv", (2, DEPTH, 128, 128))
    o_pconv = dout("o_pconv", (2, DEPTH, 4, 128))
    o_ppool = dout("o_ppool", (2, DEPTH, 30, 128))
    o_pffn = dout("o_pffn", (2, DEPTH, 88, 128))
    o_sk = dout("o_sk", (1, DEPTH, DEC_SEQ, 128))
    o_sv = dout("o_sv", (1, DEPTH, DEC_SEQ, 128))
    o_sconv = dout("o_sconv", (1, DEPTH, 4, 128))
    o_spool = dout("o_spool", (1, DEPTH, 30, 128))
    o_sffn = dout("o_sffn", (1, DEPTH, 88, 128))
    o_sgv = dout("o_sgv", (1, DEPTH, DEC_SEQ, 256))

    es = ExitStack()
    with es:
        def sb(name, shape, dt=F32):
            n = 1
            for d_ in shape[1:]:
                n *= d_
            _DBG["sbuf"] = _DBG.get("sbuf", 0) + n * (2 if dt == BF16 else 4)
            return es.enter_context(nc.sbuf_tensor(name, list(shape), dt))

        xTs = [sb(f"xT{i}", (128, 8, U)) for i in range(2)]
        hn = sb("hn", (128, 8, U), BF16)
        mixT = sb("mixT", (128, 8, U), BF16)
        act = sb("act", (128, NJ, U), BF16)
        wsS = [sb(f"wsS{i}", (128, 2048), BF16) for i in range(NSMALL)]
        wsB = [sb(f"wsB{i}", (128, WSLOT), BF16) for i in range(NBIG)]
        state_unit = [0]
        xs = [sb(f"xs{i}", (128, D)) for i in range(2)]
        yo = [sb(f"yo{i}", (128, D)) for i in range(2)]
        tabc = [sb(f"tabc{i}", (128, U)) for i in range(1)]
        tabs = [sb(f"tabs{i}", (128, U)) for i in range(1)]
        sq = [sb(f"sq{i}", (128, U), BF16) for i in range(2)]
        SC = [sb(f"scr{i}", (128, 528)) for i in range(12)]
        qab = [sb(f"qab{i}", (128, U), BF16) for i in range(2)]
        kT = [sb(f"kT{l}", (128, 128 + U), BF16) for l in range(DEPTH)]
        vtm = [sb(f"vtm{l}", (128, 1 + U // 128, 128), BF16) for l in range(DEPTH)]
        Pb = [[[sb(f"P{s}{h}{k}", (128, 128), BF16) for k in range(2)] for h in range(2)] for s in range(2)]
        rec = [sb(f"rec{i}", (128, 128)) for i in range(2)]
        rec2 = [sb(f"recb{i}", (128, 128)) for i in range(2)]
        prod = [sb(f"prod{l}", (128, 2, 2 + U)) for l in range(DEPTH)]
        ubuf = sb("ubuf", (128, 2, U))
        gv = [sb(f"gv{i}", (128, 256)) for i in range(2)]
        vn = [sb(f"vn{i}", (128, 256)) for i in range(2)]
        vlnb = [sb(f"vlnb{i}", (128, 256), BF16) for i in range(2)]
        vlnf = sb("vlnf", (128, 256))
        st6 = [sb(f"st6{i}", (128, 6)) for i in range(2)]
        mv = [sb(f"mv{i}", (128, 2)) for i in range(2)]
        lnr = [sb(f"lnr{i}", (128, 2)) for i in range(2)]
        lnr2 = [sb(f"lnrb{i}", (128, 2)) for i in range(2)]
        pin = [sb(f"pin{l}", (128, 2, 15 + U)) for l in range(DEPTH)]
        pooled = [sb(f"pooled{i}", (128, U), BF16) for i in range(2)]
        ptmp = sb("ptmp", (128, 16))
        fhalo = [sb(f"fhalo{l}", (128, 2, 2, NJ)) for l in range(DEPTH)]
        ub2 = [sb(f"ub2{i}", (128, 2, 516)) for i in range(2)]
        cstg = sb("cstg", (128, 2, 2))
        pstg = sb("pstg", (128, 15, 2))
        stg = [sb(f"stg{i}", (128, 128)) for i in range(3)]
        fin_ss = [sb(f"fss{i}", (128, 4)) for i in range(2)]
        fin_r = [sb(f"finr{i}", (128, 2)) for i in range(2)]
        colt = sb("coltS", (128, NCOLS))
        esink = sb("esink", (128, 2 * DEPTH))
        bct = sb("bcS", (128, 2048))
        ident = sb("identS", (128, 128))
        ones_bf = sb("ones_bf", (128, 128), BF16)
        wmT = sb("wmTS", (128, DEPTH * 4 * 128), BF16)
        bsr = sb("bsrS", (1, DEPTH * 4 * 128), BF16)
        wpb = sb("wpbS", (128, DEPTH * 2 * 128), BF16)
        tsc = sb("tsc", (128, DEC_SEQ))
        tss = sb("tss", (128, DEC_SEQ))
        ckst = sb("ckst", (128, 128))
        cvst = sb("cvst", (128, 128))
        banks = [es.enter_context(nc.psum_tensor(f"ps{i}", [128, 512], F32)) for i in range(8)]

        semnames = {}
        state = {"bank": 0, "wsS": 0, "wsB": 0, "stg": 0, "yo": 0, "wseq": 0, "wl": 0, "conv": True}

        def next_bank():
            i = state["bank"]
            state["bank"] = (i + 1) % 8
            return i

        def setup_load(dst_t, dst_ap, src_ap, eng="sp"):
            S.dma(eng, lambda e, o=dst_ap, i=src_ap: e.dma_start(out=o, in_=i), "d_" + dst_t, writes=[dst_t])

        setup_load("colt", colt[:], colt_d)
        setup_load("bct", bct[:], bc_d)
        setup_load("ident", ident[:], ident_d)
        setup_load("tsc", tsc[:], tab_s[0])
        setup_load("tss", tss[:], tab_s[1])
        setup_load("wmT", wmT[:], wmT_d, eng="pool")
        setup_load("bsr", bsr[:], bsr_d, eng="pool")
        setup_load("wpb", wpb[:], wpb_d, eng="pool")
        S.op("dve", lambda e: e.memset(ones_bf[:], 1.0), writes=["ones"])
        for s_ in range(2):
            for h_ in range(2):
                for k_ in range(2):
                    S.op("dve", lambda e, t=Pb[s_][h_][k_]: e.memset(t[:], 0.0), writes=[("P", s_, h_, k_)])
        for l in range(DEPTH):
            o = l * C_LAYER + C_SINK
            S.op("act", lambda e, l=l, o=o: e.activation(out=esink[:, 2 * l:2 * l + 2], in_=colt[:, o:o + 2], func=AF.Exp),
                 reads=["colt"], writes=[("esink", l)])

        def col(l, base, i=0, n=1):
            o = l * C_LAYER + base + i
            return colt[:, o:o + n]

        def _slot(n):
            kind = "wsB" if n > 2048 else "wsS"
            cnt = NBIG if kind == "wsB" else NSMALL
            i = state[kind]
            state[kind] = (i + 1) % cnt
            t = (wsB if kind == "wsB" else wsS)[i]
            return t, [(kind, i), (kind + "b", i)], f"d_{kind}{i}"

        def _img_index():
            idx = state["wl"] * NIMG_L + state["wseq"]
            state["wseq"] += 1
            assert state["wseq"] <= NIMG_L
            return idx

        def _img_store(t, n, key, idx, si):
            S.dma("sp", lambda e: e.dma_start(out=wimg[idx][:, 0:n], in_=t[:, 0:n]), f"d_wst{si}",
                  reads=key, writes=[("wimg", idx)], nbytes=n * 128 * 2)

        def _img_load(t, n, key, idx, semk):
            S.dma("sp", lambda e: e.dma_start(out=t[:, 0:n], in_=wimg[idx][:, 0:n]), semk + "i",
                  reads=[("wimg", idx)], writes=key, nbytes=n * 128 * 2)

        def wload(src_ap, view):
            n = 1
            for d_ in view:
                n *= d_
            t, key, semk = _slot(n)
            idx = _img_index()
            dst = t[:, 0:n].rearrange("p (k n) -> p k n", k=view[0])
            if state["conv"]:
                S.dma("pool", lambda e, o=dst, s=src_ap: e.dma_start(out=o, in_=s), semk, writes=key, nbytes=n * 128 * 4)
                _img_store(t, n, key, idx, semk)
            else:
                _img_load(t, n, key, idx, semk)
            return dst, key

        def wload_up(l, j):
            t, key, semk = _slot(2048)
            idx = _img_index()
            dst = t[:, 0:2048].rearrange("p (h k n) -> p h k n", h=2, k=8)
            if state["conv"]:
                for h in range(2):
                    src = w_up[l][:, h * DFF + j * 128:h * DFF + (j + 1) * 128].rearrange("(k p) n -> p k n", p=128)
                    S.dma("pool", lambda e, o=dst[:, h, :, :], s_=src: e.dma_start(out=o, in_=s_), semk + ("" if h == 0 else "h"), writes=[key[h]], nbytes=1024 * 128 * 4)
                _img_store(t, 2048, key, idx, semk)
            else:
                _img_load(t, 2048, key, idx, semk)
            return dst, key

        def mm_group(bank_i, out_fn, lhs_list, rhs_list, reads, extra_writes=(), ncols=512):
            def fn(e):
                n = len(lhs_list)
                ins = None
                for k in range(n):
                    ins = e.matmul(out_fn(), lhsT=lhs_list[k], rhs=rhs_list[k], start=(k == 0), stop=(k == n - 1))
                return ins
            S.op("pe", fn, reads=reads, writes=[("ps", bank_i)] + list(extra_writes),
                 dur=len(lhs_list) * (0.005 + max(64, ncols) / 2400.0))

        def rmsnorm(l, gbase, nt, tw, xT, XK):
            b = next_bank()
            for c in range(8):
                s_ = sq[c % 2]
                S.op("act", lambda e, c=c, s_=s_: e.activation(out=s_[:, 0:tw], in_=xT[:, c, 0:tw], func=AF.Square),
                     reads=[XK(c)], writes=[("sq", c % 2)])
                S.op("pe", lambda e, c=c, s_=s_, b=b: e.matmul(banks[b][:, 0:tw], lhsT=ones_bf[:, :], rhs=s_[:, 0:tw],
                                                             start=(c == 0), stop=(c == 7)),
                     reads=[("sq", c % 2), "ones"], writes=[("ps", b)])
            S.op("act", lambda e, b=b: e.activation(out=SC[0][:, 0:tw], in_=banks[b][:, 0:tw], func=AF.Ln,
                                                    bias=colt[:, C_EPS:C_EPS + 1], scale=1.0 / D),
                 reads=[("ps", b), "colt"], writes=["S0"], tset="lnexp")
            S.op("act", lambda e: e.activation(out=SC[1][:, 0:tw], in_=SC[0][:, 0:tw], func=AF.Exp, scale=-0.5),
                 reads=["S0"], writes=["S1"], tset="lnexp")
            for c in range(8):
                S.op("dve", lambda e, c=c: e.scalar_tensor_tensor(out=hn[:, c, 0:tw], in0=xT[:, c, 0:tw],
                                                                 scalar=col(l, gbase, c), in1=SC[1][:, 0:tw],
                                                                 op0=ALU.mult, op1=ALU.mult),
                     reads=[XK(c), "S1", "colt"], writes=[("hn", c)])

        HN = [("hn", c) for c in range(8)]

        def mm_split(b, lhs_list, tw, wkey):
            for k in range(8):
                S.op("pe", lambda e, b=b, k=k: e.matmul(banks[b][:, 0:tw], lhsT=lhs_list[k], rhs=hn[:, k, 0:tw],
                                                        start=(k == 0), stop=(k == 7)),
                     reads=[("hn", k)] + wkey, writes=[("ps", b)], dur=0.005 + max(64, tw) / 2400.0)

        def proj_fm(wv, wkey, ccol, tw, split=False):
            b = next_bank()
            if split:
                mm_split(b, [wv[:, k, ccol * 128:(ccol + 1) * 128] for k in range(8)], tw, wkey)
                return b
            mm_group(b, lambda b=b: banks[b][:, 0:tw],
                     [wv[:, k, ccol * 128:(ccol + 1) * 128] for k in range(8)],
                     [hn[:, k, 0:tw] for k in range(8)], reads=HN + wkey, ncols=tw)
            return b

        def state_out_T(src_ap, nrows, rkeys, dst_ap):
            b = next_bank()
            S.op("pe", lambda e, b=b: e.transpose(out=banks[b][0:nrows, 0:128], in_=src_ap, identity=ident[:, :]),
                 reads=list(rkeys) + ["ident"], writes=[("ps", b)])
            si = state["stg"]
            state["stg"] = (si + 1) % 3
            S.op("act", lambda e, b=b, si=si: e.activation(out=stg[si][0:nrows, :], in_=banks[b][0:nrows, 0:128], func=AF.Copy),
                 reads=[("ps", b)], writes=[("stg", si)])
            S.dma("sp", lambda e, si=si: e.dma_start(out=dst_ap, in_=stg[si][0:nrows, :]), f"d_stg{si}",
                  reads=[("stg", si)], is_output=True)

        def run_unit(kind, b_idx, ui, next_loads):
            sample = (kind == "s")
            Uu = DEC_SEQ if sample else U
            tw = Uu
            nblk = 1 if sample else U // 128
            ntok = DEC_SEQ if sample else 128
            first = (not sample) and ui == 0
            last = sample or ui == NUNIT - 1
            xsrc = x_s if sample else x_p
            ydst = y_s if sample else y_p
            t0 = 0 if sample else ui * U
            ti = 0
            par = state_unit[0] % 2
            very_first = state_unit[0] == 0
            state["conv"] = very_first
            state_unit[0] += 1
            xT = xTs[par]

            def XK(c):
                return ("x", par, c)

            if sample:
                ctab, stab, ctk, stk = tsc, tss, "tsc", "tss"
            else:
                ctab, stab, ctk, stk = tabc[ti], tabs[ti], ("tabc", ti), ("tabs", ti)
                S.dma("sp", lambda e: e.dma_start(out=ctab[:], in_=tab_p[0][:, t0:t0 + U]), f"d_tabc{ti}", writes=[ctk])
                S.dma("sp", lambda e: e.dma_start(out=stab[:], in_=tab_p[1][:, t0:t0 + U]), f"d_tabs{ti}", writes=[stk])

            chk('tables')
            for tb in range(nblk):
                slot = tb % 2
                if sample or very_first or tb >= 2:
                    S.dma("sp", lambda e, slot=slot, tb=tb: e.dma_start(out=xs[slot][0:ntok, :],
                                                                       in_=xsrc[b_idx][t0 + tb * 128:t0 + tb * 128 + ntok, :]),
                          f"d_xs{slot}", writes=[("xs", slot)])
                for half in range(2):
                    b = next_bank()

                    def tfn(e, b=b, half=half, slot=slot):
                        ins = None
                        for c4 in range(4):
                            c = half * 4 + c4
                            ins = e.transpose(out=banks[b][:, c4 * 128:c4 * 128 + ntok], in_=xs[slot][0:ntok, c * 128:(c + 1) * 128],
                                              identity=ident[0:ntok, 0:ntok])
                        return ins
                    chk('xdma')
                    S.op("pe", tfn, reads=[("xs", slot), "ident"], writes=[("ps", b)], dur=0.5)
                    chk('xtr')
                    for c4 in range(4):
                        c = half * 4 + c4
                        eng = "act" if half == 0 else "dve"
                        if eng == "act":
                            S.op("act", lambda e, b=b, c=c, c4=c4, tb=tb: e.activation(out=xT[:, c, tb * 128:tb * 128 + ntok],
                                                                                     in_=banks[b][:, c4 * 128:c4 * 128 + ntok], func=AF.Copy),
                                 reads=[("ps", b)], writes=[XK(c)], n=128)
                        else:
                            S.op("dve", lambda e, b=b, c=c, c4=c4, tb=tb: e.tensor_copy(out=xT[:, c, tb * 128:tb * 128 + ntok],
                                                                                      in_=banks[b][:, c4 * 128:c4 * 128 + ntok]),
                                 reads=[("ps", b)], writes=[XK(c)], n=128)
            if next_loads is not None:
                nb, nu = next_loads
                for tb in range(2):
                    S.dma("sp", lambda e, tb=tb: e.dma_start(out=xs[tb][:, :], in_=x_p[nb][nu * U + tb * 128:nu * U + (tb + 1) * 128, :]),
                          f"d_xs{tb}", writes=[("xs", tb)])

            def run_layer(l):
                lo = l * C_LAYER
                state["wl"] = l
                state["wseq"] = 0
                if first:
                    S.op("dve", lambda e, l=l: e.memset(prod[l][:, :, 0:2], 0.0), writes=[("prod", l, 0), ("prod", l, 1)])
                    S.op("dve", lambda e, l=l: e.memset(pin[l][:, :, 0:15], 0.0), writes=[("pin", l, 0), ("pin", l, 1)])
                    S.op("dve", lambda e, l=l: e.memset(fhalo[l][:], 0.0), writes=[("fhalo", l, jj_) for jj_ in range(NJ)])
                if sample:
                    S.dma("sp", lambda e, l=l: e.dma_start(out=ckst[:], in_=ck[l]), "d_ckst", writes=["ckst"])
                    S.dma("sp", lambda e, l=l: e.dma_start(out=cvst[:], in_=cv[l]), "d_cvst", writes=["cvst"])
                    b = next_bank()
                    S.op("pe", lambda e, b=b: e.transpose(out=banks[b][:, 0:128], in_=ckst[:, :], identity=ident[:, :]),
                         reads=["ckst", "ident"], writes=[("ps", b)])
                    S.op("act", lambda e, b=b, l=l: e.activation(out=kT[l][:, 0:128], in_=banks[b][:, 0:128], func=AF.Copy),
                         reads=[("ps", b)], writes=[("kT", l)])
                    S.op("dve", lambda e, l=l: e.tensor_copy(out=vtm[l][:, 0, :], in_=cvst[:, :]), reads=["cvst"], writes=[("vtm", l)])
                    for hf in range(2):
                        S.dma("sp", lambda e, l=l, hf=hf: e.dma_start(out=prod[l][:, hf, 0:2],
                                                                    in_=st_conv[l][:, hf * 128:(hf + 1) * 128].rearrange("t p -> p t"),
                                                                    allow_slow_non_contiguous=True),
                              f"d_sconv{l}{hf}", writes=[("prod", l, hf)])
                    for hf in range(2):
                        S.dma("sp", lambda e, l=l, hf=hf: e.dma_start(out=pin[l][:, hf, 0:15],
                                                                    in_=st_pool[l][:, hf * 128:(hf + 1) * 128].rearrange("t p -> p t"),
                                                                    allow_slow_non_contiguous=True),
                              f"d_spool{l}{hf}", writes=[("pin", l, hf)])
                    for t_ in range(2):
                        S.dma("sp", lambda e, l=l, t_=t_: e.dma_start(out=fhalo[l][:, t_].rearrange("p h j -> p (h j)"),
                                                                    in_=st_ffn[l][t_].rearrange("(j p) -> p j", p=128),
                                                                    allow_slow_non_contiguous=True),
                              f"d_sffn{l}{t_}", writes=[("fhalo", l, jj_) for jj_ in range(NJ)])

                chk('xload')
                rmsnorm(l, C_GMIX, 1, tw, xT, XK)

                chk('norm1')
                wv01, wk01 = wload(w_in_p[l][:, 0:256].rearrange("(k p) n -> p k n", p=128), (8, 256))
                wv23, wk23 = wload(w_in_p[l][:, 256:512].rearrange("(k p) n -> p k n", p=128), (8, 256))
                wv45, wk45 = wload(w_in_p[l][:, 512:768].rearrange("(k p) n -> p k n", p=128), (8, 256))

                def rope(bm, bs_, dst_fn, dkey, keep_f32):
                    S.op("dve", lambda e: e.tensor_tensor(out=SC[2][:, 0:tw], in0=banks[bm][:, 0:tw], in1=ctab[:, 0:tw], op=ALU.mult),
                         reads=[("ps", bm), ctk], writes=["S2"])
                    S.op("dve", lambda e: e.tensor_tensor(out=SC[3][:, 0:tw], in0=banks[bs_][:, 0:tw], in1=stab[:, 0:tw], op=ALU.mult),
                         reads=[("ps", bs_), stk], writes=["S3"])
                    if keep_f32:
                        S.op("dve", lambda e: e.tensor_tensor(out=SC[4][:, 0:tw], in0=SC[2][:, 0:tw], in1=SC[3][:, 0:tw], op=ALU.add),
                             reads=["S2", "S3"], writes=["S4"])
                        S.op("act", lambda e: e.activation(out=dst_fn(), in_=SC[4][:, 0:tw], func=AF.Copy), reads=["S4"], writes=[dkey])
                    else:
                        S.op("dve", lambda e: e.tensor_tensor(out=dst_fn(), in0=SC[2][:, 0:tw], in1=SC[3][:, 0:tw], op=ALU.add),
                             reads=["S2", "S3"], writes=[dkey])

                bqa = proj_fm(wv01, wk01, 0, tw, split=True)
                bqas = proj_fm(wv23, wk23, 1, tw, split=True)
                rope(bqa, bqas, lambda: qab[0][:, 0:tw], ("q", 0), False)
                bqb = proj_fm(wv01, wk01, 1, tw, split=True)
                bqbs = proj_fm(wv45, wk45, 0, tw, split=True)
                rope(bqb, bqbs, lambda: qab[1][:, 0:tw], ("q", 1), False)
                bk = proj_fm(wv23, wk23, 0, tw)
                bks = proj_fm(wv45, wk45, 1, tw)
                rope(bk, bks, lambda: kT[l][:, 128:128 + tw], ("kT", l), True)
                if last:
                    nk_rows = DEC_SEQ if sample else 128
                    dst = (o_sk if sample else o_pk)[b_idx if not sample else 0][l]
                    state_out_T(SC[4][:, tw - nk_rows:tw], nk_rows, ["S4"], dst)

                chk('rope')
                wvA, wkA = wload(w_in_p[l][:, 768:1024].rearrange("(k p) n -> p k n", p=128), (8, 256))
                wvB, wkB = wload(w_in_p[l][:, 1024:1280].rearrange("(k p) n -> p k n", p=128), (8, 256))
                wvC, wkC = wload(w_in_p[l][:, 1280:1536].rearrange("(k p) n -> p k n", p=128), (8, 256))
                for hf in range(2):
                    bh = proj_fm(wvC, wkC, hf, tw)
                    S.op("act", lambda e, bh=bh: e.activation(out=SC[5][:, 0:tw], in_=banks[bh][:, 0:tw], func=AF.Copy),
                         reads=[("ps", bh)], writes=["S5"])
                    bc_ = proj_fm(wvB, wkB, hf, tw)
                    S.op("dve", lambda e, bc_=bc_, hf=hf: e.tensor_tensor(out=prod[l][:, hf, 2:2 + tw], in0=banks[bc_][:, 0:tw],
                                                                         in1=SC[5][:, 0:tw], op=ALU.mult),
                         reads=[("ps", bc_), "S5"], writes=[("prod", l, hf)])
                    cwb = C_CONV + hf * 3
                    S.op("dve", lambda e, hf=hf, cwb=cwb: e.tensor_scalar(out=SC[6][:, 0:tw], in0=prod[l][:, hf, 2:2 + tw],
                                                                         scalar1=col(l, cwb, 2), scalar2=None, op0=ALU.mult),
                         reads=[("prod", l, hf), "colt"], writes=["S6"])
                    S.op("dve", lambda e, hf=hf, cwb=cwb: e.scalar_tensor_tensor(out=SC[6][:, 0:tw], in0=prod[l][:, hf, 1:1 + tw],
                                                                                scalar=col(l, cwb, 1), in1=SC[6][:, 0:tw],
                                                                                op0=ALU.mult, op1=ALU.add),
                         reads=[("prod", l, hf), "colt", "S6"], writes=["S6"])
                    S.op("dve", lambda e, hf=hf, cwb=cwb: e.scalar_tensor_tensor(out=SC[6][:, 0:tw], in0=prod[l][:, hf, 0:tw],
                                                                                scalar=col(l, cwb, 0), in1=SC[6][:, 0:tw],
                                                                                op0=ALU.mult, op1=ALU.add),
                         reads=[("prod", l, hf), "colt", "S6"], writes=["S6"])
                    bb = proj_fm(wvA, wkA, hf, tw)
                    S.op("dve", lambda e, bb=bb, hf=hf: e.tensor_tensor(out=mixT[:, 2 + hf, 0:tw], in0=banks[bb][:, 0:tw],
                                                                       in1=SC[6][:, 0:tw], op=ALU.mult),
                         reads=[("ps", bb), "S6"], writes=[("mix", 2 + hf)])
                if last:
                    S.op("dve", lambda e: e.tensor_copy(out=cstg[:].rearrange("p t h -> p h t"), in_=prod[l][:, :, tw:tw + 2]),
                         reads=[("prod", l, 0), ("prod", l, 1)], writes=["cstg"])
                    dst = (o_sconv if sample else o_pconv)[0 if sample else b_idx][l]
                    state_out_T(cstg[:].rearrange("p t h -> p (t h)"), 4, ["cstg"], dst)
                if not last:
                    S.op("dve", lambda e: e.tensor_copy(out=prod[l][:, :, 0:2], in_=prod[l][:, :, tw:tw + 2]),
                         reads=[("prod", l, 0), ("prod", l, 1)], writes=[("prod", l, 0), ("prod", l, 1)])

                chk('conv')
                wvD, wkD = wload(w_in_p[l][:, 1536:1792].rearrange("(k p) n -> p k n", p=128), (8, 256))
                for hf in range(2):
                    bu = proj_fm(wvD, wkD, hf, tw)
                    S.op("act", lambda e, bu=bu, hf=hf: e.activation(out=ubuf[:, hf, 0:tw], in_=banks[bu][:, 0:tw], func=AF.Gelu_apprx_tanh),
                         reads=[("ps", bu)], writes=[("u", hf)], tset="gelu")

                chk('gu')
                wvE, wkE = wload(w_in_p[l][:, 1792:2048].rearrange("(k p) n -> p k n", p=128), (8, 256))
                for hf in range(2):
                    bp = proj_fm(wvE, wkE, hf, tw)
                    pk = ("pin", l, hf)
                    S.op("act", lambda e, bp=bp, hf=hf: e.activation(out=pin[l][:, hf, 15:15 + tw], in_=banks[bp][:, 0:tw], func=AF.Copy),
                         reads=[("ps", bp)], writes=[pk])
                    W = 15 + tw
                    P_ = pin[l]
                    s2, s4, s8, s16 = SC[7], SC[8], SC[9], SC[10]
                    S.op("dve", lambda e, hf=hf: e.tensor_tensor(out=s2[:, 1:W], in0=P_[:, hf, 1:W], in1=P_[:, hf, 0:W - 1], op=ALU.add),
                         reads=[pk], writes=["S7"])
                    S.op("dve", lambda e: e.tensor_tensor(out=s4[:, 3:W], in0=s2[:, 3:W], in1=s2[:, 1:W - 2], op=ALU.add),
                         reads=["S7"], writes=["S8"])
                    if hf == 1:
                        S.op("dve", lambda e: e.tensor_tensor(out=s8[:, 7:W], in0=s4[:, 7:W], in1=s4[:, 3:W - 4], op=ALU.add),
                             reads=["S8"], writes=["S9"])
                        S.op("dve", lambda e: e.tensor_tensor(out=s16[:, 15:W], in0=s8[:, 15:W], in1=s8[:, 7:W - 8], op=ALU.add),
                             reads=["S9"], writes=["S10"])
                        wlo, whi, klo, khi = s8, s16, "S9", "S10"
                    else:
                        wlo, whi, klo, khi = s2, s4, "S7", "S8"
                    pl = pooled[hf]
                    for (p0, p1, wsrc, wkey_) in ((0, 64, wlo, klo), (64, 128, whi, khi)):
                        S.op("dve", lambda e, p0=p0, p1=p1, wsrc=wsrc, hf=hf, pl=pl: e.scalar_tensor_tensor(
                            out=pl[p0:p1, 0:tw], in0=wsrc[p0:p1, 15:15 + tw], scalar=colt[p0:p1, C_INVW + hf:C_INVW + hf + 1],
                            in1=P_[p0:p1, hf, 15:15 + tw], op0=ALU.mult, op1=ALU.subtract),
                            reads=[wkey_, pk, "colt"], writes=[("pooled", hf)])
                        if first:
                            S.op("dve", lambda e, p0=p0, p1=p1, wsrc=wsrc, hf=hf: e.tensor_tensor(
                                out=ptmp[p0:p1, 0:16], in0=wsrc[p0:p1, 15:31], in1=colt[p0:p1, C_RC + hf * 16:C_RC + hf * 16 + 16], op=ALU.mult),
                                reads=[wkey_, "colt"], writes=["ptmp"])
                            S.op("dve", lambda e, p0=p0, p1=p1, hf=hf, pl=pl: e.tensor_tensor(
                                out=pl[p0:p1, 0:16], in0=ptmp[p0:p1, 0:16], in1=P_[p0:p1, hf, 15:31], op=ALU.subtract),
                                reads=["ptmp", pk], writes=[("pooled", hf)])
                    bpo = next_bank()
                    S.op("pe", lambda e, bpo=bpo, hf=hf, pl=pl: e.matmul(banks[bpo][:, 0:tw], lhsT=wpb[:, (l * 2 + hf) * 128:(l * 2 + hf + 1) * 128],
                                                               rhs=pl[:, 0:tw], start=True, stop=True),
                         reads=[("pooled", hf), "wpb"], writes=[("ps", bpo)])
                    S.op("act", lambda e, bpo=bpo, hf=hf: e.activation(out=mixT[:, 6 + hf, 0:tw], in_=banks[bpo][:, 0:tw], func=AF.Identity,
                                                                      scale=col(l, C_PSC, hf)),
                         reads=[("ps", bpo), "colt"], writes=[("mix", 6 + hf)])
                if last:
                    S.op("dve", lambda e: e.tensor_copy(out=pstg[:].rearrange("p t h -> p h t"), in_=pin[l][:, :, tw:tw + 15]),
                         reads=[("pin", l, 0), ("pin", l, 1)], writes=["pstg"])
                    dst = (o_spool if sample else o_ppool)[0 if sample else b_idx][l]
                    state_out_T(pstg[:].rearrange("p t h -> p (t h)"), 30, ["pstg"], dst)
                if not last:
                    S.op("dve", lambda e: e.tensor_copy(out=pin[l][:, :, 0:15], in_=pin[l][:, :, tw:tw + 15]),
                         reads=[("pin", l, 0), ("pin", l, 1)], writes=[("pin", l, 0), ("pin", l, 1)])

                chk('pool')
                wvTv, wkTv = wload(w_in_p[l][:, 2048:2176].rearrange("(k p) n -> p k n", p=128), (8, 128))
                wvTg, wkTg = wload(w_in_p[l][:, 2176:2432].rearrange("(k p) n -> p k n", p=128), (8, 256))
                for tb in range(nblk):
                    bt = next_bank()
                    def tmfn(e, bt=bt, tb=tb):
                        ins = None
                        for k in range(8):
                            ins = e.matmul(banks[bt][0:ntok, 0:128], lhsT=hn[:, k, tb * 128:tb * 128 + ntok], rhs=wvTv[:, k, :],
                                           start=(k == 0), stop=(k == 7))
                        for k in range(8):
                            ins = e.matmul(banks[bt][0:ntok, 128:384], lhsT=hn[:, k, tb * 128:tb * 128 + ntok], rhs=wvTg[:, k, :],
                                           start=(k == 0), stop=(k == 7))
                        return ins
                    S.op("pe", tmfn, reads=HN + wkTv + wkTg, writes=[("ps", bt)], dur=8 * 0.07 + 8 * 0.12)
                    S.op("act", lambda e, bt=bt, tb=tb: e.activation(out=vtm[l][0:ntok, 1 + tb, :], in_=banks[bt][0:ntok, 0:128], func=AF.Copy),
                         reads=[("ps", bt)], writes=[("vtm", l)], n=128)
                    if last and (sample or tb == nblk - 1):
                        si = state["stg"]
                        state["stg"] = (si + 1) % 3
                        S.op("dve", lambda e, bt=bt, si=si: e.tensor_copy(out=stg[si][0:ntok, :], in_=banks[bt][0:ntok, 0:128]),
                             reads=[("ps", bt)], writes=[("stg", si)])
                        dst = (o_sv if sample else o_pv)[0 if sample else b_idx][l]
                        S.dma("sp", lambda e, si=si, dst=dst: e.dma_start(out=dst, in_=stg[si][0:ntok, :]), f"d_stg{si}",
                              reads=[("stg", si)], is_output=True)
                    gi = tb % 2
                    S.op("act", lambda e, bt=bt, gi=gi: e.activation(out=gv[gi][0:ntok, :], in_=banks[bt][0:ntok, 128:384], func=AF.Gelu_apprx_tanh),
                         reads=[("ps", bt)], writes=[("gv", gi)], n=256, tset="gelu")
                    S.op("dve", lambda e, gi=gi: e.bn_stats(out=st6[gi][0:ntok, :], in_=gv[gi][0:ntok, :]), reads=[("gv", gi)], writes=[("st6", gi)], n=256)
                    S.op("dve", lambda e, gi=gi: e.bn_aggr(out=mv[gi][0:ntok, :], in_=st6[gi][0:ntok, :]), reads=[("st6", gi)], writes=[("mv", gi)], n=30)
                    S.op("act", lambda e, gi=gi: e.activation(out=lnr[gi][0:ntok, 0:1], in_=mv[gi][0:ntok, 1:2], func=AF.Ln,
                                                             bias=colt[0:ntok, C_EPS:C_EPS + 1], scale=1.0),
                         reads=[("mv", gi), "colt"], writes=[("lnr", gi)], n=2, tset="lnexp")
                    S.op("act", lambda e, gi=gi: e.activation(out=lnr2[gi][0:ntok, 0:1], in_=lnr[gi][0:ntok, 0:1], func=AF.Exp, scale=-0.5),
                         reads=[("lnr", gi)], writes=[("lnr2", gi)], n=2, tset="lnexp")
                    S.op("dve", lambda e, gi=gi: e.tensor_scalar(out=vn[gi][0:ntok, :], in0=gv[gi][0:ntok, :], scalar1=mv[gi][0:ntok, 0:1],
                                                                scalar2=lnr2[gi][0:ntok, 0:1], op0=ALU.subtract, op1=ALU.mult),
                         reads=[("gv", gi), ("mv", gi), ("lnr2", gi)], writes=[("vn", gi)], n=140)
                    S.op("dve", lambda e, gi=gi: e.tensor_tensor(out=vn[gi][0:ntok, :], in0=vn[gi][0:ntok, :],
                                                                in1=bct[0:ntok, (l * 2) * 256:(l * 2 + 1) * 256], op=ALU.mult),
                         reads=[("vn", gi), "bct"], writes=[("vn", gi)], n=256)
                    S.op("dve", lambda e, gi=gi: e.tensor_tensor(out=vlnb[gi][0:ntok, :], in0=vn[gi][0:ntok, :],
                                                                in1=bct[0:ntok, (l * 2 + 1) * 256:(l * 2 + 2) * 256], op=ALU.add),
                         reads=[("vn", gi), "bct"], writes=[("vlnb", gi)], n=256)
                    if sample:
                        S.op("dve", lambda e, gi=gi: e.tensor_tensor(out=vlnf[0:ntok, :], in0=vn[gi][0:ntok, :],
                                                                    in1=bct[0:ntok, (l * 2 + 1) * 256:(l * 2 + 2) * 256], op=ALU.add),
                             reads=[("vn", gi), "bct"], writes=["vlnf"])
                        S.dma("sp", lambda e: e.dma_start(out=o_sgv[0][l], in_=vlnf[0:ntok, :]), "d_vlnf", reads=["vlnf"], is_output=True)
                    bg = next_bank()

                    def gfn(e, bg=bg, gi=gi):
                        ins = None
                        for g in range(4):
                            gp, gq = g // 2, g % 2
                            wo = (l * 4 + g) * 128
                            reg = lambda a, b_: banks[bg][gq * 64:(gq + 1) * 64, gp * 128 + a:gp * 128 + b_]
                            vcol = vlnb[gi]
                            if sample:
                                e.matmul(reg(0, 32), lhsT=ones_bf[0:1, 0:64], rhs=bsr[0:1, wo:wo + 32], start=True, stop=False)
                                ins = e.matmul(reg(0, 32), lhsT=vcol[0:32, g * 64:(g + 1) * 64], rhs=wmT[0:32, wo:wo + 32], start=False, stop=True)
                            else:
                                e.matmul(reg(0, 128), lhsT=ones_bf[0:1, 0:64], rhs=bsr[0:1, wo:wo + 128], start=True, stop=False)
                                e.matmul(reg(0, 64), lhsT=vcol[0:64, g * 64:(g + 1) * 64], rhs=wmT[0:64, wo:wo + 64], start=False, stop=False)
                                ins = e.matmul(reg(64, 128), lhsT=vcol[0:128, g * 64:(g + 1) * 64], rhs=wmT[0:128, wo + 64:wo + 128],
                                               start=False, stop=True)
                        return ins
                    S.op("pe", gfn, reads=[("vlnb", gi), "wmT", "bsr", "ones"], writes=[("ps", bg)], dur=0.8)
                    for gp in range(2):
                        S.op("dve", lambda e, bg=bg, gp=gp, tb=tb: e.tensor_tensor(out=mixT[:, 4 + gp, tb * 128:tb * 128 + ntok],
                                                                                 in0=banks[bg][:, gp * 128:gp * 128 + ntok],
                                                                                 in1=ubuf[:, gp, tb * 128:tb * 128 + ntok], op=ALU.mult),
                             reads=[("ps", bg), ("u", gp)], writes=[("mix", 4 + gp)], n=128)

                chk('tokmajor')
                def attn_block(qb_):
                    nq = ntok
                    qc0 = qb_ * 128
                    if sample:
                        kbs = [(0, 0, 128, 0, 128, [(0, 128, 0, 32)]), (1, 128, 160, 1, 32, [(0, 32, 0, 32)])]
                    else:
                        kbs = []
                        if not (first and qb_ == 0):
                            kbs.append((0, qc0, qc0 + 128, qb_, 128, [(0, 128, 0, 64), (64, 128, 64, 128)]))
                        kbs.append((1, 128 + qc0, 256 + qc0, qb_ + 1, 128, [(0, 64, 0, 64), (0, 128, 64, 128)]))
                    for X in range(2):
                        pset = (qb_ * 2 + X) % 2
                        sb_ = [next_bank(), next_bank()]
                        for hh in range(2):
                            def sfn(e, hh=hh, sb_=sb_, X=X):
                                ins = None
                                for (kt, kc0, kc1, vb, nk, regs) in kbs:
                                    ins = e.matmul(banks[sb_[hh]][0:nk, kt * 128:kt * 128 + nq],
                                                   lhsT=kT[l][hh * 64:(hh + 1) * 64, kc0:kc1],
                                                   rhs=qab[X][hh * 64:(hh + 1) * 64, qc0:qc0 + nq], start=True, stop=True)
                                return ins
                            S.op("pe", sfn, reads=[("kT", l), ("q", X)], writes=[("ps", sb_[hh])], dur=0.15)
                            for (kt, kc0, kc1, vb, nk, regs) in kbs:
                                for (p0, p1, c0, c1) in regs:
                                    S.op("act", lambda e, hh=hh, kt=kt, p0=p0, p1=p1, c0=c0, c1=c1, sb_=sb_, pset=pset: e.activation(
                                        out=Pb[pset][hh][kt][p0:p1, c0:c1], in_=banks[sb_[hh]][p0:p1, kt * 128 + c0:kt * 128 + c1],
                                        func=AF.Exp, scale=0.125),
                                        reads=[("ps", sb_[hh])], writes=[("P", pset, hh, kt)], n=64, tset="lnexp")
                        ob = next_bank()

                        def ofn(e, ob=ob, pset=pset):
                            ins = None
                            for hh in range(2):
                                for i, (kt, kc0, kc1, vb, nk, regs) in enumerate(kbs):
                                    ins = e.matmul(banks[ob][hh * 64:(hh + 1) * 64, 0:nq], lhsT=vtm[l][0:nk, vb, hh * 64:(hh + 1) * 64],
                                                   rhs=Pb[pset][hh][kt][0:nk, 0:nq], start=(i == 0), stop=(i == len(kbs) - 1))
                                for i, (kt, kc0, kc1, vb, nk, regs) in enumerate(kbs):
                                    ins = e.matmul(banks[ob][hh * 64:(hh + 1) * 64, 128:128 + nq], lhsT=ones_bf[0:nk, 0:64],
                                                   rhs=Pb[pset][hh][kt][0:nk, 0:nq], start=(i == 0), stop=(i == len(kbs) - 1))
                            return ins
                        S.op("pe", ofn, reads=[("vtm", l), "ones"] + [("P", pset, hh, kt) for hh in range(2) for kt in range(2)],
                             writes=[("ps", ob)], dur=0.55)
                        ri = X
                        S.op("act", lambda e, ob=ob, ri=ri, X=X: e.activation(out=rec[ri][:, 0:nq], in_=banks[ob][:, 128:128 + nq], func=AF.Ln,
                                                                             bias=esink[:, 2 * l + X:2 * l + X + 1], scale=1.0),
                             reads=[("ps", ob), ("esink", l)], writes=[("rec", ri)], n=128, tset="lnexp")
                        S.op("act", lambda e, ri=ri: e.activation(out=rec2[ri][:, 0:nq], in_=rec[ri][:, 0:nq], func=AF.Exp, scale=-1.0),
                             reads=[("rec", ri)], writes=[("rec2", ri)], n=128, tset="lnexp")
                        S.op("dve", lambda e, ob=ob, ri=ri, X=X: e.tensor_tensor(out=mixT[:, X, qc0:qc0 + nq], in0=banks[ob][:, 0:nq],
                                                                                in1=rec2[ri][:, 0:nq], op=ALU.mult),
                             reads=[("ps", ob), ("rec2", ri)], writes=[("mix", X)], n=128)
                for qb__ in range(nblk):
                    attn_block(qb__)
                if not last:
                    S.op("act", lambda e: e.activation(out=kT[l][:, 0:128], in_=kT[l][:, tw:tw + 128], func=AF.Copy),
                         reads=[("kT", l)], writes=[("kT", l)])
                    S.op("dve", lambda e: e.tensor_copy(out=vtm[l][:, 0, :], in_=vtm[l][:, nblk, :]), reads=[("vtm", l)], writes=[("vtm", l)])

                chk('attn')
                MIX = [("mix", c) for c in range(8)]
                for n2 in range(4):
                    wvO, wkO = wload(w_out_p[l][:, n2 * 256:(n2 + 1) * 256].rearrange("(k p) n -> p k n", p=128), (8, 256))
                    for h2 in range(2):
                        n = n2 * 2 + h2
                        bo = next_bank()
                        mm_group(bo, lambda bo=bo: banks[bo][:, 0:tw], [wvO[:, k, h2 * 128:(h2 + 1) * 128] for k in range(8)],
                                 [mixT[:, k, 0:tw] for k in range(8)], reads=MIX + wkO)
                        S.op("dve", lambda e, bo=bo, n=n: e.tensor_tensor(out=xT[:, n, 0:tw], in0=banks[bo][:, 0:tw], in1=xT[:, n, 0:tw], op=ALU.add),
                             reads=[("ps", bo), XK(n)], writes=[XK(n)])

                chk('wout')
                rmsnorm(l, C_GFFN, 1, tw, xT, XK)
                chk('norm2')
                fcw = lo + C_FCW
                pending = []

                def ffn_A(j):
                    wvU, wkU = wload_up(l, j)
                    s_ = j % 2
                    Ub, cg, cvv, sg = ub2[s_], SC[6 + s_], SC[8 + s_], SC[10 + s_]
                    kU, kcg, kcv, ksg = f"U{s_}", f"S{6 + s_}", f"S{8 + s_}", f"S{10 + s_}"
                    hs = ((0, cg, kcg), (1, cvv, kcv))
                    bus = []
                    for (h, cb_, ck_) in hs:
                        bu = next_bank()
                        bus.append(bu)
                        if j < 2:
                            mm_split(bu, [wvU[:, h, k, :] for k in range(8)], tw, wkU)
                        else:
                            mm_group(bu, lambda bu=bu: banks[bu][:, 0:tw], [wvU[:, h, k, :] for k in range(8)],
                                     [hn[:, k, 0:tw] for k in range(8)], reads=HN + wkU)
                    S.op("act", lambda e: e.activation(out=Ub[:, :, 0:2], in_=fhalo[l][:, :, :, j].rearrange("p t h -> p h t"), func=AF.Copy),
                         reads=[("fhalo", l, j)], writes=[kU + "h"], n=4)
                    for (h, cb_, ck_), bu in zip(hs, bus):
                        jj = h * NJ + j
                        S.op("act", lambda e, h=h, bu=bu: e.activation(out=Ub[:, h, 2:2 + tw], in_=banks[bu][:, 0:tw], func=AF.Copy),
                             reads=[("ps", bu)], writes=[kU + str(h)])
                        S.op("act", lambda e, cb_=cb_, bu=bu, jj=jj: e.activation(out=cb_[:, 0:tw], in_=banks[bu][:, 0:tw], func=AF.Identity,
                                                                                scale=colt[:, fcw + jj * 3 + 2:fcw + jj * 3 + 3]),
                             reads=[("ps", bu), "colt"], writes=[ck_])
                    for tap in (1, 0):
                        for (h, cb_, ck_) in hs:
                            jj = h * NJ + j
                            S.op("dve", lambda e, h=h, cb_=cb_, jj=jj, tap=tap: e.scalar_tensor_tensor(
                                out=cb_[:, 0:tw], in0=Ub[:, h, tap:tap + tw], scalar=colt[:, fcw + jj * 3 + tap:fcw + jj * 3 + tap + 1],
                                in1=cb_[:, 0:tw], op0=ALU.mult, op1=ALU.add),
                                reads=[kU + str(h), kU + "h", ck_, "colt"], writes=[ck_])
                    S.op("dve", lambda e: e.tensor_copy(out=fhalo[l][:, :, :, j].rearrange("p t h -> p h t"), in_=Ub[:, :, tw:tw + 2]),
                         reads=[kU + "0", kU + "1"], writes=[("fhalo", l, j)], n=4)

                    def ffn_B():
                        S.op("act", lambda e: e.activation(out=sg[:, 0:tw], in_=cg[:, 0:tw], func=AF.Silu), reads=[kcg], writes=[ksg], tset="silu")
                        S.op("dve", lambda e: e.tensor_tensor(out=act[:, j, 0:tw], in0=sg[:, 0:tw], in1=cvv[:, 0:tw], op=ALU.mult),
                             reads=[ksg, kcv], writes=[("act", j)])
                    return ffn_B

                for j in range(NJ):
                    fb = ffn_A(j)
                    if pending:
                        pending.pop(0)()
                    pending.append(fb)
                while pending:
                    pending.pop(0)()
                if last:
                    dst = (o_sffn if sample else o_pffn)[0 if sample else b_idx][l]
                    state_out_T(fhalo[l][:].rearrange("p t h j -> p (t h j)"), 88, [("fhalo", l, jj_) for jj_ in range(NJ)], dst)
                chk('ffnup')
                ACTK = [("act", j) for j in range(NJ)]
                for n in range(8):
                    hk = NJ // 2
                    wvW0, wkW0 = wload(w_down[l][0:hk * 128, n * 128:(n + 1) * 128].rearrange("(k p) n -> p k n", p=128), (hk, 128))
                    wvW1, wkW1 = wload(w_down[l][hk * 128:DFF, n * 128:(n + 1) * 128].rearrange("(k p) n -> p k n", p=128), (hk, 128))
                    bd = next_bank()
                    lhs_all = [wvW0[:, k, :] for k in range(hk)] + [wvW1[:, k, :] for k in range(hk)]
                    for (k0, k1, wk_) in ((0, hk, wkW0), (hk, NJ, wkW1)):
                        def dfn(e, bd=bd, k0=k0, k1=k1, lhs_all=lhs_all):
                            ins = None
                            for k in range(k0, k1):
                                ins = e.matmul(banks[bd][:, 0:tw], lhsT=lhs_all[k], rhs=act[:, k, 0:tw], start=(k == 0), stop=(k == NJ - 1))
                            return ins
                        S.op("pe", dfn, reads=ACTK[k0:k1] + wk_, writes=[("ps", bd)], dur=(k1 - k0) * 0.218)
                    S.op("dve", lambda e, bd=bd, n=n: e.tensor_tensor(out=xT[:, n, 0:tw], in0=banks[bd][:, 0:tw], in1=xT[:, n, 0:tw], op=ALU.add),
                         reads=[("ps", bd), XK(n)], writes=[XK(n)])

            for l_ in range(DEPTH):
                run_layer(l_)
                chk('layer')

            for tb in range(nblk):
                bs2 = [next_bank(), next_bank()]
                fi = tb % 2
                for half in range(2):
                    def tfn2(e, half=half, bs2=bs2, tb=tb):
                        ins = None
                        for c4 in range(4):
                            c = half * 4 + c4
                            ins = e.transpose(out=banks[bs2[half]][0:ntok, c4 * 128:(c4 + 1) * 128], in_=xT[:, c, tb * 128:tb * 128 + ntok],
                                              identity=ident[:, :])
                        return ins
                    S.op("pe", tfn2, reads=[XK(half * 4 + c4) for c4 in range(4)] + ["ident"], writes=[("ps", bs2[half])], dur=0.5)
                    S.op("act", lambda e, half=half, bs2=bs2, fi=fi: e.activation(out=SC[half][0:ntok, 0:512], in_=banks[bs2[half]][0:ntok, 0:512],
                                                                                func=AF.Square, accum_out=fin_ss[fi][0:ntok, half:half + 1]),
                         reads=[("ps", bs2[half])], writes=[f"S{half}", ("fss", fi)])
                S.op("dve", lambda e, fi=fi: e.tensor_tensor(out=fin_ss[fi][0:ntok, 2:3], in0=fin_ss[fi][0:ntok, 0:1], in1=fin_ss[fi][0:ntok, 1:2], op=ALU.add),
                     reads=[("fss", fi)], writes=[("fss", fi)])
                S.op("act", lambda e, fi=fi: e.activation(out=fin_ss[fi][0:ntok, 3:4], in_=fin_ss[fi][0:ntok, 2:3], func=AF.Ln,
                                                         bias=colt[0:ntok, C_EPS:C_EPS + 1], scale=1.0 / D),
                     reads=[("fss", fi)], writes=[("fss", fi)], n=2, tset="lnexp")
                S.op("act", lambda e, fi=fi: e.activation(out=fin_r[fi][0:ntok, 0:1], in_=fin_ss[fi][0:ntok, 3:4], func=AF.Exp, scale=-0.5),
                     reads=[("fss", fi)], writes=[("finr", fi)], n=2, tset="lnexp")
                yi = state["yo"]
                state["yo"] = (yi + 1) % 2
                for half in range(2):
                    S.op("dve", lambda e, half=half, bs2=bs2, fi=fi, yi=yi: e.scalar_tensor_tensor(
                        out=yo[yi][0:ntok, half * 512:(half + 1) * 512], in0=banks[bs2[half]][0:ntok, 0:512], scalar=fin_r[fi][0:ntok, 0:1],
                        in1=bct[0:ntok, 1024 + half * 512:1024 + (half + 1) * 512], op0=ALU.mult, op1=ALU.mult),
                        reads=[("ps", bs2[half]), ("finr", fi), "bct"], writes=[("yo", yi)])
                S.dma("sp", lambda e, yi=yi, tb=tb: e.dma_start(out=ydst[0 if sample else b_idx][t0 + tb * 128:t0 + tb * 128 + ntok, :],
                                                              in_=yo[yi][0:ntok, :]),
                      f"d_yo{yi}", reads=[("yo", yi)], is_output=True)

        run_unit_inner = run_unit

        def run_unit(*a):
            run_unit_inner(*a)
            chk('unit')

        units = [("p", b, u) for b in range(2) for u in range(NUNIT)]
        try:
            chk("setup")
            for i, (k_, b_, u_) in enumerate(units):
                nxt = None
                if i + 1 < len(units):
                    nxt = (units[i + 1][1], units[i + 1][2])
                run_unit(k_, b_, u_, nxt)
            run_unit("s", 0, 0, None)
        except _Stop as ex:
            print("STOPPED at", ex)
        S.finish()
        _DBG["est"] = S.est_total
        _DBG["S"] = S
        _DBG["nops"] = len(S.ops)

        semkeys = set()
        for e_ in Sched.ENGS:
            for it in S.prog[e_]:
                if it[0] == "wait":
                    semkeys.add(it[1])
                else:
                    semkeys.add(it[2])
        sems = {}
        for k_ in sorted(semkeys):
            sems[k_] = es.enter_context(nc.semaphore(k_.replace("_", "")))

        def replay(eng_name):
            def body(e):
                for it in S.prog[eng_name]:
                    if it[0] == "wait":
                        e.wait_ge(sems[it[1]], it[2])
                    else:
                        ins = it[1](e)
                        ins.then_inc(sems[it[2]], it[3])
            return body

        with nc.Block() as block:
            block.sync(replay("sp"))
            block.tensor(replay("pe"))
            block.scalar(replay("act"))
            block.vector(replay("dve"))
            block.gpsimd(replay("pool"))
    return nc


_CACHE = {}


def kernel(**inp):
    inp = {k: np.asarray(v) for k, v in inp.items()}
    shared = _prep_shared(inp)
    if "nc" not in _CACHE:
        _CACHE["nc"] = build_program()
    nc = _CACHE["nc"]
    in_maps = []
    for i in range(NCORES):
        m = dict(shared)
        m["x_p"] = np.ascontiguousarray(inp["x_prompt"][2 * i:2 * i + 2])
        m["x_s"] = np.ascontiguousarray(inp["x_sample"][i:i + 1])
        m["ck"] = np.ascontiguousarray(inp["cache_attn_k"][i].reshape(DEPTH, 128, 128))
        m["cv"] = np.ascontiguousarray(inp["cache_attn_v"][i].reshape(DEPTH, 128, 128))
        m["st_conv"] = np.ascontiguousarray(inp["state_conv"][i])
        m["st_pool"] = np.ascontiguousarray(inp["state_pool"][i])
        m["st_ffn"] = np.ascontiguousarray(inp["state_ffn_conv"][i])
        in_maps.append(m)
    res = run_bass_kernel_spmd(nc, in_maps, core_ids=list(range(NCORES)))
    R = res.results

    def cat(name):
        return np.concatenate([np.asarray(r[name]) for r in R], axis=0)

    B, DB = 16, 8
    y_prompt = cat("y_p")
    y_sample = cat("y_s")
    pk = cat("o_pk").reshape(B, DEPTH, 128, 2, 64)
    pv = cat("o_pv").reshape(B, DEPTH, 128, 2, 64)
    pconv = cat("o_pconv").reshape(B, DEPTH, 2, 256)
    ppool = cat("o_ppool").reshape(B, DEPTH, 15, 256)
    pffn = cat("o_pffn").reshape(B, DEPTH, 2, 2 * DFF)
    sk = cat("o_sk").reshape(DB, DEPTH, DEC_SEQ, 2, 64)
    sv = cat("o_sv").reshape(DB, DEPTH, DEC_SEQ, 2, 64)
    sconv = cat("o_sconv").reshape(DB, DEPTH, 2, 256)
    spool = cat("o_spool").reshape(DB, DEPTH, 15, 256)
    sffn = cat("o_sffn").reshape(DB, DEPTH, 2, 2 * DFF)
    sgv = cat("o_sgv").reshape(DB, DEPTH, DEC_SEQ, 256)
    outs = (y_prompt, y_sample, pk, pv, pconv, ppool, pffn, sk, sv, sconv, spool, sffn, sgv)
    return tuple(np.ascontiguousarray(o, dtype=np.float32) for o in outs)
```

```python
import numpy as np
from contextlib import ExitStack
import concourse.bass as bass
import concourse.mybir as mybir
from concourse.bass_utils import run_bass_kernel_spmd

F32 = mybir.dt.float32
BF16 = mybir.dt.bfloat16
AF = mybir.ActivationFunctionType
ALU = mybir.AluOpType

NCORES = 8
D = 1024
SEQ = 2048
DEPTH = 2
DEC_SEQ = 32
PAST = 2048
DFF = 2816
NJ = DFF // 128
U = 512
NUNIT = SEQ // U
EPS = 1e-6
NCOLW = 19 * 128
WSLOT = 3072
NSMALL = 6
NBIG = 0

C_GMIX, C_GFFN, C_CONV, C_FCW, C_PSC, C_SINK = 0, 8, 16, 22, 22 + 132, 22 + 132 + 2
C_LAYER = 22 + 132 + 4
C_INVW = 2 * C_LAYER
C_RC = C_INVW + 2
C_EPS = C_RC + 32
C_EPSN = C_EPS + 1
NCOLS = C_EPSN + 1


class _Stop(Exception):
    pass


_DBG = {"stop": None, "n": 0, "log": []}


def chk(name):
    _DBG["n"] += 1
    _DBG["log"].append((_DBG["n"], name))
    if _DBG["stop"] is not None and _DBG["n"] >= _DBG["stop"]:
        raise _Stop(name)


class Sched:
    ENGS = ("pe", "act", "dve", "pool", "sp")
    LAT = 0.08
    DMA_FIXED = 2.0
    DMA_BW = 280e3

    def __init__(self):
        self.ops = []
        self.prog = {e: [] for e in self.ENGS}
        self.reorder = True

    def op(self, eng, fn, reads=(), writes=(), n=512, dur=None, tset=None):
        if dur is None:
            if eng == "pe":
                dur = 0.03 + max(64, n) / 2400.0
            elif eng == "act":
                dur = 0.14 + n / 1200.0
            elif eng == "dve":
                dur = 0.16 + n / 930.0
            else:
                dur = 0.3 + n / 500.0
        import sys as _sys
        self.ops.append(dict(kind="op", eng=eng, fn=fn, reads=list(reads), writes=list(writes), dur=dur, tset=tset,
                             line=_sys._getframe(1).f_lineno))

    def dma(self, eng, fn, semkey, reads=(), writes=(), is_output=False, nbytes=65536):
        issue = 1.1 if eng == "pool" else 0.12
        self.ops.append(dict(kind="dma", eng=eng, fn=fn, semkey=semkey, reads=list(reads), writes=list(writes),
                             dur=issue, nbytes=nbytes, is_output=is_output))

    def _analyse(self):
        ops = self.ops
        last_w = {}
        readers = {}
        for i, o in enumerate(ops):
            preds = {}

            def add(j, typ):
                if j is None or j == i:
                    return
                if typ == "raw" or j not in preds:
                    preds[j] = typ if preds.get(j) != "raw" else "raw"
            for b in o["reads"]:
                add(last_w.get(b), "raw")
                if isinstance(b, tuple) and b[0] == "ps":
                    for r in readers.get(b, ()):
                        if ops[r]["eng"] != o["eng"]:
                            add(r, "excl")
            for b in o["writes"]:
                add(last_w.get(b), "waw")
                for r in readers.get(b, ()):
                    add(r, "war")
            for b in o["reads"]:
                readers.setdefault(b, []).append(i)
            for b in o["writes"]:
                last_w[b] = i
                readers[b] = []
            o["preds"] = preds

    def _schedule(self):
        import heapq
        ops = self.ops
        n = len(ops)
        succs = [[] for _ in range(n)]
        indeg = [0] * n
        for i, o in enumerate(ops):
            indeg[i] = len(o["preds"])
            for j in o["preds"]:
                succs[j].append(i)
        tail = [0.0] * n
        for i in range(n - 1, -1, -1):
            o = ops[i]
            d = o["dur"] + (self.DMA_FIXED + o["nbytes"] / self.DMA_BW if o["kind"] == "dma" else 0.0)
            m = 0.0
            for k in succs[i]:
                if tail[k] > m:
                    m = tail[k]
            tail[i] = d + m
        ready = {e: [] for e in self.ENGS}
        for i in range(n):
            if indeg[i] == 0:
                heapq.heappush(ready[ops[i]["eng"]], i)
        t_eng = {e: 0.0 for e in self.ENGS}
        finish = [0.0] * n
        order = {e: [] for e in self.ENGS}
        dma_free = 0.0
        cur_set = None
        done = 0
        K = 16 if self.reorder else 1
        DELTA = 0.25
        while done < n:
            best = None
            for e in self.ENGS:
                h = ready[e]
                if not h:
                    continue
                cands = heapq.nsmallest(K, h)
                for i in cands:
                    est = t_eng[e]
                    for j in ops[i]["preds"]:
                        f = finish[j] + (self.LAT if ops[j]["eng"] != e or ops[j]["kind"] == "dma" else 0.0)
                        if f > est:
                            est = f
                    ts_ = ops[i].get("tset")
                    if e == "act" and ts_ is not None and ts_ != cur_set:
                        est += 1.3
                    if self.reorder:
                        key = (est if est > t_eng[e] + DELTA else t_eng[e], -tail[i], i)
                    else:
                        key = (i, est)
                    if best is None or key < best[0]:
                        best = (key, e, i, est)
            _, e, i, est = best
            ready[e].remove(i)
            heapq.heapify(ready[e])
            o = ops[i]
            if o["kind"] == "dma":
                t_eng[e] = est + o["dur"]
                st = max(t_eng[e], dma_free)
                dma_free = st + o["nbytes"] / self.DMA_BW
                finish[i] = dma_free + self.DMA_FIXED
            else:
                t_eng[e] = est + o["dur"]
                finish[i] = t_eng[e]
                if e == "act" and o.get("tset") is not None:
                    cur_set = o["tset"]
            order[e].append(i)
            o["t0"] = est
            done += 1
            for k in succs[i]:
                indeg[k] -= 1
                if indeg[k] == 0:
                    heapq.heappush(ready[ops[k]["eng"]], k)
        self.order = order
        self.est_total = max(finish) if n else 0.0

    def finish(self):
        self._analyse()
        self._schedule()
        ops = self.ops
        tok = [None] * len(ops)
        pos = [0] * len(ops)
        dcnt = {}
        for e in self.ENGS:
            c = 0
            for i in self.order[e]:
                o = ops[i]
                if o["kind"] == "dma":
                    dcnt[o["semkey"]] = dcnt.get(o["semkey"], 0) + 16
                    tok[i] = (o["semkey"], dcnt[o["semkey"]])
                else:
                    c += 1
                    tok[i] = ("E_" + e, c)
                    pos[i] = c
        out_tokens = {}
        for e in self.ENGS:
            seen = {}
            own = "E_" + e
            c = 0
            prog = self.prog[e]
            for i in self.order[e]:
                o = ops[i]
                for j, typ in o["preds"].items():
                    pj = ops[j]
                    t = tok[j]
                    if pj["kind"] != "dma" and pj["eng"] == e:
                        if not (e in ("act", "dve", "pool") and o["kind"] == "op"):
                            continue
                    if seen.get(t[0], 0) < t[1]:
                        prog.append(("wait", t[0], t[1]))
                        seen[t[0]] = t[1]
                if o["kind"] == "dma":
                    prog.append(("ins", o["fn"], o["semkey"], 16))
                    if o["is_output"]:
                        out_tokens[o["semkey"]] = max(out_tokens.get(o["semkey"], 0), tok[i][1])
                else:
                    c += 1
                    prog.append(("ins", o["fn"], own, 1))
            if e == "sp":
                self._sp_seen = seen
        for k, v in out_tokens.items():
            if self._sp_seen.get(k, 0) < v:
                self.prog["sp"].append(("wait", k, v))


def _rope_tables(pos):
    half = 8
    inv_freq = np.power(np.float32(500000.0), -np.arange(half, dtype=np.float32) * np.float32(2.0 / 16)).astype(np.float32)
    ang = pos.astype(np.float32)[:, None] * inv_freq[None, :]
    cos = np.cos(ang).astype(np.float32).T
    sin = np.sin(ang).astype(np.float32).T
    T = pos.shape[0]
    ct = np.ones((128, T), np.float32)
    st = np.zeros((128, T), np.float32)
    for hh in range(2):
        b = hh * 64
        ct[b:b + 8] = cos
        ct[b + 8:b + 16] = cos
        st[b:b + 8] = -sin
        st[b + 8:b + 16] = sin
    return np.stack([ct, st], 0)


def _head_cols(base, h, swap):
    c = np.arange(64) + base + 64 * h
    if swap:
        c = np.concatenate([c[8:16], c[0:8], c[16:]])
    return c


def _prep_shared(inp):
    w_in = inp["w_in"]
    qb, kb = 0, 256
    cols = []
    for sw in (False, True):
        cols += [_head_cols(qb, 0, sw), _head_cols(qb, 2, sw)]
        cols += [_head_cols(qb, 1, sw), _head_cols(qb, 3, sw)]
        cols += [_head_cols(kb, 0, sw), _head_cols(kb, 1, sw)]
    cols.append(np.arange(512, 1280))
    cols.append(np.arange(1280, 1536))
    cols.append(np.arange(1792, 2048))
    cols.append(np.arange(384, 512))
    cols.append(np.arange(1536, 1792))
    cols = np.concatenate(cols)
    assert cols.shape[0] == NCOLW
    w_in_p = np.ascontiguousarray(w_in[:, :, cols])
    rperm = np.concatenate([np.arange(0, 64), np.arange(128, 192), np.arange(64, 128), np.arange(192, 256), np.arange(256, 1024)])
    w_out_p = np.ascontiguousarray(inp["w_out"][:, rperm, :])

    colt = np.zeros((128, NCOLS), np.float32)
    for l in range(DEPTH):
        o = l * C_LAYER
        colt[:, o + C_GMIX:o + C_GMIX + 8] = inp["g_mix"][l].reshape(8, 128).T
        colt[:, o + C_GFFN:o + C_GFFN + 8] = inp["g_ffn"][l].reshape(8, 128).T
        cw = inp["conv_w"][l]
        for hf in range(2):
            colt[:, o + C_CONV + hf * 3:o + C_CONV + hf * 3 + 3] = cw[:, hf * 128:(hf + 1) * 128].T
        fw = inp["ffn_conv_w"][l]
        colt[:, o + C_FCW:o + C_FCW + 132] = fw.reshape(3, 44, 128).transpose(2, 1, 0).reshape(128, 132)
        colt[:, o + C_PSC:o + C_PSC + 2] = inp["pool_scale"][l].reshape(2, 128).T
        sk = inp["attn_sink"][l]
        colt[0:64, o + C_SINK] = sk[0]
        colt[64:128, o + C_SINK] = sk[2]
        colt[0:64, o + C_SINK + 1] = sk[1]
        colt[64:128, o + C_SINK + 1] = sk[3]
    wins = (2.0, 4.0, 8.0, 16.0)
    for hf in range(2):
        for q in range(2):
            win = wins[hf * 2 + q]
            colt[q * 64:(q + 1) * 64, C_INVW + hf] = np.float32(1.0) / np.float32(win)
            cnt = np.minimum(np.arange(16) + 1, win).astype(np.float32)
            colt[q * 64:(q + 1) * 64, C_RC + hf * 16:C_RC + hf * 16 + 16] = (np.float32(1.0) / cnt)[None, :]
    colt[:, C_EPS] = EPS
    colt[:, C_EPSN] = EPS * 1024

    bc = np.zeros((128, 2 * 2 * 256 + 1024), np.float32)
    for l in range(DEPTH):
        bc[:, (l * 2 + 0) * 256:(l * 2 + 1) * 256] = inp["gmlp_ln_g"][l][None, :]
        bc[:, (l * 2 + 1) * 256:(l * 2 + 2) * 256] = inp["gmlp_ln_b"][l][None, :]
    bc[:, 1024:] = inp["g_final"][None, :]

    wmT = np.ascontiguousarray(inp["gmlp_w"].transpose(3, 0, 1, 2))
    bsr = np.ascontiguousarray(inp["gmlp_b"].reshape(1, DEPTH * 4 * 128))
    wpb = np.zeros((128, DEPTH, 2, 128), np.float32)
    for l in range(DEPTH):
        for g in range(4):
            hf, q = g // 2, g % 2
            wpb[q * 64:(q + 1) * 64, l, hf, q * 64:(q + 1) * 64] = inp["pool_w"][l, g]
    return {
        "w_in_p": w_in_p, "w_out_p": w_out_p, "w_up": np.ascontiguousarray(inp["w_up"]),
        "w_down": np.ascontiguousarray(inp["w_down"]),
        "colt": colt, "bc": bc, "wmT": wmT.reshape(128, DEPTH * 4 * 128), "bsr": bsr,
        "wpb": wpb.reshape(128, DEPTH * 2 * 128),
        "ident": np.eye(128, dtype=np.float32),
        "tab_p": _rope_tables(np.arange(SEQ)), "tab_s": _rope_tables(PAST + np.arange(DEC_SEQ)),
    }


def build_program():
    nc = bass.Bass("TRN2", target_bir_lowering=False)
    S = Sched()

    def din(name, shape):
        return nc.dram_tensor(name, list(shape), F32, kind="ExternalInput").ap()

    def dout(name, shape):
        return nc.dram_tensor(name, list(shape), F32, kind="ExternalOutput").ap()

    x_p = din("x_p", (2, SEQ, D))
    x_s = din("x_s", (1, DEC_SEQ, D))
    ck = din("ck", (DEPTH, 128, 128))
    cv = din("cv", (DEPTH, 128, 128))
    st_conv = din("st_conv", (DEPTH, 2, 256))
    st_pool = din("st_pool", (DEPTH, 15, 256))
    st_ffn = din("st_ffn", (DEPTH, 2, 2 * DFF))
    w_in_p = din("w_in_p", (DEPTH, D, NCOLW))
    w_out_p = din("w_out_p", (DEPTH, D, D))
    w_up = din("w_up", (DEPTH, D, 2 * DFF))
    w_down = din("w_down", (DEPTH, DFF, D))
    colt_d = din("colt", (128, NCOLS))
    bc_d = din("bc", (128, 2048))
    wmT_d = din("wmT", (128, DEPTH * 4 * 128))
    bsr_d = din("bsr", (1, DEPTH * 4 * 128))
    wpb_d = din("wpb", (128, DEPTH * 2 * 128))
    ident_d = din("ident", (128, 128))
    tab_p = din("tab_p", (2, 128, SEQ))
    tab_s = din("tab_s", (2, 128, DEC_SEQ))

    NIMG_L = 52
    wimg = nc.dram_tensor("wimg", [DEPTH * NIMG_L, 128, 2048], BF16, kind="Internal").ap()

    y_p = dout("y_p", (2, SEQ, D))
    y_s = dout("y_s", (1, DEC_SEQ, D))
    o_pk = dout("o_pk", (2, DEPTH, 128, 128))
    o_pv = dout("o_pv", (2, DEPTH, 128, 128))
    o_pconv = dout("o_pconv", (2, DEPTH, 4, 128))
    o_ppool = dout("o_ppool", (2, DEPTH, 30, 128))
    o_pffn = dout("o_pffn", (2, DEPTH, 88, 128))
    o_sk = dout("o_sk", (1, DEPTH, DEC_SEQ, 128))
    o_sv = dout("o_sv", (1, DEPTH, DEC_SEQ, 128))
    o_sconv = dout("o_sconv", (1, DEPTH, 4, 128))
    o_spool = dout("o_spool", (1, DEPTH, 30, 128))
    o_sffn = dout("o_sffn", (1, DEPTH, 88, 128))
    o_sgv = dout("o_sgv", (1, DEPTH, DEC_SEQ, 256))

    es = ExitStack()
    with es:
        def sb(name, shape, dt=F32):
            n = 1
            for d_ in shape[1:]:
                n *= d_
            _DBG["sbuf"] = _DBG.get("sbuf", 0) + n * (2 if dt == BF16 else 4)
            return es.enter_context(nc.sbuf_tensor(name, list(shape), dt))

        xTs = [sb(f"xT{i}", (128, 8, U)) for i in range(2)]
        hn = sb("hn", (128, 8, U), BF16)
        mixT = sb("mixT", (128, 8, U), BF16)
        act = sb("act", (128, NJ, U), BF16)
        wsS = [sb(f"wsS{i}", (128, 2048), BF16) for i in range(NSMALL)]
        wsB = [sb(f"wsB{i}", (128, WSLOT), BF16) for i in range(NBIG)]
        state_unit = [0]
        xs = [sb(f"xs{i}", (128, D)) for i in range(2)]
        yo = [sb(f"yo{i}", (128, D)) for i in range(2)]
        tabc = [sb(f"tabc{i}", (128, U)) for i in range(1)]
        tabs = [sb(f"tabs{i}", (128, U)) for i in range(1)]
        sq = [sb(f"sq{i}", (128, U), BF16) for i in range(2)]
        SC = [sb(f"scr{i}", (128, 528)) for i in range(12)]
        qab = [sb(f"qab{i}", (128, U), BF16) for i in range(2)]
        kT = [sb(f"kT{l}", (128, 128 + U), BF16) for l in range(DEPTH)]
        vtm = [sb(f"vtm{l}", (128, 1 + U // 128, 128), BF16) for l in range(DEPTH)]
        Pb = [[[sb(f"P{s}{h}{k}", (128, 128), BF16) for k in range(2)] for h in range(2)] for s in range(2)]
        rec = [sb(f"rec{i}", (128, 128)) for i in range(2)]
        rec2 = [sb(f"recb{i}", (128, 128)) for i in range(2)]
        prod = [sb(f"prod{l}", (128, 2, 2 + U)) for l in range(DEPTH)]
        ubuf = sb("ubuf", (128, 2, U))
        gv = [sb(f"gv{i}", (128, 256)) for i in range(2)]
        vn = [sb(f"vn{i}", (128, 256)) for i in range(2)]
        vlnb = [sb(f"vlnb{i}", (128, 256), BF16) for i in range(2)]
        vlnf = sb("vlnf", (128, 256))
        st6 = [sb(f"st6{i}", (128, 6)) for i in range(2)]
        mv = [sb(f"mv{i}", (128, 2)) for i in range(2)]
        lnr = [sb(f"lnr{i}", (128, 2)) for i in range(2)]
        lnr2 = [sb(f"lnrb{i}", (128, 2)) for i in range(2)]
        pin = [sb(f"pin{l}", (128, 2, 15 + U)) for l in range(DEPTH)]
        pooled = [sb(f"pooled{i}", (128, U), BF16) for i in range(2)]
        ptmp = sb("ptmp", (128, 16))
        fhalo = [sb(f"fhalo{l}", (128, 2, 2, NJ)) for l in range(DEPTH)]
        ub2 = [sb(f"ub2{i}", (128, 2, 516)) for i in range(2)]
        cstg = sb("cstg", (128, 2, 2))
        pstg = sb("pstg", (128, 15, 2))
        stg = [sb(f"stg{i}", (128, 128)) for i in range(3)]
        fin_ss = [sb(f"fss{i}", (128, 4)) for i in range(2)]
        fin_r = [sb(f"finr{i}", (128, 2)) for i in range(2)]
        colt = sb("coltS", (128, NCOLS))
        esink = sb("esink", (128, 2 * DEPTH))
        bct = sb("bcS", (128, 2048))
        ident = sb("identS", (128, 128))
        ones_bf = sb("ones_bf", (128, 128), BF16)
        wmT = sb("wmTS", (128, DEPTH * 4 * 128), BF16)
        bsr = sb("bsrS", (1, DEPTH * 4 * 128), BF16)
        wpb = sb("wpbS", (128, DEPTH * 2 * 128), BF16)
        tsc = sb("tsc", (128, DEC_SEQ))
        tss = sb("tss", (128, DEC_SEQ))
        ckst = sb("ckst", (128, 128))
        cvst = sb("cvst", (128, 128))
        banks = [es.enter_context(nc.psum_tensor(f"ps{i}", [128, 512], F32)) for i in range(8)]

        semnames = {}
        state = {"bank": 0, "wsS": 0, "wsB": 0, "stg": 0, "yo": 0, "wseq": 0, "wl": 0, "conv": True}

        def next_bank():
            i = state["bank"]
            state["bank"] = (i + 1) % 8
            return i

        def setup_load(dst_t, dst_ap, src_ap, eng="sp"):
            S.dma(eng, lambda e, o=dst_ap, i=src_ap: e.dma_start(out=o, in_=i), "d_" + dst_t, writes=[dst_t])

        setup_load("colt", colt[:], colt_d)
        setup_load("bct", bct[:], bc_d)
        setup_load("ident", ident[:], ident_d)
        setup_load("tsc", tsc[:], tab_s[0])
        setup_load("tss", tss[:], tab_s[1])
        setup_load("wmT", wmT[:], wmT_d, eng="pool")
        setup_load("bsr", bsr[:], bsr_d, eng="pool")
        setup_load("wpb", wpb[:], wpb_d, eng="pool")
        S.op("dve", lambda e: e.memset(ones_bf[:], 1.0), writes=["ones"])
        for s_ in range(2):
            for h_ in range(2):
                for k_ in range(2):
                    S.op("dve", lambda e, t=Pb[s_][h_][k_]: e.memset(t[:], 0.0), writes=[("P", s_, h_, k_)])
        for l in range(DEPTH):
            o = l * C_LAYER + C_SINK
            S.op("act", lambda e, l=l, o=o: e.activation(out=esink[:, 2 * l:2 * l + 2], in_=colt[:, o:o + 2], func=AF.Exp),
                 reads=["colt"], writes=[("esink", l)])

        def col(l, base, i=0, n=1):
            o = l * C_LAYER + base + i
            return colt[:, o:o + n]

        def _slot(n):
            kind = "wsB" if n > 2048 else "wsS"
            cnt = NBIG if kind == "wsB" else NSMALL
            i = state[kind]
            state[kind] = (i + 1) % cnt
            t = (wsB if kind == "wsB" else wsS)[i]
            return t, [(kind, i), (kind + "b", i)], f"d_{kind}{i}"

        def _img_index():
            idx = state["wl"] * NIMG_L + state["wseq"]
            state["wseq"] += 1
            assert state["wseq"] <= NIMG_L
            return idx

        def _img_store(t, n, key, idx, si):
            S.dma("sp", lambda e: e.dma_start(out=wimg[idx][:, 0:n], in_=t[:, 0:n]), f"d_wst{si}",
                  reads=key, writes=[("wimg", idx)], nbytes=n * 128 * 2)

        def _img_load(t, n, key, idx, semk):
            S.dma("sp", lambda e: e.dma_start(out=t[:, 0:n], in_=wimg[idx][:, 0:n]), semk + "i",
                  reads=[("wimg", idx)], writes=key, nbytes=n * 128 * 2)

        def wload(src_ap, view):
            n = 1
            for d_ in view:
                n *= d_
            t, key, semk = _slot(n)
            idx = _img_index()
            dst = t[:, 0:n].rearrange("p (k n) -> p k n", k=view[0])
            if state["conv"]:
                S.dma("pool", lambda e, o=dst, s=src_ap: e.dma_start(out=o, in_=s), semk, writes=key, nbytes=n * 128 * 4)
                _img_store(t, n, key, idx, semk)
            else:
                _img_load(t, n, key, idx, semk)
            return dst, key

        def wload_up(l, j):
            t, key, semk = _slot(2048)
            idx = _img_index()
            dst = t[:, 0:2048].rearrange("p (h k n) -> p h k n", h=2, k=8)
            if state["conv"]:
                for h in range(2):
                    src = w_up[l][:, h * DFF + j * 128:h * DFF + (j + 1) * 128].rearrange("(k p) n -> p k n", p=128)
                    S.dma("pool", lambda e, o=dst[:, h, :, :], s_=src: e.dma_start(out=o, in_=s_), semk + ("" if h == 0 else "h"), writes=[key[h]], nbytes=1024 * 128 * 4)
                _img_store(t, 2048, key, idx, semk)
            else:
                _img_load(t, 2048, key, idx, semk)
            return dst, key

        def mm_group(bank_i, out_fn, lhs_list, rhs_list, reads, extra_writes=(), ncols=512):
            def fn(e):
                n = len(lhs_list)
                ins = None
                for k in range(n):
                    ins = e.matmul(out_fn(), lhsT=lhs_list[k], rhs=rhs_list[k], start=(k == 0), stop=(k == n - 1))
                return ins
            S.op("pe", fn, reads=reads, writes=[("ps", bank_i)] + list(extra_writes),
                 dur=len(lhs_list) * (0.005 + max(64, ncols) / 2400.0))

        def rmsnorm(l, gbase, nt, tw, xT, XK):
            b = next_bank()
            for c in range(8):
                s_ = sq[c % 2]
                S.op("act", lambda e, c=c, s_=s_: e.activation(out=s_[:, 0:tw], in_=xT[:, c, 0:tw], func=AF.Square),
                     reads=[XK(c)], writes=[("sq", c % 2)])
                S.op("pe", lambda e, c=c, s_=s_, b=b: e.matmul(banks[b][:, 0:tw], lhsT=ones_bf[:, :], rhs=s_[:, 0:tw],
                                                             start=(c == 0), stop=(c == 7)),
                     reads=[("sq", c % 2), "ones"], writes=[("ps", b)])
            S.op("act", lambda e, b=b: e.activation(out=SC[0][:, 0:tw], in_=banks[b][:, 0:tw], func=AF.Ln,
                                                    bias=colt[:, C_EPS:C_EPS + 1], scale=1.0 / D),
                 reads=[("ps", b), "colt"], writes=["S0"], tset="lnexp")
            S.op("act", lambda e: e.activation(out=SC[1][:, 0:tw], in_=SC[0][:, 0:tw], func=AF.Exp, scale=-0.5),
                 reads=["S0"], writes=["S1"], tset="lnexp")
            for c in range(8):
                S.op("dve", lambda e, c=c: e.scalar_tensor_tensor(out=hn[:, c, 0:tw], in0=xT[:, c, 0:tw],
                                                                 scalar=col(l, gbase, c), in1=SC[1][:, 0:tw],
                                                                 op0=ALU.mult, op1=ALU.mult),
                     reads=[XK(c), "S1", "colt"], writes=[("hn", c)])

        HN = [("hn", c) for c in range(8)]

        def mm_split(b, lhs_list, tw, wkey):
            for k in range(8):
                S.op("pe", lambda e, b=b, k=k: e.matmul(banks[b][:, 0:tw], lhsT=lhs_list[k], rhs=hn[:, k, 0:tw],
                                                        start=(k == 0), stop=(k == 7)),
                     reads=[("hn", k)] + wkey, writes=[("ps", b)], dur=0.005 + max(64, tw) / 2400.0)

        def proj_fm(wv, wkey, ccol, tw, split=False):
            b = next_bank()
            if split:
                mm_split(b, [wv[:, k, ccol * 128:(ccol + 1) * 128] for k in range(8)], tw, wkey)
                return b
            mm_group(b, lambda b=b: banks[b][:, 0:tw],
                     [wv[:, k, ccol * 128:(ccol + 1) * 128] for k in range(8)],
                     [hn[:, k, 0:tw] for k in range(8)], reads=HN + wkey, ncols=tw)
            return b

        def state_out_T(src_ap, nrows, rkeys, dst_ap):
            b = next_bank()
            S.op("pe", lambda e, b=b: e.transpose(out=banks[b][0:nrows, 0:128], in_=src_ap, identity=ident[:, :]),
                 reads=list(rkeys) + ["ident"], writes=[("ps", b)])
            si = state["stg"]
            state["stg"] = (si + 1) % 3
            S.op("act", lambda e, b=b, si=si: e.activation(out=stg[si][0:nrows, :], in_=banks[b][0:nrows, 0:128], func=AF.Copy),
                 reads=[("ps", b)], writes=[("stg", si)])
            S.dma("sp", lambda e, si=si: e.dma_start(out=dst_ap, in_=stg[si][0:nrows, :]), f"d_stg{si}",
                  reads=[("stg", si)], is_output=True)

        def run_unit(kind, b_idx, ui, next_loads):
            sample = (kind == "s")
            Uu = DEC_SEQ if sample else U
            tw = Uu
            nblk = 1 if sample else U // 128
            ntok = DEC_SEQ if sample else 128
            first = (not sample) and ui == 0
            last = sample or ui == NUNIT - 1
            xsrc = x_s if sample else x_p
            ydst = y_s if sample else y_p
            t0 = 0 if sample else ui * U
            ti = 0
            par = state_unit[0] % 2
            very_first = state_unit[0] == 0
            state["conv"] = very_first
            state_unit[0] += 1
            xT = xTs[par]

            def XK(c):
                return ("x", par, c)

            if sample:
                ctab, stab, ctk, stk = tsc, tss, "tsc", "tss"
            else:
                ctab, stab, ctk, stk = tabc[ti], tabs[ti], ("tabc", ti), ("tabs", ti)
                S.dma("sp", lambda e: e.dma_start(out=ctab[:], in_=tab_p[0][:, t0:t0 + U]), f"d_tabc{ti}", writes=[ctk])
                S.dma("sp", lambda e: e.dma_start(out=stab[:], in_=tab_p[1][:, t0:t0 + U]), f"d_tabs{ti}", writes=[stk])

            chk('tables')
            for tb in range(nblk):
                slot = tb % 2
                if sample or very_first or tb >= 2:
                    S.dma("sp", lambda e, slot=slot, tb=tb: e.dma_start(out=xs[slot][0:ntok, :],
                                                                       in_=xsrc[b_idx][t0 + tb * 128:t0 + tb * 128 + ntok, :]),
                          f"d_xs{slot}", writes=[("xs", slot)])
                for half in range(2):
                    b = next_bank()

                    def tfn(e, b=b, half=half, slot=slot):
                        ins = None
                        for c4 in range(4):
                            c = half * 4 + c4
                            ins = e.transpose(out=banks[b][:, c4 * 128:c4 * 128 + ntok], in_=xs[slot][0:ntok, c * 128:(c + 1) * 128],
                                              identity=ident[0:ntok, 0:ntok])
                        return ins
                    chk('xdma')
                    S.op("pe", tfn, reads=[("xs", slot), "ident"], writes=[("ps", b)], dur=0.5)
                    chk('xtr')
                    for c4 in range(4):
                        c = half * 4 + c4
                        eng = "act" if half == 0 else "dve"
                        if eng == "act":
                            S.op("act", lambda e, b=b, c=c, c4=c4, tb=tb: e.activation(out=xT[:, c, tb * 128:tb * 128 + ntok],
                                                                                     in_=banks[b][:, c4 * 128:c4 * 128 + ntok], func=AF.Copy),
                                 reads=[("ps", b)], writes=[XK(c)], n=128)
                        else:
                            S.op("dve", lambda e, b=b, c=c, c4=c4, tb=tb: e.tensor_copy(out=xT[:, c, tb * 128:tb * 128 + ntok],
                                                                                      in_=banks[b][:, c4 * 128:c4 * 128 + ntok]),
                                 reads=[("ps", b)], writes=[XK(c)], n=128)
            if next_loads is not None:
                nb, nu = next_loads
                for tb in range(2):
                    S.dma("sp", lambda e, tb=tb: e.dma_start(out=xs[tb][:, :], in_=x_p[nb][nu * U + tb * 128:nu * U + (tb + 1) * 128, :]),
                          f"d_xs{tb}", writes=[("xs", tb)])

            def run_layer(l):
                lo = l * C_LAYER
                state["wl"] = l
                state["wseq"] = 0
                if first:
                    S.op("dve", lambda e, l=l: e.memset(prod[l][:, :, 0:2], 0.0), writes=[("prod", l, 0), ("prod", l, 1)])
                    S.op("dve", lambda e, l=l: e.memset(pin[l][:, :, 0:15], 0.0), writes=[("pin", l, 0), ("pin", l, 1)])
                    S.op("dve", lambda e, l=l: e.memset(fhalo[l][:], 0.0), writes=[("fhalo", l, jj_) for jj_ in range(NJ)])
                if sample:
                    S.dma("sp", lambda e, l=l: e.dma_start(out=ckst[:], in_=ck[l]), "d_ckst", writes=["ckst"])
                    S.dma("sp", lambda e, l=l: e.dma_start(out=cvst[:], in_=cv[l]), "d_cvst", writes=["cvst"])
                    b = next_bank()
                    S.op("pe", lambda e, b=b: e.transpose(out=banks[b][:, 0:128], in_=ckst[:, :], identity=ident[:, :]),
                         reads=["ckst", "ident"], writes=[("ps", b)])
                    S.op("act", lambda e, b=b, l=l: e.activation(out=kT[l][:, 0:128], in_=banks[b][:, 0:128], func=AF.Copy),
                         reads=[("ps", b)], writes=[("kT", l)])
                    S.op("dve", lambda e, l=l: e.tensor_copy(out=vtm[l][:, 0, :], in_=cvst[:, :]), reads=["cvst"], writes=[("vtm", l)])
                    for hf in range(2):
                        S.dma("sp", lambda e, l=l, hf=hf: e.dma_start(out=prod[l][:, hf, 0:2],
                                                                    in_=st_conv[l][:, hf * 128:(hf + 1) * 128].rearrange("t p -> p t"),
                                                                    allow_slow_non_contiguous=True),
                              f"d_sconv{l}{hf}", writes=[("prod", l, hf)])
                    for hf in range(2):
                        S.dma("sp", lambda e, l=l, hf=hf: e.dma_start(out=pin[l][:, hf, 0:15],
                                                                    in_=st_pool[l][:, hf * 128:(hf + 1) * 128].rearrange("t p -> p t"),
                                                                    allow_slow_non_contiguous=True),
                              f"d_spool{l}{hf}", writes=[("pin", l, hf)])
                    for t_ in range(2):
                        S.dma("sp", lambda e, l=l, t_=t_: e.dma_start(out=fhalo[l][:, t_].rearrange("p h j -> p (h j)"),
                                                                    in_=st_ffn[l][t_].rearrange("(j p) -> p j", p=128),
                                                                    allow_slow_non_contiguous=True),
                              f"d_sffn{l}{t_}", writes=[("fhalo", l, jj_) for jj_ in range(NJ)])

                chk('xload')
                rmsnorm(l, C_GMIX, 1, tw, xT, XK)

                chk('norm1')
                wv01, wk01 = wload(w_in_p[l][:, 0:256].rearrange("(k p) n -> p k n", p=128), (8, 256))
                wv23, wk23 = wload(w_in_p[l][:, 256:512].rearrange("(k p) n -> p k n", p=128), (8, 256))
                wv45, wk45 = wload(w_in_p[l][:, 512:768].rearrange("(k p) n -> p k n", p=128), (8, 256))

                def rope(bm, bs_, dst_fn, dkey, keep_f32):
                    S.op("dve", lambda e: e.tensor_tensor(out=SC[2][:, 0:tw], in0=banks[bm][:, 0:tw], in1=ctab[:, 0:tw], op=ALU.mult),
                         reads=[("ps", bm), ctk], writes=["S2"])
                    S.op("dve", lambda e: e.tensor_tensor(out=SC[3][:, 0:tw], in0=banks[bs_][:, 0:tw], in1=stab[:, 0:tw], op=ALU.mult),
                         reads=[("ps", bs_), stk], writes=["S3"])
                    if keep_f32:
                        S.op("dve", lambda e: e.tensor_tensor(out=SC[4][:, 0:tw], in0=SC[2][:, 0:tw], in1=SC[3][:, 0:tw], op=ALU.add),
                             reads=["S2", "S3"], writes=["S4"])
                        S.op("act", lambda e: e.activation(out=dst_fn(), in_=SC[4][:, 0:tw], func=AF.Copy), reads=["S4"], writes=[dkey])
                    else:
                        S.op("dve", lambda e: e.tensor_tensor(out=dst_fn(), in0=SC[2][:, 0:tw], in1=SC[3][:, 0:tw], op=ALU.add),
                             reads=["S2", "S3"], writes=[dkey])

                bqa = proj_fm(wv01, wk01, 0, tw, split=True)
                bqas = proj_fm(wv23, wk23, 1, tw, split=True)
                rope(bqa, bqas, lambda: qab[0][:, 0:tw], ("q", 0), False)
                bqb = proj_fm(wv01, wk01, 1, tw, split=True)
                bqbs = proj_fm(wv45, wk45, 0, tw, split=True)
                rope(bqb, bqbs, lambda: qab[1][:, 0:tw], ("q", 1), False)
                bk = proj_fm(wv23, wk23, 0, tw)
                bks = proj_fm(wv45, wk45, 1, tw)
                rope(bk, bks, lambda: kT[l][:, 128:128 + tw], ("kT", l), True)
                if last:
                    nk_rows = DEC_SEQ if sample else 128
                    dst = (o_sk if sample else o_pk)[b_idx if not sample else 0][l]
                    state_out_T(SC[4][:, tw - nk_rows:tw], nk_rows, ["S4"], dst)

                chk('rope')
                wvA, wkA = wload(w_in_p[l][:, 768:1024].rearrange("(k p) n -> p k n", p=128), (8, 256))
                wvB, wkB = wload(w_in_p[l][:, 1024:1280].rearrange("(k p) n -> p k n", p=128), (8, 256))
                wvC, wkC = wload(w_in_p[l][:, 1280:1536].rearrange("(k p) n -> p k n", p=128), (8, 256))
                for hf in range(2):
                    bh = proj_fm(wvC, wkC, hf, tw)
                    S.op("act", lambda e, bh=bh: e.activation(out=SC[5][:, 0:tw], in_=banks[bh][:, 0:tw], func=AF.Copy),
                         reads=[("ps", bh)], writes=["S5"])
                    bc_ = proj_fm(wvB, wkB, hf, tw)
                    S.op("dve", lambda e, bc_=bc_, hf=hf: e.tensor_tensor(out=prod[l][:, hf, 2:2 + tw], in0=banks[bc_][:, 0:tw],
                                                                         in1=SC[5][:, 0:tw], op=ALU.mult),
                         reads=[("ps", bc_), "S5"], writes=[("prod", l, hf)])
                    cwb = C_CONV + hf * 3
                    S.op("dve", lambda e, hf=hf, cwb=cwb: e.tensor_scalar(out=SC[6][:, 0:tw], in0=prod[l][:, hf, 2:2 + tw],
                                                                         scalar1=col(l, cwb, 2), scalar2=None, op0=ALU.mult),
                         reads=[("prod", l, hf), "colt"], writes=["S6"])
                    S.op("dve", lambda e, hf=hf, cwb=cwb: e.scalar_tensor_tensor(out=SC[6][:, 0:tw], in0=prod[l][:, hf, 1:1 + tw],
                                                                                scalar=col(l, cwb, 1), in1=SC[6][:, 0:tw],
                                                                                op0=ALU.mult, op1=ALU.add),
                         reads=[("prod", l, hf), "colt", "S6"], writes=["S6"])
                    S.op("dve", lambda e, hf=hf, cwb=cwb: e.scalar_tensor_tensor(out=SC[6][:, 0:tw], in0=prod[l][:, hf, 0:tw],
                                                                                scalar=col(l, cwb, 0), in1=SC[6][:, 0:tw],
                                                                                op0=ALU.mult, op1=ALU.add),
                         reads=[("prod", l, hf), "colt", "S6"], writes=["S6"])
                    bb = proj_fm(wvA, wkA, hf, tw)
                    S.op("dve", lambda e, bb=bb, hf=hf: e.tensor_tensor(out=mixT[:, 2 + hf, 0:tw], in0=banks[bb][:, 0:tw],
                                                                       in1=SC[6][:, 0:tw], op=ALU.mult),
                         reads=[("ps", bb), "S6"], writes=[("mix", 2 + hf)])
                if last:
                    S.op("dve", lambda e: e.tensor_copy(out=cstg[:].rearrange("p t h -> p h t"), in_=prod[l][:, :, tw:tw + 2]),
                         reads=[("prod", l, 0), ("prod", l, 1)], writes=["cstg"])
                    dst = (o_sconv if sample else o_pconv)[0 if sample else b_idx][l]
                    state_out_T(cstg[:].rearrange("p t h -> p (t h)"), 4, ["cstg"], dst)
                if not last:
                    S.op("dve", lambda e: e.tensor_copy(out=prod[l][:, :, 0:2], in_=prod[l][:, :, tw:tw + 2]),
                         reads=[("prod", l, 0), ("prod", l, 1)], writes=[("prod", l, 0), ("prod", l, 1)])

                chk('conv')
                wvD, wkD = wload(w_in_p[l][:, 1536:1792].rearrange("(k p) n -> p k n", p=128), (8, 256))
                for hf in range(2):
                    bu = proj_fm(wvD, wkD, hf, tw)
                    S.op("act", lambda e, bu=bu, hf=hf: e.activation(out=ubuf[:, hf, 0:tw], in_=banks[bu][:, 0:tw], func=AF.Gelu_apprx_tanh),
                         reads=[("ps", bu)], writes=[("u", hf)], tset="gelu")

                chk('gu')
                wvE, wkE = wload(w_in_p[l][:, 1792:2048].rearrange("(k p) n -> p k n", p=128), (8, 256))
                for hf in range(2):
                    bp = proj_fm(wvE, wkE, hf, tw)
                    pk = ("pin", l, hf)
                    S.op("act", lambda e, bp=bp, hf=hf: e.activation(out=pin[l][:, hf, 15:15 + tw], in_=banks[bp][:, 0:tw], func=AF.Copy),
                         reads=[("ps", bp)], writes=[pk])
                    W = 15 + tw
                    P_ = pin[l]
                    s2, s4, s8, s16 = SC[7], SC[8], SC[9], SC[10]
                    S.op("dve", lambda e, hf=hf: e.tensor_tensor(out=s2[:, 1:W], in0=P_[:, hf, 1:W], in1=P_[:, hf, 0:W - 1], op=ALU.add),
                         reads=[pk], writes=["S7"])
                    S.op("dve", lambda e: e.tensor_tensor(out=s4[:, 3:W], in0=s2[:, 3:W], in1=s2[:, 1:W - 2], op=ALU.add),
                         reads=["S7"], writes=["S8"])
                    if hf == 1:
                        S.op("dve", lambda e: e.tensor_tensor(out=s8[:, 7:W], in0=s4[:, 7:W], in1=s4[:, 3:W - 4], op=ALU.add),
                             reads=["S8"], writes=["S9"])
                        S.op("dve", lambda e: e.tensor_tensor(out=s16[:, 15:W], in0=s8[:, 15:W], in1=s8[:, 7:W - 8], op=ALU.add),
                             reads=["S9"], writes=["S10"])
                        wlo, whi, klo, khi = s8, s16, "S9", "S10"
                    else:
                        wlo, whi, klo, khi = s2, s4, "S7", "S8"
                    pl = pooled[hf]
                    for (p0, p1, wsrc, wkey_) in ((0, 64, wlo, klo), (64, 128, whi, khi)):
                        S.op("dve", lambda e, p0=p0, p1=p1, wsrc=wsrc, hf=hf, pl=pl: e.scalar_tensor_tensor(
                            out=pl[p0:p1, 0:tw], in0=wsrc[p0:p1, 15:15 + tw], scalar=colt[p0:p1, C_INVW + hf:C_INVW + hf + 1],
                            in1=P_[p0:p1, hf, 15:15 + tw], op0=ALU.mult, op1=ALU.subtract),
                            reads=[wkey_, pk, "colt"], writes=[("pooled", hf)])
                        if first:
                            S.op("dve", lambda e, p0=p0, p1=p1, wsrc=wsrc, hf=hf: e.tensor_tensor(
                                out=ptmp[p0:p1, 0:16], in0=wsrc[p0:p1, 15:31], in1=colt[p0:p1, C_RC + hf * 16:C_RC + hf * 16 + 16], op=ALU.mult),
                                reads=[wkey_, "colt"], writes=["ptmp"])
                            S.op("dve", lambda e, p0=p0, p1=p1, hf=hf, pl=pl: e.tensor_tensor(
                                out=pl[p0:p1, 0:16], in0=ptmp[p0:p1, 0:16], in1=P_[p0:p1, hf, 15:31], op=ALU.subtract),
                                reads=["ptmp", pk], writes=[("pooled", hf)])
                    bpo = next_bank()
                    S.op("pe", lambda e, bpo=bpo, hf=hf, pl=pl: e.matmul(banks[bpo][:, 0:tw], lhsT=wpb[:, (l * 2 + hf) * 128:(l * 2 + hf + 1) * 128],
                                                               rhs=pl[:, 0:tw], start=True, stop=True),
                         reads=[("pooled", hf), "wpb"], writes=[("ps", bpo)])
                    S.op("act", lambda e, bpo=bpo, hf=hf: e.activation(out=mixT[:, 6 + hf, 0:tw], in_=banks[bpo][:, 0:tw], func=AF.Identity,
                                                                      scale=col(l, C_PSC, hf)),
                         reads=[("ps", bpo), "colt"], writes=[("mix", 6 + hf)])
                if last:
                    S.op("dve", lambda e: e.tensor_copy(out=pstg[:].rearrange("p t h -> p h t"), in_=pin[l][:, :, tw:tw + 15]),
                         reads=[("pin", l, 0), ("pin", l, 1)], writes=["pstg"])
                    dst = (o_spool if sample else o_ppool)[0 if sample else b_idx][l]
                    state_out_T(pstg[:].rearrange("p t h -> p (t h)"), 30, ["pstg"], dst)
                if not last:
                    S.op("dve", lambda e: e.tensor_copy(out=pin[l][:, :, 0:15], in_=pin[l][:, :, tw:tw + 15]),
                         reads=[("pin", l, 0), ("pin", l, 1)], writes=[("pin", l, 0), ("pin", l, 1)])

                chk('pool')
                wvTv, wkTv = wload(w_in_p[l][:, 2048:2176].rearrange("(k p) n -> p k n", p=128), (8, 128))
                wvTg, wkTg = wload(w_in_p[l][:, 2176:2432].rearrange("(k p) n -> p k n", p=128), (8, 256))
                for tb in range(nblk):
                    bt = next_bank()
                    def tmfn(e, bt=bt, tb=tb):
                        ins = None
                        for k in range(8):
                            ins = e.matmul(banks[bt][0:ntok, 0:128], lhsT=hn[:, k, tb * 128:tb * 128 + ntok], rhs=wvTv[:, k, :],
                                           start=(k == 0), stop=(k == 7))
                        for k in range(8):
                            ins = e.matmul(banks[bt][0:ntok, 128:384], lhsT=hn[:, k, tb * 128:tb * 128 + ntok], rhs=wvTg[:, k, :],
                                           start=(k == 0), stop=(k == 7))
                        return ins
                    S.op("pe", tmfn, reads=HN + wkTv + wkTg, writes=[("ps", bt)], dur=8 * 0.07 + 8 * 0.12)
                    S.op("act", lambda e, bt=bt, tb=tb: e.activation(out=vtm[l][0:ntok, 1 + tb, :], in_=banks[bt][0:ntok, 0:128], func=AF.Copy),
                         reads=[("ps", bt)], writes=[("vtm", l)], n=128)
                    if last and (sample or tb == nblk - 1):
                        si = state["stg"]
                        state["stg"] = (si + 1) % 3
                        S.op("dve", lambda e, bt=bt, si=si: e.tensor_copy(out=stg[si][0:ntok, :], in_=banks[bt][0:ntok, 0:128]),
                             reads=[("ps", bt)], writes=[("stg", si)])
                        dst = (o_sv if sample else o_pv)[0 if sample else b_idx][l]
                        S.dma("sp", lambda e, si=si, dst=dst: e.dma_start(out=dst, in_=stg[si][0:ntok, :]), f"d_stg{si}",
                              reads=[("stg", si)], is_output=True)
                    gi = tb % 2
                    S.op("act", lambda e, bt=bt, gi=gi: e.activation(out=gv[gi][0:ntok, :], in_=banks[bt][0:ntok, 128:384], func=AF.Gelu_apprx_tanh),
                         reads=[("ps", bt)], writes=[("gv", gi)], n=256, tset="gelu")
                    S.op("dve", lambda e, gi=gi: e.bn_stats(out=st6[gi][0:ntok, :], in_=gv[gi][0:ntok, :]), reads=[("gv", gi)], writes=[("st6", gi)], n=256)
                    S.op("dve", lambda e, gi=gi: e.bn_aggr(out=mv[gi][0:ntok, :], in_=st6[gi][0:ntok, :]), reads=[("st6", gi)], writes=[("mv", gi)], n=30)
                    S.op("act", lambda e, gi=gi: e.activation(out=lnr[gi][0:ntok, 0:1], in_=mv[gi][0:ntok, 1:2], func=AF.Ln,
                                                             bias=colt[0:ntok, C_EPS:C_EPS + 1], scale=1.0),
                         reads=[("mv", gi), "colt"], writes=[("lnr", gi)], n=2, tset="lnexp")
                    S.op("act", lambda e, gi=gi: e.activation(out=lnr2[gi][0:ntok, 0:1], in_=lnr[gi][0:ntok, 0:1], func=AF.Exp, scale=-0.5),
                         reads=[("lnr", gi)], writes=[("lnr2", gi)], n=2, tset="lnexp")
                    S.op("dve", lambda e, gi=gi: e.tensor_scalar(out=vn[gi][0:ntok, :], in0=gv[gi][0:ntok, :], scalar1=mv[gi][0:ntok, 0:1],
                                                                scalar2=lnr2[gi][0:ntok, 0:1], op0=ALU.subtract, op1=ALU.mult),
                         reads=[("gv", gi), ("mv", gi), ("lnr2", gi)], writes=[("vn", gi)], n=140)
                    S.op("dve", lambda e, gi=gi: e.tensor_tensor(out=vn[gi][0:ntok, :], in0=vn[gi][0:ntok, :],
                                                                in1=bct[0:ntok, (l * 2) * 256:(l * 2 + 1) * 256], op=ALU.mult),
                         reads=[("vn", gi), "bct"], writes=[("vn", gi)], n=256)
                    S.op("dve", lambda e, gi=gi: e.tensor_tensor(out=vlnb[gi][0:ntok, :], in0=vn[gi][0:ntok, :],
                                                                in1=bct[0:ntok, (l * 2 + 1) * 256:(l * 2 + 2) * 256], op=ALU.add),
                         reads=[("vn", gi), "bct"], writes=[("vlnb", gi)], n=256)
                    if sample:
                        S.op("dve", lambda e, gi=gi: e.tensor_tensor(out=vlnf[0:ntok, :], in0=vn[gi][0:ntok, :],
                                                                    in1=bct[0:ntok, (l * 2 + 1) * 256:(l * 2 + 2) * 256], op=ALU.add),
                             reads=[("vn", gi), "bct"], writes=["vlnf"])
                        S.dma("sp", lambda e: e.dma_start(out=o_sgv[0][l], in_=vlnf[0:ntok, :]), "d_vlnf", reads=["vlnf"], is_output=True)
                    bg = next_bank()

                    def gfn(e, bg=bg, gi=gi):
                        ins = None
                        for g in range(4):
                            gp, gq = g // 2, g % 2
                            wo = (l * 4 + g) * 128
                            reg = lambda a, b_: banks[bg][gq * 64:(gq + 1) * 64, gp * 128 + a:gp * 128 + b_]
                            vcol = vlnb[gi]
                            if sample:
                                e.matmul(reg(0, 32), lhsT=ones_bf[0:1, 0:64], rhs=bsr[0:1, wo:wo + 32], start=True, stop=False)
                                ins = e.matmul(reg(0, 32), lhsT=vcol[0:32, g * 64:(g + 1) * 64], rhs=wmT[0:32, wo:wo + 32], start=False, stop=True)
                            else:
                                e.matmul(reg(0, 128), lhsT=ones_bf[0:1, 0:64], rhs=bsr[0:1, wo:wo + 128], start=True, stop=False)
                                e.matmul(reg(0, 64), lhsT=vcol[0:64, g * 64:(g + 1) * 64], rhs=wmT[0:64, wo:wo + 64], start=False, stop=False)
                                ins = e.matmul(reg(64, 128), lhsT=vcol[0:128, g * 64:(g + 1) * 64], rhs=wmT[0:128, wo + 64:wo + 128],
                                               start=False, stop=True)
                        return ins
                    S.op("pe", gfn, reads=[("vlnb", gi), "wmT", "bsr", "ones"], writes=[("ps", bg)], dur=0.8)
                    for gp in range(2):
                        S.op("dve", lambda e, bg=bg, gp=gp, tb=tb: e.tensor_tensor(out=mixT[:, 4 + gp, tb * 128:tb * 128 + ntok],
                                                                                 in0=banks[bg][:, gp * 128:gp * 128 + ntok],
                                                                                 in1=ubuf[:, gp, tb * 128:tb * 128 + ntok], op=ALU.mult),
                             reads=[("ps", bg), ("u", gp)], writes=[("mix", 4 + gp)], n=128)

                chk('tokmajor')
                def attn_block(qb_):
                    nq = ntok
                    qc0 = qb_ * 128
                    if sample:
                        kbs = [(0, 0, 128, 0, 128, [(0, 128, 0, 32)]), (1, 128, 160, 1, 32, [(0, 32, 0, 32)])]
                    else:
                        kbs = []
                        if not (first and qb_ == 0):
                            kbs.append((0, qc0, qc0 + 128, qb_, 128, [(0, 128, 0, 64), (64, 128, 64, 128)]))
                        kbs.append((1, 128 + qc0, 256 + qc0, qb_ + 1, 128, [(0, 64, 0, 64), (0, 128, 64, 128)]))
                    for X in range(2):
                        pset = (qb_ * 2 + X) % 2
                        sb_ = [next_bank(), next_bank()]
                        for hh in range(2):
                            def sfn(e, hh=hh, sb_=sb_, X=X):
                                ins = None
                                for (kt, kc0, kc1, vb, nk, regs) in kbs:
                                    ins = e.matmul(banks[sb_[hh]][0:nk, kt * 128:kt * 128 + nq],
                                                   lhsT=kT[l][hh * 64:(hh + 1) * 64, kc0:kc1],
                                                   rhs=qab[X][hh * 64:(hh + 1) * 64, qc0:qc0 + nq], start=True, stop=True)
                                return ins
                            S.op("pe", sfn, reads=[("kT", l), ("q", X)], writes=[("ps", sb_[hh])], dur=0.15)
                            for (kt, kc0, kc1, vb, nk, regs) in kbs:
                                for (p0, p1, c0, c1) in regs:
                                    S.op("act", lambda e, hh=hh, kt=kt, p0=p0, p1=p1, c0=c0, c1=c1, sb_=sb_, pset=pset: e.activation(
                                        out=Pb[pset][hh][kt][p0:p1, c0:c1], in_=banks[sb_[hh]][p0:p1, kt * 128 + c0:kt * 128 + c1],
                                        func=AF.Exp, scale=0.125),
                                        reads=[("ps", sb_[hh])], writes=[("P", pset, hh, kt)], n=64, tset="lnexp")
                        ob = next_bank()

                        def ofn(e, ob=ob, pset=pset):
                            ins = None
                            for hh in range(2):
                                for i, (kt, kc0, kc1, vb, nk, regs) in enumerate(kbs):
                                    ins = e.matmul(banks[ob][hh * 64:(hh + 1) * 64, 0:nq], lhsT=vtm[l][0:nk, vb, hh * 64:(hh + 1) * 64],
                                                   rhs=Pb[pset][hh][kt][0:nk, 0:nq], start=(i == 0), stop=(i == len(kbs) - 1))
                                for i, (kt, kc0, kc1, vb, nk, regs) in enumerate(kbs):
                                    ins = e.matmul(banks[ob][hh * 64:(hh + 1) * 64, 128:128 + nq], lhsT=ones_bf[0:nk, 0:64],
                                                   rhs=Pb[pset][hh][kt][0:nk, 0:nq], start=(i == 0), stop=(i == len(kbs) - 1))
                            return ins
                        S.op("pe", ofn, reads=[("vtm", l), "ones"] + [("P", pset, hh, kt) for hh in range(2) for kt in range(2)],
                             writes=[("ps", ob)], dur=0.55)
                        ri = X
                        S.op("act", lambda e, ob=ob, ri=ri, X=X: e.activation(out=rec[ri][:, 0:nq], in_=banks[ob][:, 128:128 + nq], func=AF.Ln,
                                                                             bias=esink[:, 2 * l + X:2 * l + X + 1], scale=1.0),
                             reads=[("ps", ob), ("esink", l)], writes=[("rec", ri)], n=128, tset="lnexp")
                        S.op("act", lambda e, ri=ri: e.activation(out=rec2[ri][:, 0:nq], in_=rec[ri][:, 0:nq], func=AF.Exp, scale=-1.0),
                             reads=[("rec", ri)], writes=[("rec2", ri)], n=128, tset="lnexp")
                        S.op("dve", lambda e, ob=ob, ri=ri, X=X: e.tensor_tensor(out=mixT[:, X, qc0:qc0 + nq], in0=banks[ob][:, 0:nq],
                                                                                in1=rec2[ri][:, 0:nq], op=ALU.mult),
                             reads=[("ps", ob), ("rec2", ri)], writes=[("mix", X)], n=128)
                for qb__ in range(nblk):
                    attn_block(qb__)
                if not last:
                    S.op("act", lambda e: e.activation(out=kT[l][:, 0:128], in_=kT[l][:, tw:tw + 128], func=AF.Copy),
                         reads=[("kT", l)], writes=[("kT", l)])
                    S.op("dve", lambda e: e.tensor_copy(out=vtm[l][:, 0, :], in_=vtm[l][:, nblk, :]), reads=[("vtm", l)], writes=[("vtm", l)])

                chk('attn')
                MIX = [("mix", c) for c in range(8)]
                for n2 in range(4):
                    wvO, wkO = wload(w_out_p[l][:, n2 * 256:(n2 + 1) * 256].rearrange("(k p) n -> p k n", p=128), (8, 256))
                    for h2 in range(2):
                        n = n2 * 2 + h2
                        bo = next_bank()
                        mm_group(bo, lambda bo=bo: banks[bo][:, 0:tw], [wvO[:, k, h2 * 128:(h2 + 1) * 128] for k in range(8)],
                                 [mixT[:, k, 0:tw] for k in range(8)], reads=MIX + wkO)
                        S.op("dve", lambda e, bo=bo, n=n: e.tensor_tensor(out=xT[:, n, 0:tw], in0=banks[bo][:, 0:tw], in1=xT[:, n, 0:tw], op=ALU.add),
                             reads=[("ps", bo), XK(n)], writes=[XK(n)])

                chk('wout')
                rmsnorm(l, C_GFFN, 1, tw, xT, XK)
                chk('norm2')
                fcw = lo + C_FCW
                pending = []

                def ffn_A(j):
                    wvU, wkU = wload_up(l, j)
                    s_ = j % 2
                    Ub, cg, cvv, sg = ub2[s_], SC[6 + s_], SC[8 + s_], SC[10 + s_]
                    kU, kcg, kcv, ksg = f"U{s_}", f"S{6 + s_}", f"S{8 + s_}", f"S{10 + s_}"
                    hs = ((0, cg, kcg), (1, cvv, kcv))
                    bus = []
                    for (h, cb_, ck_) in hs:
                        bu = next_bank()
                        bus.append(bu)
                        if j < 2:
                            mm_split(bu, [wvU[:, h, k, :] for k in range(8)], tw, wkU)
                        else:
                            mm_group(bu, lambda bu=bu: banks[bu][:, 0:tw], [wvU[:, h, k, :] for k in range(8)],
                                     [hn[:, k, 0:tw] for k in range(8)], reads=HN + wkU)
                    S.op("act", lambda e: e.activation(out=Ub[:, :, 0:2], in_=fhalo[l][:, :, :, j].rearrange("p t h -> p h t"), func=AF.Copy),
                         reads=[("fhalo", l, j)], writes=[kU + "h"], n=4)
                    for (h, cb_, ck_), bu in zip(hs, bus):
                        jj = h * NJ + j
                        S.op("act", lambda e, h=h, bu=bu: e.activation(out=Ub[:, h, 2:2 + tw], in_=banks[bu][:, 0:tw], func=AF.Copy),
                             reads=[("ps", bu)], writes=[kU + str(h)])
                        S.op("act", lambda e, cb_=cb_, bu=bu, jj=jj: e.activation(out=cb_[:, 0:tw], in_=banks[bu][:, 0:tw], func=AF.Identity,
                                                                                scale=colt[:, fcw + jj * 3 + 2:fcw + jj * 3 + 3]),
                             reads=[("ps", bu), "colt"], writes=[ck_])
                    for tap in (1, 0):
                        for (h, cb_, ck_) in hs:
                            jj = h * NJ + j
                            S.op("dve", lambda e, h=h, cb_=cb_, jj=jj, tap=tap: e.scalar_tensor_tensor(
                                out=cb_[:, 0:tw], in0=Ub[:, h, tap:tap + tw], scalar=colt[:, fcw + jj * 3 + tap:fcw + jj * 3 + tap + 1],
                                in1=cb_[:, 0:tw], op0=ALU.mult, op1=ALU.add),
                                reads=[kU + str(h), kU + "h", ck_, "colt"], writes=[ck_])
                    S.op("dve", lambda e: e.tensor_copy(out=fhalo[l][:, :, :, j].rearrange("p t h -> p h t"), in_=Ub[:, :, tw:tw + 2]),
                         reads=[kU + "0", kU + "1"], writes=[("fhalo", l, j)], n=4)

                    def ffn_B():
                        S.op("act", lambda e: e.activation(out=sg[:, 0:tw], in_=cg[:, 0:tw], func=AF.Silu), reads=[kcg], writes=[ksg], tset="silu")
                        S.op("dve", lambda e: e.tensor_tensor(out=act[:, j, 0:tw], in0=sg[:, 0:tw], in1=cvv[:, 0:tw], op=ALU.mult),
                             reads=[ksg, kcv], writes=[("act", j)])
                    return ffn_B

                for j in range(NJ):
                    fb = ffn_A(j)
                    if pending:
                        pending.pop(0)()
                    pending.append(fb)
                while pending:
                    pending.pop(0)()
                if last:
                    dst = (o_sffn if sample else o_pffn)[0 if sample else b_idx][l]
                    state_out_T(fhalo[l][:].rearrange("p t h j -> p (t h j)"), 88, [("fhalo", l, jj_) for jj_ in range(NJ)], dst)
                chk('ffnup')
                ACTK = [("act", j) for j in range(NJ)]
                for n in range(8):
                    hk = NJ // 2
                    wvW0, wkW0 = wload(w_down[l][0:hk * 128, n * 128:(n + 1) * 128].rearrange("(k p) n -> p k n", p=128), (hk, 128))
                    wvW1, wkW1 = wload(w_down[l][hk * 128:DFF, n * 128:(n + 1) * 128].rearrange("(k p) n -> p k n", p=128), (hk, 128))
                    bd = next_bank()
                    lhs_all = [wvW0[:, k, :] for k in range(hk)] + [wvW1[:, k, :] for k in range(hk)]
                    for (k0, k1, wk_) in ((0, hk, wkW0), (hk, NJ, wkW1)):
                        def dfn(e, bd=bd, k0=k0, k1=k1, lhs_all=lhs_all):
                            ins = None
                            for k in range(k0, k1):
                                ins = e.matmul(banks[bd][:, 0:tw], lhsT=lhs_all[k], rhs=act[:, k, 0:tw], start=(k == 0), stop=(k == NJ - 1))
                            return ins
                        S.op("pe", dfn, reads=ACTK[k0:k1] + wk_, writes=[("ps", bd)], dur=(k1 - k0) * 0.218)
                    S.op("dve", lambda e, bd=bd, n=n: e.tensor_tensor(out=xT[:, n, 0:tw], in0=banks[bd][:, 0:tw], in1=xT[:, n, 0:tw], op=ALU.add),
                         reads=[("ps", bd), XK(n)], writes=[XK(n)])

            for l_ in range(DEPTH):
                run_layer(l_)
                chk('layer')

            for tb in range(nblk):
                bs2 = [next_bank(), next_bank()]
                fi = tb % 2
                for half in range(2):
                    def tfn2(e, half=half, bs2=bs2, tb=tb):
                        ins = None
                        for c4 in range(4):
                            c = half * 4 + c4
                            ins = e.transpose(out=banks[bs2[half]][0:ntok, c4 * 128:(c4 + 1) * 128], in_=xT[:, c, tb * 128:tb * 128 + ntok],
                                              identity=ident[:, :])
                        return ins
                    S.op("pe", tfn2, reads=[XK(half * 4 + c4) for c4 in range(4)] + ["ident"], writes=[("ps", bs2[half])], dur=0.5)
                    S.op("act", lambda e, half=half, bs2=bs2, fi=fi: e.activation(out=SC[half][0:ntok, 0:512], in_=banks[bs2[half]][0:ntok, 0:512],
                                                                                func=AF.Square, accum_out=fin_ss[fi][0:ntok, half:half + 1]),
                         reads=[("ps", bs2[half])], writes=[f"S{half}", ("fss", fi)])
                S.op("dve", lambda e, fi=fi: e.tensor_tensor(out=fin_ss[fi][0:ntok, 2:3], in0=fin_ss[fi][0:ntok, 0:1], in1=fin_ss[fi][0:ntok, 1:2], op=ALU.add),
                     reads=[("fss", fi)], writes=[("fss", fi)])
                S.op("act", lambda e, fi=fi: e.activation(out=fin_ss[fi][0:ntok, 3:4], in_=fin_ss[fi][0:ntok, 2:3], func=AF.Ln,
                                                         bias=colt[0:ntok, C_EPS:C_EPS + 1], scale=1.0 / D),
                     reads=[("fss", fi)], writes=[("fss", fi)], n=2, tset="lnexp")
                S.op("act", lambda e, fi=fi: e.activation(out=fin_r[fi][0:ntok, 0:1], in_=fin_ss[fi][0:ntok, 3:4], func=AF.Exp, scale=-0.5),
                     reads=[("fss", fi)], writes=[("finr", fi)], n=2, tset="lnexp")
                yi = state["yo"]
                state["yo"] = (yi + 1) % 2
                for half in range(2):
                    S.op("dve", lambda e, half=half, bs2=bs2, fi=fi, yi=yi: e.scalar_tensor_tensor(
                        out=yo[yi][0:ntok, half * 512:(half + 1) * 512], in0=banks[bs2[half]][0:ntok, 0:512], scalar=fin_r[fi][0:ntok, 0:1],
                        in1=bct[0:ntok, 1024 + half * 512:1024 + (half + 1) * 512], op0=ALU.mult, op1=ALU.mult),
                        reads=[("ps", bs2[half]), ("finr", fi), "bct"], writes=[("yo", yi)])
                S.dma("sp", lambda e, yi=yi, tb=tb: e.dma_start(out=ydst[0 if sample else b_idx][t0 + tb * 128:t0 + tb * 128 + ntok, :],
                                                              in_=yo[yi][0:ntok, :]),
                      f"d_yo{yi}", reads=[("yo", yi)], is_output=True)

        run_unit_inner = run_unit

        def run_unit(*a):
            run_unit_inner(*a)
            chk('unit')

        units = [("p", b, u) for b in range(2) for u in range(NUNIT)]
        try:
            chk("setup")
            for i, (k_, b_, u_) in enumerate(units):
                nxt = None
                if i + 1 < len(units):
                    nxt = (units[i + 1][1], units[i + 1][2])
                run_unit(k_, b_, u_, nxt)
            run_unit("s", 0, 0, None)
        except _Stop as ex:
            print("STOPPED at", ex)
        S.finish()
        _DBG["est"] = S.est_total
        _DBG["S"] = S
        _DBG["nops"] = len(S.ops)

        semkeys = set()
        for e_ in Sched.ENGS:
            for it in S.prog[e_]:
                if it[0] == "wait":
                    semkeys.add(it[1])
                else:
                    semkeys.add(it[2])
        sems = {}
        for k_ in sorted(semkeys):
            sems[k_] = es.enter_context(nc.semaphore(k_.replace("_", "")))

        def replay(eng_name):
            def body(e):
                for it in S.prog[eng_name]:
                    if it[0] == "wait":
                        e.wait_ge(sems[it[1]], it[2])
                    else:
                        ins = it[1](e)
                        ins.then_inc(sems[it[2]], it[3])
            return body

        with nc.Block() as block:
            block.sync(replay("sp"))
            block.tensor(replay("pe"))
            block.scalar(replay("act"))
            block.vector(replay("dve"))
            block.gpsimd(replay("pool"))
    return nc


_CACHE = {}


def kernel(**inp):
    inp = {k: np.asarray(v) for k, v in inp.items()}
    shared = _prep_shared(inp)
    if "nc" not in _CACHE:
        _CACHE["nc"] = build_program()
    nc = _CACHE["nc"]
    in_maps = []
    for i in range(NCORES):
        m = dict(shared)
        m["x_p"] = np.ascontiguousarray(inp["x_prompt"][2 * i:2 * i + 2])
        m["x_s"] = np.ascontiguousarray(inp["x_sample"][i:i + 1])
        m["ck"] = np.ascontiguousarray(inp["cache_attn_k"][i].reshape(DEPTH, 128, 128))
        m["cv"] = np.ascontiguousarray(inp["cache_attn_v"][i].reshape(DEPTH, 128, 128))
        m["st_conv"] = np.ascontiguousarray(inp["state_conv"][i])
        m["st_pool"] = np.ascontiguousarray(inp["state_pool"][i])
        m["st_ffn"] = np.ascontiguousarray(inp["state_ffn_conv"][i])
        in_maps.append(m)
    res = run_bass_kernel_spmd(nc, in_maps, core_ids=list(range(NCORES)))
    R = res.results

    def cat(name):
        return np.concatenate([np.asarray(r[name]) for r in R], axis=0)

    B, DB = 16, 8
    y_prompt = cat("y_p")
    y_sample = cat("y_s")
    pk = cat("o_pk").reshape(B, DEPTH, 128, 2, 64)
    pv = cat("o_pv").reshape(B, DEPTH, 128, 2, 64)
    pconv = cat("o_pconv").reshape(B, DEPTH, 2, 256)
    ppool = cat("o_ppool").reshape(B, DEPTH, 15, 256)
    pffn = cat("o_pffn").reshape(B, DEPTH, 2, 2 * DFF)
    sk = cat("o_sk").reshape(DB, DEPTH, DEC_SEQ, 2, 64)
    sv = cat("o_sv").reshape(DB, DEPTH, DEC_SEQ, 2, 64)
    sconv = cat("o_sconv").reshape(DB, DEPTH, 2, 256)
    spool = cat("o_spool").reshape(DB, DEPTH, 15, 256)
    sffn = cat("o_sffn").reshape(DB, DEPTH, 2, 2 * DFF)
    sgv = cat("o_sgv").reshape(DB, DEPTH, DEC_SEQ, 256)
    outs = (y_prompt, y_sample, pk, pv, pconv, ppool, pffn, sk, sv, sconv, spool, sffn, sgv)
    return tuple(np.ascontiguousarray(o, dtype=np.float32) for o in outs)
```

```python
import numpy as np
from contextlib import ExitStack
import concourse.bass as bass
import concourse.mybir as mybir
from concourse.bass_utils import run_bass_kernel_spmd

F32 = mybir.dt.float32
BF16 = mybir.dt.bfloat16
AF = mybir.ActivationFunctionType
ALU = mybir.AluOpType

NCORES = 8
D = 1024
SEQ = 2048
DEPTH = 2
DEC_SEQ = 32
PAST = 2048
DFF = 2816
NJ = DFF // 128
U = 512
NUNIT = SEQ // U
EPS = 1e-6
NCOLW = 19 * 128
WSLOT = 3072
NSMALL = 6
NBIG = 0

C_GMIX, C_GFFN, C_CONV, C_FCW, C_PSC, C_SINK = 0, 8, 16, 22, 22 + 132, 22 + 132 + 2
C_LAYER = 22 + 132 + 4
C_INVW = 2 * C_LAYER
C_RC = C_INVW + 2
C_EPS = C_RC + 32
C_EPSN = C_EPS + 1
NCOLS = C_EPSN + 1


class _Stop(Exception):
    pass


_DBG = {"stop": None, "n": 0, "log": []}


def chk(name):
    _DBG["n"] += 1
    _DBG["log"].append((_DBG["n"], name))
    if _DBG["stop"] is not None and _DBG["n"] >= _DBG["stop"]:
        raise _Stop(name)


class Sched:
    ENGS = ("pe", "act", "dve", "pool", "sp")
    LAT = 0.08
    DMA_FIXED = 2.0
    DMA_BW = 280e3

    def __init__(self):
        self.ops = []
        self.prog = {e: [] for e in self.ENGS}
        self.reorder = True

    def op(self, eng, fn, reads=(), writes=(), n=512, dur=None, tset=None):
        if dur is None:
            if eng == "pe":
                dur = 0.03 + max(64, n) / 2400.0
            elif eng == "act":
                dur = 0.17 + n / 1200.0
            elif eng == "dve":
                dur = 0.13 + n / 960.0
            else:
                dur = 0.3 + n / 500.0
        import sys as _sys
        self.ops.append(dict(kind="op", eng=eng, fn=fn, reads=list(reads), writes=list(writes), dur=dur, tset=tset,
                             line=_sys._getframe(1).f_lineno))

    def dma(self, eng, fn, semkey, reads=(), writes=(), is_output=False, nbytes=65536):
        issue = 1.1 if eng == "pool" else 0.12
        self.ops.append(dict(kind="dma", eng=eng, fn=fn, semkey=semkey, reads=list(reads), writes=list(writes),
                             dur=issue, nbytes=nbytes, is_output=is_output))

    def _analyse(self):
        ops = self.ops
        last_w = {}
        readers = {}
        for i, o in enumerate(ops):
            preds = {}

            def add(j, typ):
                if j is None or j == i:
                    return
                if typ == "raw" or j not in preds:
                    preds[j] = typ if preds.get(j) != "raw" else "raw"
            for b in o["reads"]:
                add(last_w.get(b), "raw")
                if isinstance(b, tuple) and b[0] == "ps":
                    for r in readers.get(b, ()):
                        if ops[r]["eng"] != o["eng"]:
                            add(r, "excl")
            for b in o["writes"]:
                add(last_w.get(b), "waw")
                for r in readers.get(b, ()):
                    add(r, "war")
            for b in o["reads"]:
                readers.setdefault(b, []).append(i)
            for b in o["writes"]:
                last_w[b] = i
                readers[b] = []
            o["preds"] = preds

    def _schedule(self):
        import heapq
        ops = self.ops
        n = len(ops)
        succs = [[] for _ in range(n)]
        indeg = [0] * n
        for i, o in enumerate(ops):
            indeg[i] = len(o["preds"])
            for j in o["preds"]:
                succs[j].append(i)
        tail = [0.0] * n
        for i in range(n - 1, -1, -1):
            o = ops[i]
            d = o["dur"] + (self.DMA_FIXED + o["nbytes"] / self.DMA_BW if o["kind"] == "dma" else 0.0)
            m = 0.0
            for k in succs[i]:
                if tail[k] > m:
                    m = tail[k]
            tail[i] = d + m
        ready = {e: [] for e in self.ENGS}
        for i in range(n):
            if indeg[i] == 0:
                heapq.heappush(ready[ops[i]["eng"]], i)
        t_eng = {e: 0.0 for e in self.ENGS}
        finish = [0.0] * n
        order = {e: [] for e in self.ENGS}
        dma_free = 0.0
        cur_set = None
        done = 0
        K = 16 if self.reorder else 1
        DELTA = 0.25
        while done < n:
            best = None
            for e in self.ENGS:
                h = ready[e]
                if not h:
                    continue
                cands = heapq.nsmallest(K, h)
                for i in cands:
                    est = t_eng[e]
                    for j in ops[i]["preds"]:
                        f = finish[j] + (self.LAT if ops[j]["eng"] != e or ops[j]["kind"] == "dma" else 0.0)
                        if f > est:
                            est = f
                    ts_ = ops[i].get("tset")
                    if e == "act" and ts_ is not None and ts_ != cur_set:
                        est += 1.3
                    if self.reorder:
                        key = (est if est > t_eng[e] + DELTA else t_eng[e], -tail[i], i)
                    else:
                        key = (i, est)
                    if best is None or key < best[0]:
                        best = (key, e, i, est)
            _, e, i, est = best
            ready[e].remove(i)
            heapq.heapify(ready[e])
            o = ops[i]
            if o["kind"] == "dma":
                t_eng[e] = est + o["dur"]
                st = max(t_eng[e], dma_free)
                dma_free = st + o["nbytes"] / self.DMA_BW
                finish[i] = dma_free + self.DMA_FIXED
            else:
                t_eng[e] = est + o["dur"]
                finish[i] = t_eng[e]
                if e == "act" and o.get("tset") is not None:
                    cur_set = o["tset"]
            order[e].append(i)
            o["t0"] = est
            done += 1
            for k in succs[i]:
                indeg[k] -= 1
                if indeg[k] == 0:
                    heapq.heappush(ready[ops[k]["eng"]], k)
        self.order = order
        self.est_total = max(finish) if n else 0.0

    def finish(self):
        self._analyse()
        self._schedule()
        ops = self.ops
        tok = [None] * len(ops)
        pos = [0] * len(ops)
        dcnt = {}
        for e in self.ENGS:
            c = 0
            for i in self.order[e]:
                o = ops[i]
                if o["kind"] == "dma":
                    dcnt[o["semkey"]] = dcnt.get(o["semkey"], 0) + 16
                    tok[i] = (o["semkey"], dcnt[o["semkey"]])
                else:
                    c += 1
                    tok[i] = ("E_" + e, c)
                    pos[i] = c
        out_tokens = {}
        for e in self.ENGS:
            seen = {}
            own = "E_" + e
            c = 0
            prog = self.prog[e]
            for i in self.order[e]:
                o = ops[i]
                for j, typ in o["preds"].items():
                    pj = ops[j]
                    t = tok[j]
                    if pj["kind"] != "dma" and pj["eng"] == e:
                        if not (e in ("act", "dve", "pool") and o["kind"] == "op"):
                            continue
                    if seen.get(t[0], 0) < t[1]:
                        prog.append(("wait", t[0], t[1]))
                        seen[t[0]] = t[1]
                if o["kind"] == "dma":
                    prog.append(("ins", o["fn"], o["semkey"], 16))
                    if o["is_output"]:
                        out_tokens[o["semkey"]] = max(out_tokens.get(o["semkey"], 0), tok[i][1])
                else:
                    c += 1
                    prog.append(("ins", o["fn"], own, 1))
            if e == "sp":
                self._sp_seen = seen
        for k, v in out_tokens.items():
            if self._sp_seen.get(k, 0) < v:
                self.prog["sp"].append(("wait", k, v))


def _rope_tables(pos):
    half = 8
    inv_freq = np.power(np.float32(500000.0), -np.arange(half, dtype=np.float32) * np.float32(2.0 / 16)).astype(np.float32)
    ang = pos.astype(np.float32)[:, None] * inv_freq[None, :]
    cos = np.cos(ang).astype(np.float32).T
    sin = np.sin(ang).astype(np.float32).T
    T = pos.shape[0]
    ct = np.ones((128, T), np.float32)
    st = np.zeros((128, T), np.float32)
    for hh in range(2):
        b = hh * 64
        ct[b:b + 8] = cos
        ct[b + 8:b + 16] = cos
        st[b:b + 8] = -sin
        st[b + 8:b + 16] = sin
    return np.stack([ct, st], 0)


def _head_cols(base, h, swap):
    c = np.arange(64) + base + 64 * h
    if swap:
        c = np.concatenate([c[8:16], c[0:8], c[16:]])
    return c


def _prep_shared(inp):
    w_in = inp["w_in"]
    qb, kb = 0, 256
    cols = []
    for sw in (False, True):
        cols += [_head_cols(qb, 0, sw), _head_cols(qb, 2, sw)]
        cols += [_head_cols(qb, 1, sw), _head_cols(qb, 3, sw)]
        cols += [_head_cols(kb, 0, sw), _head_cols(kb, 1, sw)]
    cols.append(np.arange(512, 1280))
    cols.append(np.arange(1280, 1536))
    cols.append(np.arange(1792, 2048))
    cols.append(np.arange(384, 512))
    cols.append(np.arange(1536, 1792))
    cols = np.concatenate(cols)
    assert cols.shape[0] == NCOLW
    w_in_p = np.ascontiguousarray(w_in[:, :, cols])
    rperm = np.concatenate([np.arange(0, 64), np.arange(128, 192), np.arange(64, 128), np.arange(192, 256), np.arange(256, 1024)])
    w_out_p = np.ascontiguousarray(inp["w_out"][:, rperm, :])

    colt = np.zeros((128, NCOLS), np.float32)
    for l in range(DEPTH):
        o = l * C_LAYER
        colt[:, o + C_GMIX:o + C_GMIX + 8] = inp["g_mix"][l].reshape(8, 128).T
        colt[:, o + C_GFFN:o + C_GFFN + 8] = inp["g_ffn"][l].reshape(8, 128).T
        cw = inp["conv_w"][l]
        for hf in range(2):
            colt[:, o + C_CONV + hf * 3:o + C_CONV + hf * 3 + 3] = cw[:, hf * 128:(hf + 1) * 128].T
        fw = inp["ffn_conv_w"][l]
        colt[:, o + C_FCW:o + C_FCW + 132] = fw.reshape(3, 44, 128).transpose(2, 1, 0).reshape(128, 132)
        colt[:, o + C_PSC:o + C_PSC + 2] = inp["pool_scale"][l].reshape(2, 128).T
        sk = inp["attn_sink"][l]
        colt[0:64, o + C_SINK] = sk[0]
        colt[64:128, o + C_SINK] = sk[2]
        colt[0:64, o + C_SINK + 1] = sk[1]
        colt[64:128, o + C_SINK + 1] = sk[3]
    wins = (2.0, 4.0, 8.0, 16.0)
    for hf in range(2):
        for q in range(2):
            win = wins[hf * 2 + q]
            colt[q * 64:(q + 1) * 64, C_INVW + hf] = np.float32(1.0) / np.float32(win)
            cnt = np.minimum(np.arange(16) + 1, win).astype(np.float32)
            colt[q * 64:(q + 1) * 64, C_RC + hf * 16:C_RC + hf * 16 + 16] = (np.float32(1.0) / cnt)[None, :]
    colt[:, C_EPS] = EPS
    colt[:, C_EPSN] = EPS * 1024

    bc = np.zeros((128, 2 * 2 * 256 + 1024), np.float32)
    for l in range(DEPTH):
        bc[:, (l * 2 + 0) * 256:(l * 2 + 1) * 256] = inp["gmlp_ln_g"][l][None, :]
        bc[:, (l * 2 + 1) * 256:(l * 2 + 2) * 256] = inp["gmlp_ln_b"][l][None, :]
    bc[:, 1024:] = inp["g_final"][None, :]

    wmT = np.ascontiguousarray(inp["gmlp_w"].transpose(3, 0, 1, 2))
    bsr = np.ascontiguousarray(inp["gmlp_b"].reshape(1, DEPTH * 4 * 128))
    wpb = np.zeros((128, DEPTH, 2, 128), np.float32)
    for l in range(DEPTH):
        for g in range(4):
            hf, q = g // 2, g % 2
            wpb[q * 64:(q + 1) * 64, l, hf, q * 64:(q + 1) * 64] = inp["pool_w"][l, g]
    return {
        "w_in_p": w_in_p, "w_out_p": w_out_p, "w_up": np.ascontiguousarray(inp["w_up"]),
        "w_down": np.ascontiguousarray(inp["w_down"]),
        "colt": colt, "bc": bc, "wmT": wmT.reshape(128, DEPTH * 4 * 128), "bsr": bsr,
        "wpb": wpb.reshape(128, DEPTH * 2 * 128),
        "ident": np.eye(128, dtype=np.float32),
        "tab_p": _rope_tables(np.arange(SEQ)), "tab_s": _rope_tables(PAST + np.arange(DEC_SEQ)),
    }


def build_program():
    nc = bass.Bass("TRN2", target_bir_lowering=False)
    S = Sched()

    def din(name, shape):
        return nc.dram_tensor(name, list(shape), F32, kind="ExternalInput").ap()

    def dout(name, shape):
        return nc.dram_tensor(name, list(shape), F32, kind="ExternalOutput").ap()

    x_p = din("x_p", (2, SEQ, D))
    x_s = din("x_s", (1, DEC_SEQ, D))
    ck = din("ck", (DEPTH, 128, 128))
    cv = din("cv", (DEPTH, 128, 128))
    st_conv = din("st_conv", (DEPTH, 2, 256))
    st_pool = din("st_pool", (DEPTH, 15, 256))
    st_ffn = din("st_ffn", (DEPTH, 2, 2 * DFF))
    w_in_p = din("w_in_p", (DEPTH, D, NCOLW))
    w_out_p = din("w_out_p", (DEPTH, D, D))
    w_up = din("w_up", (DEPTH, D, 2 * DFF))
    w_down = din("w_down", (DEPTH, DFF, D))
    colt_d = din("colt", (128, NCOLS))
    bc_d = din("bc", (128, 2048))
    wmT_d = din("wmT", (128, DEPTH * 4 * 128))
    bsr_d = din("bsr", (1, DEPTH * 4 * 128))
    wpb_d = din("wpb", (128, DEPTH * 2 * 128))
    ident_d = din("ident", (128, 128))
    tab_p = din("tab_p", (2, 128, SEQ))
    tab_s = din("tab_s", (2, 128, DEC_SEQ))

    NIMG_L = 52
    wimg = nc.dram_tensor("wimg", [DEPTH * NIMG_L, 128, 2048], BF16, kind="Internal").ap()

    y_p = dout("y_p", (2, SEQ, D))
    y_s = dout("y_s", (1, DEC_SEQ, D))
    o_pk = dout("o_pk", (2, DEPTH, 128, 128))
    o_pv = dout("o_pv", (2, DEPTH, 128, 128))
    o_pconv = dout("o_pconv", (2, DEPTH, 4, 128))
    o_ppool = dout("o_ppool", (2, DEPTH, 30, 128))
    o_pffn = dout("o_pffn", (2, DEPTH, 88, 128))
    o_sk = dout("o_sk", (1, DEPTH, DEC_SEQ, 128))
    o_sv = dout("o_sv", (1, DEPTH, DEC_SEQ, 128))
    o_sconv = dout("o_sconv", (1, DEPTH, 4, 128))
    o_spool = dout("o_spool", (1, DEPTH, 30, 128))
    o_sffn = dout("o_sffn", (1, DEPTH, 88, 128))
    o_sgv = dout("o_sgv", (1, DEPTH, DEC_SEQ, 256))

    es = ExitStack()
    with es:
        def sb(name, shape, dt=F32):
            n = 1
            for d_ in shape[1:]:
                n *= d_
            _DBG["sbuf"] = _DBG.get("sbuf", 0) + n * (2 if dt == BF16 else 4)
            return es.enter_context(nc.sbuf_tensor(name, list(shape), dt))

        xTs = [sb(f"xT{i}", (128, 8, U)) for i in range(2)]
        hn = sb("hn", (128, 8, U), BF16)
        mixT = sb("mixT", (128, 8, U), BF16)
        act = sb("act", (128, NJ, U), BF16)
        wsS = [sb(f"wsS{i}", (128, 2048), BF16) for i in range(NSMALL)]
        wsB = [sb(f"wsB{i}", (128, WSLOT), BF16) for i in range(NBIG)]
        state_unit = [0]
        xs = [sb(f"xs{i}", (128, D)) for i in range(2)]
        yo = [sb(f"yo{i}", (128, D)) for i in range(2)]
        tabc = [sb(f"tabc{i}", (128, U)) for i in range(1)]
        tabs = [sb(f"tabs{i}", (128, U)) for i in range(1)]
        sq = [sb(f"sq{i}", (128, U), BF16) for i in range(2)]
        SC = [sb(f"scr{i}", (128, 528)) for i in range(12)]
        qab = [sb(f"qab{i}", (128, U), BF16) for i in range(2)]
        kT = [sb(f"kT{l}", (128, 128 + U), BF16) for l in range(DEPTH)]
        vtm = [sb(f"vtm{l}", (128, 1 + U // 128, 128), BF16) for l in range(DEPTH)]
        Pb = [[[sb(f"P{s}{h}{k}", (128, 128), BF16) for k in range(2)] for h in range(2)] for s in range(2)]
        rec = [sb(f"rec{i}", (128, 128)) for i in range(2)]
        rec2 = [sb(f"recb{i}", (128, 128)) for i in range(2)]
        prod = [sb(f"prod{l}", (128, 2, 2 + U)) for l in range(DEPTH)]
        ubuf = sb("ubuf", (128, 2, U))
        gv = [sb(f"gv{i}", (128, 256)) for i in range(2)]
        vn = [sb(f"vn{i}", (128, 256)) for i in range(2)]
        vlnb = [sb(f"vlnb{i}", (128, 256), BF16) for i in range(2)]
        vlnf = sb("vlnf", (128, 256))
        st6 = [sb(f"st6{i}", (128, 6)) for i in range(2)]
        mv = [sb(f"mv{i}", (128, 2)) for i in range(2)]
        lnr = [sb(f"lnr{i}", (128, 2)) for i in range(2)]
        lnr2 = [sb(f"lnrb{i}", (128, 2)) for i in range(2)]
        pin = [sb(f"pin{l}", (128, 2, 15 + U)) for l in range(DEPTH)]
        pooled = [sb(f"pooled{i}", (128, U), BF16) for i in range(2)]
        ptmp = sb("ptmp", (128, 16))
        fhalo = [sb(f"fhalo{l}", (128, 2, 2, NJ)) for l in range(DEPTH)]
        ub2 = [sb(f"ub2{i}", (128, 2, 516)) for i in range(2)]
        cstg = sb("cstg", (128, 2, 2))
        pstg = sb("pstg", (128, 15, 2))
        stg = [sb(f"stg{i}", (128, 128)) for i in range(3)]
        fin_ss = [sb(f"fss{i}", (128, 4)) for i in range(2)]
        fin_r = [sb(f"finr{i}", (128, 2)) for i in range(2)]
        colt = sb("coltS", (128, NCOLS))
        esink = sb("esink", (128, 2 * DEPTH))
        bct = sb("bcS", (128, 2048))
        ident = sb("identS", (128, 128))
        ones_bf = sb("ones_bf", (128, 128), BF16)
        wmT = sb("wmTS", (128, DEPTH * 4 * 128), BF16)
        bsr = sb("bsrS", (1, DEPTH * 4 * 128), BF16)
        wpb = sb("wpbS", (128, DEPTH * 2 * 128), BF16)
        tsc = sb("tsc", (128, DEC_SEQ))
        tss = sb("tss", (128, DEC_SEQ))
        ckst = sb("ckst", (128, 128))
        cvst = sb("cvst", (128, 128))
        banks = [es.enter_context(nc.psum_tensor(f"ps{i}", [128, 512], F32)) for i in range(8)]

        semnames = {}
        state = {"bank": 0, "wsS": 0, "wsB": 0, "stg": 0, "yo": 0, "wseq": 0, "wl": 0, "conv": True}

        def next_bank():
            i = state["bank"]
            state["bank"] = (i + 1) % 8
            return i

        def setup_load(dst_t, dst_ap, src_ap, eng="sp"):
            S.dma(eng, lambda e, o=dst_ap, i=src_ap: e.dma_start(out=o, in_=i), "d_" + dst_t, writes=[dst_t])

        setup_load("colt", colt[:], colt_d)
        setup_load("bct", bct[:], bc_d)
        setup_load("ident", ident[:], ident_d)
        setup_load("tsc", tsc[:], tab_s[0])
        setup_load("tss", tss[:], tab_s[1])
        setup_load("wmT", wmT[:], wmT_d, eng="pool")
        setup_load("bsr", bsr[:], bsr_d, eng="pool")
        setup_load("wpb", wpb[:], wpb_d, eng="pool")
        S.op("dve", lambda e: e.memset(ones_bf[:], 1.0), writes=["ones"])
        for s_ in range(2):
            for h_ in range(2):
                for k_ in range(2):
                    S.op("dve", lambda e, t=Pb[s_][h_][k_]: e.memset(t[:], 0.0), writes=[("P", s_, h_, k_)])
        for l in range(DEPTH):
            o = l * C_LAYER + C_SINK
            S.op("act", lambda e, l=l, o=o: e.activation(out=esink[:, 2 * l:2 * l + 2], in_=colt[:, o:o + 2], func=AF.Exp),
                 reads=["colt"], writes=[("esink", l)])

        def col(l, base, i=0, n=1):
            o = l * C_LAYER + base + i
            return colt[:, o:o + n]

        def _slot(n):
            kind = "wsB" if n > 2048 else "wsS"
            cnt = NBIG if kind == "wsB" else NSMALL
            i = state[kind]
            state[kind] = (i + 1) % cnt
            t = (wsB if kind == "wsB" else wsS)[i]
            return t, [(kind, i), (kind + "b", i)], f"d_{kind}{i}"

        def _img_index():
            idx = state["wl"] * NIMG_L + state["wseq"]
            state["wseq"] += 1
            assert state["wseq"] <= NIMG_L
            return idx

        def _img_store(t, n, key, idx, si):
            S.dma("sp", lambda e: e.dma_start(out=wimg[idx][:, 0:n], in_=t[:, 0:n]), f"d_wst{si}",
                  reads=key, writes=[("wimg", idx)], nbytes=n * 128 * 2)

        def _img_load(t, n, key, idx, semk):
            S.dma("sp", lambda e: e.dma_start(out=t[:, 0:n], in_=wimg[idx][:, 0:n]), semk + "i",
                  reads=[("wimg", idx)], writes=key, nbytes=n * 128 * 2)

        def wload(src_ap, view):
            n = 1
            for d_ in view:
                n *= d_
            t, key, semk = _slot(n)
            idx = _img_index()
            dst = t[:, 0:n].rearrange("p (k n) -> p k n", k=view[0])
            if state["conv"]:
                S.dma("pool", lambda e, o=dst, s=src_ap: e.dma_start(out=o, in_=s), semk, writes=key, nbytes=n * 128 * 4)
                _img_store(t, n, key, idx, semk)
            else:
                _img_load(t, n, key, idx, semk)
            return dst, key

        def wload_up(l, j):
            t, key, semk = _slot(2048)
            idx = _img_index()
            dst = t[:, 0:2048].rearrange("p (h k n) -> p h k n", h=2, k=8)
            if state["conv"]:
                for h in range(2):
                    src = w_up[l][:, h * DFF + j * 128:h * DFF + (j + 1) * 128].rearrange("(k p) n -> p k n", p=128)
                    S.dma("pool", lambda e, o=dst[:, h, :, :], s_=src: e.dma_start(out=o, in_=s_), semk + ("" if h == 0 else "h"), writes=[key[h]], nbytes=1024 * 128 * 4)
                _img_store(t, 2048, key, idx, semk)
            else:
                _img_load(t, 2048, key, idx, semk)
            return dst, key

        def mm_group(bank_i, out_fn, lhs_list, rhs_list, reads, extra_writes=(), ncols=512):
            def fn(e):
                n = len(lhs_list)
                ins = None
                for k in range(n):
                    ins = e.matmul(out_fn(), lhsT=lhs_list[k], rhs=rhs_list[k], start=(k == 0), stop=(k == n - 1))
                return ins
            S.op("pe", fn, reads=reads, writes=[("ps", bank_i)] + list(extra_writes),
                 dur=len(lhs_list) * (0.005 + max(64, ncols) / 2400.0))

        def rmsnorm(l, gbase, nt, tw, xT, XK):
            b = next_bank()
            for c in range(8):
                s_ = sq[c % 2]
                S.op("act", lambda e, c=c, s_=s_: e.activation(out=s_[:, 0:tw], in_=xT[:, c, 0:tw], func=AF.Square),
                     reads=[XK(c)], writes=[("sq", c % 2)])
                S.op("pe", lambda e, c=c, s_=s_, b=b: e.matmul(banks[b][:, 0:tw], lhsT=ones_bf[:, :], rhs=s_[:, 0:tw],
                                                             start=(c == 0), stop=(c == 7)),
                     reads=[("sq", c % 2), "ones"], writes=[("ps", b)])
            S.op("act", lambda e, b=b: e.activation(out=SC[0][:, 0:tw], in_=banks[b][:, 0:tw], func=AF.Ln,
                                                    bias=colt[:, C_EPS:C_EPS + 1], scale=1.0 / D),
                 reads=[("ps", b), "colt"], writes=["S0"], tset="lnexp")
            S.op("act", lambda e: e.activation(out=SC[1][:, 0:tw], in_=SC[0][:, 0:tw], func=AF.Exp, scale=-0.5),
                 reads=["S0"], writes=["S1"], tset="lnexp")
            for c in range(8):
                S.op("dve", lambda e, c=c: e.scalar_tensor_tensor(out=hn[:, c, 0:tw], in0=xT[:, c, 0:tw],
                                                                 scalar=col(l, gbase, c), in1=SC[1][:, 0:tw],
                                                                 op0=ALU.mult, op1=ALU.mult),
                     reads=[XK(c), "S1", "colt"], writes=[("hn", c)])

        HN = [("hn", c) for c in range(8)]

        def mm_split(b, lhs_list, tw, wkey):
            for k in range(8):
                S.op("pe", lambda e, b=b, k=k: e.matmul(banks[b][:, 0:tw], lhsT=lhs_list[k], rhs=hn[:, k, 0:tw],
                                                        start=(k == 0), stop=(k == 7)),
                     reads=[("hn", k)] + wkey, writes=[("ps", b)], dur=0.005 + max(64, tw) / 2400.0)

        def proj_fm(wv, wkey, ccol, tw, split=False):
            b = next_bank()
            if split:
                mm_split(b, [wv[:, k, ccol * 128:(ccol + 1) * 128] for k in range(8)], tw, wkey)
                return b
            mm_group(b, lambda b=b: banks[b][:, 0:tw],
                     [wv[:, k, ccol * 128:(ccol + 1) * 128] for k in range(8)],
                     [hn[:, k, 0:tw] for k in range(8)], reads=HN + wkey, ncols=tw)
            return b

        def state_out_T(src_ap, nrows, rkeys, dst_ap):
            b = next_bank()
            S.op("pe", lambda e, b=b: e.transpose(out=banks[b][0:nrows, 0:128], in_=src_ap, identity=ident[:, :]),
                 reads=list(rkeys) + ["ident"], writes=[("ps", b)])
            si = state["stg"]
            state["stg"] = (si + 1) % 3
            S.op("act", lambda e, b=b, si=si: e.activation(out=stg[si][0:nrows, :], in_=banks[b][0:nrows, 0:128], func=AF.Copy),
                 reads=[("ps", b)], writes=[("stg", si)])
            S.dma("sp", lambda e, si=si: e.dma_start(out=dst_ap, in_=stg[si][0:nrows, :]), f"d_stg{si}",
                  reads=[("stg", si)], is_output=True)

        def run_unit(kind, b_idx, ui, next_loads):
            sample = (kind == "s")
            Uu = DEC_SEQ if sample else U
            tw = Uu
            nblk = 1 if sample else U // 128
            ntok = DEC_SEQ if sample else 128
            first = (not sample) and ui == 0
            last = sample or ui == NUNIT - 1
            xsrc = x_s if sample else x_p
            ydst = y_s if sample else y_p
            t0 = 0 if sample else ui * U
            ti = 0
            par = state_unit[0] % 2
            very_first = state_unit[0] == 0
            state["conv"] = very_first
            state_unit[0] += 1
            xT = xTs[par]

            def XK(c):
                return ("x", par, c)

            if sample:
                ctab, stab, ctk, stk = tsc, tss, "tsc", "tss"
            else:
                ctab, stab, ctk, stk = tabc[ti], tabs[ti], ("tabc", ti), ("tabs", ti)
                S.dma("sp", lambda e: e.dma_start(out=ctab[:], in_=tab_p[0][:, t0:t0 + U]), f"d_tabc{ti}", writes=[ctk])
                S.dma("sp", lambda e: e.dma_start(out=stab[:], in_=tab_p[1][:, t0:t0 + U]), f"d_tabs{ti}", writes=[stk])

            chk('tables')
            for tb in range(nblk):
                slot = tb % 2
                if sample or very_first or tb >= 2:
                    S.dma("sp", lambda e, slot=slot, tb=tb: e.dma_start(out=xs[slot][0:ntok, :],
                                                                       in_=xsrc[b_idx][t0 + tb * 128:t0 + tb * 128 + ntok, :]),
                          f"d_xs{slot}", writes=[("xs", slot)])
                for half in range(2):
                    b = next_bank()

                    def tfn(e, b=b, half=half, slot=slot):
                        ins = None
                        for c4 in range(4):
                            c = half * 4 + c4
                            ins = e.transpose(out=banks[b][:, c4 * 128:c4 * 128 + ntok], in_=xs[slot][0:ntok, c * 128:(c + 1) * 128],
                                              identity=ident[0:ntok, 0:ntok])
                        return ins
                    chk('xdma')
                    S.op("pe", tfn, reads=[("xs", slot), "ident"], writes=[("ps", b)], dur=0.5)
                    chk('xtr')
                    for c4 in range(4):
                        c = half * 4 + c4
                        eng = "act" if half == 0 else "dve"
                        if eng == "act":
                            S.op("act", lambda e, b=b, c=c, c4=c4, tb=tb: e.activation(out=xT[:, c, tb * 128:tb * 128 + ntok],
                                                                                     in_=banks[b][:, c4 * 128:c4 * 128 + ntok], func=AF.Copy),
                                 reads=[("ps", b)], writes=[XK(c)], n=128)
                        else:
                            S.op("dve", lambda e, b=b, c=c, c4=c4, tb=tb: e.tensor_copy(out=xT[:, c, tb * 128:tb * 128 + ntok],
                                                                                      in_=banks[b][:, c4 * 128:c4 * 128 + ntok]),
                                 reads=[("ps", b)], writes=[XK(c)], n=128)
            if next_loads is not None:
                nb, nu = next_loads
                for tb in range(2):
                    S.dma("sp", lambda e, tb=tb: e.dma_start(out=xs[tb][:, :], in_=x_p[nb][nu * U + tb * 128:nu * U + (tb + 1) * 128, :]),
                          f"d_xs{tb}", writes=[("xs", tb)])

            def run_layer(l):
                lo = l * C_LAYER
                state["wl"] = l
                state["wseq"] = 0
                if first:
                    S.op("dve", lambda e, l=l: e.memset(prod[l][:, :, 0:2], 0.0), writes=[("prod", l, 0), ("prod", l, 1)])
                    S.op("dve", lambda e, l=l: e.memset(pin[l][:, :, 0:15], 0.0), writes=[("pin", l, 0), ("pin", l, 1)])
                    S.op("dve", lambda e, l=l: e.memset(fhalo[l][:], 0.0), writes=[("fhalo", l, jj_) for jj_ in range(NJ)])
                if sample:
                    S.dma("sp", lambda e, l=l: e.dma_start(out=ckst[:], in_=ck[l]), "d_ckst", writes=["ckst"])
                    S.dma("sp", lambda e, l=l: e.dma_start(out=cvst[:], in_=cv[l]), "d_cvst", writes=["cvst"])
                    b = next_bank()
                    S.op("pe", lambda e, b=b: e.transpose(out=banks[b][:, 0:128], in_=ckst[:, :], identity=ident[:, :]),
                         reads=["ckst", "ident"], writes=[("ps", b)])
                    S.op("act", lambda e, b=b, l=l: e.activation(out=kT[l][:, 0:128], in_=banks[b][:, 0:128], func=AF.Copy),
                         reads=[("ps", b)], writes=[("kT", l)])
                    S.op("dve", lambda e, l=l: e.tensor_copy(out=vtm[l][:, 0, :], in_=cvst[:, :]), reads=["cvst"], writes=[("vtm", l)])
                    for hf in range(2):
                        S.dma("sp", lambda e, l=l, hf=hf: e.dma_start(out=prod[l][:, hf, 0:2],
                                                                    in_=st_conv[l][:, hf * 128:(hf + 1) * 128].rearrange("t p -> p t"),
                                                                    allow_slow_non_contiguous=True),
                              f"d_sconv{l}{hf}", writes=[("prod", l, hf)])
                    for hf in range(2):
                        S.dma("sp", lambda e, l=l, hf=hf: e.dma_start(out=pin[l][:, hf, 0:15],
                                                                    in_=st_pool[l][:, hf * 128:(hf + 1) * 128].rearrange("t p -> p t"),
                                                                    allow_slow_non_contiguous=True),
                              f"d_spool{l}{hf}", writes=[("pin", l, hf)])
                    for t_ in range(2):
                        S.dma("sp", lambda e, l=l, t_=t_: e.dma_start(out=fhalo[l][:, t_].rearrange("p h j -> p (h j)"),
                                                                    in_=st_ffn[l][t_].rearrange("(j p) -> p j", p=128),
                                                                    allow_slow_non_contiguous=True),
                              f"d_sffn{l}{t_}", writes=[("fhalo", l, jj_) for jj_ in range(NJ)])

                chk('xload')
                rmsnorm(l, C_GMIX, 1, tw, xT, XK)

                chk('norm1')
                wv01, wk01 = wload(w_in_p[l][:, 0:256].rearrange("(k p) n -> p k n", p=128), (8, 256))
                wv23, wk23 = wload(w_in_p[l][:, 256:512].rearrange("(k p) n -> p k n", p=128), (8, 256))
                wv45, wk45 = wload(w_in_p[l][:, 512:768].rearrange("(k p) n -> p k n", p=128), (8, 256))

                def rope(bm, bs_, dst_fn, dkey, keep_f32):
                    S.op("dve", lambda e: e.tensor_tensor(out=SC[2][:, 0:tw], in0=banks[bm][:, 0:tw], in1=ctab[:, 0:tw], op=ALU.mult),
                         reads=[("ps", bm), ctk], writes=["S2"])
                    S.op("dve", lambda e: e.tensor_tensor(out=SC[3][:, 0:tw], in0=banks[bs_][:, 0:tw], in1=stab[:, 0:tw], op=ALU.mult),
                         reads=[("ps", bs_), stk], writes=["S3"])
                    if keep_f32:
                        S.op("dve", lambda e: e.tensor_tensor(out=SC[4][:, 0:tw], in0=SC[2][:, 0:tw], in1=SC[3][:, 0:tw], op=ALU.add),
                             reads=["S2", "S3"], writes=["S4"])
                        S.op("act", lambda e: e.activation(out=dst_fn(), in_=SC[4][:, 0:tw], func=AF.Copy), reads=["S4"], writes=[dkey])
                    else:
                        S.op("dve", lambda e: e.tensor_tensor(out=dst_fn(), in0=SC[2][:, 0:tw], in1=SC[3][:, 0:tw], op=ALU.add),
                             reads=["S2", "S3"], writes=[dkey])

                bqa = proj_fm(wv01, wk01, 0, tw, split=True)
                bqas = proj_fm(wv23, wk23, 1, tw, split=True)
                rope(bqa, bqas, lambda: qab[0][:, 0:tw], ("q", 0), False)
                bqb = proj_fm(wv01, wk01, 1, tw, split=True)
                bqbs = proj_fm(wv45, wk45, 0, tw, split=True)
                rope(bqb, bqbs, lambda: qab[1][:, 0:tw], ("q", 1), False)
                bk = proj_fm(wv23, wk23, 0, tw)
                bks = proj_fm(wv45, wk45, 1, tw)
                rope(bk, bks, lambda: kT[l][:, 128:128 + tw], ("kT", l), True)
                if last:
                    nk_rows = DEC_SEQ if sample else 128
                    dst = (o_sk if sample else o_pk)[b_idx if not sample else 0][l]
                    state_out_T(SC[4][:, tw - nk_rows:tw], nk_rows, ["S4"], dst)

                chk('rope')
                wvD, wkD = wload(w_in_p[l][:, 1536:1792].rearrange("(k p) n -> p k n", p=128), (8, 256))
                for hf in range(2):
                    bu = proj_fm(wvD, wkD, hf, tw)
                    S.op("act", lambda e, bu=bu, hf=hf: e.activation(out=ubuf[:, hf, 0:tw], in_=banks[bu][:, 0:tw], func=AF.Gelu_apprx_tanh),
                         reads=[("ps", bu)], writes=[("u", hf)], tset="gelu")

                chk('pool')
                wvTv, wkTv = wload(w_in_p[l][:, 2048:2176].rearrange("(k p) n -> p k n", p=128), (8, 128))
                wvTg, wkTg = wload(w_in_p[l][:, 2176:2432].rearrange("(k p) n -> p k n", p=128), (8, 256))
                for tb in range(nblk):
                    bt = next_bank()
                    def tmfn(e, bt=bt, tb=tb):
                        ins = None
                        for k in range(8):
                            ins = e.matmul(banks[bt][0:ntok, 0:128], lhsT=hn[:, k, tb * 128:tb * 128 + ntok], rhs=wvTv[:, k, :],
                                           start=(k == 0), stop=(k == 7))
                        for k in range(8):
                            ins = e.matmul(banks[bt][0:ntok, 128:384], lhsT=hn[:, k, tb * 128:tb * 128 + ntok], rhs=wvTg[:, k, :],
                                           start=(k == 0), stop=(k == 7))
                        return ins
                    S.op("pe", tmfn, reads=HN + wkTv + wkTg, writes=[("ps", bt)], dur=8 * 0.07 + 8 * 0.12)
                    S.op("act", lambda e, bt=bt, tb=tb: e.activation(out=vtm[l][0:ntok, 1 + tb, :], in_=banks[bt][0:ntok, 0:128], func=AF.Copy),
                         reads=[("ps", bt)], writes=[("vtm", l)], n=128)
                    if last and (sample or tb == nblk - 1):
                        si = state["stg"]
                        state["stg"] = (si + 1) % 3
                        S.op("dve", lambda e, bt=bt, si=si: e.tensor_copy(out=stg[si][0:ntok, :], in_=banks[bt][0:ntok, 0:128]),
                             reads=[("ps", bt)], writes=[("stg", si)])
                        dst = (o_sv if sample else o_pv)[0 if sample else b_idx][l]
                        S.dma("sp", lambda e, si=si, dst=dst: e.dma_start(out=dst, in_=stg[si][0:ntok, :]), f"d_stg{si}",
                              reads=[("stg", si)], is_output=True)
                    gi = tb % 2
                    S.op("act", lambda e, bt=bt, gi=gi: e.activation(out=gv[gi][0:ntok, :], in_=banks[bt][0:ntok, 128:384], func=AF.Gelu_apprx_tanh),
                         reads=[("ps", bt)], writes=[("gv", gi)], n=256, tset="gelu")
                    S.op("dve", lambda e, gi=gi: e.bn_stats(out=st6[gi][0:ntok, :], in_=gv[gi][0:ntok, :]), reads=[("gv", gi)], writes=[("st6", gi)], n=256)
                    S.op("dve", lambda e, gi=gi: e.bn_aggr(out=mv[gi][0:ntok, :], in_=st6[gi][0:ntok, :]), reads=[("st6", gi)], writes=[("mv", gi)], n=30)
                    S.op("act", lambda e, gi=gi: e.activation(out=lnr[gi][0:ntok, 0:1], in_=mv[gi][0:ntok, 1:2], func=AF.Ln,
                                                             bias=colt[0:ntok, C_EPS:C_EPS + 1], scale=1.0),
                         reads=[("mv", gi), "colt"], writes=[("lnr", gi)], n=2, tset="lnexp")
                    S.op("act", lambda e, gi=gi: e.activation(out=lnr2[gi][0:ntok, 0:1], in_=lnr[gi][0:ntok, 0:1], func=AF.Exp, scale=-0.5),
                         reads=[("lnr", gi)], writes=[("lnr2", gi)], n=2, tset="lnexp")
                    S.op("dve", lambda e, gi=gi: e.tensor_scalar(out=vn[gi][0:ntok, :], in0=gv[gi][0:ntok, :], scalar1=mv[gi][0:ntok, 0:1],
                                                                scalar2=lnr2[gi][0:ntok, 0:1], op0=ALU.subtract, op1=ALU.mult),
                         reads=[("gv", gi), ("mv", gi), ("lnr2", gi)], writes=[("vn", gi)], n=140)
                    S.op("dve", lambda e, gi=gi: e.tensor_tensor(out=vn[gi][0:ntok, :], in0=vn[gi][0:ntok, :],
                                                                in1=bct[0:ntok, (l * 2) * 256:(l * 2 + 1) * 256], op=ALU.mult),
                         reads=[("vn", gi), "bct"], writes=[("vn", gi)], n=256)
                    S.op("dve", lambda e, gi=gi: e.tensor_tensor(out=vlnb[gi][0:ntok, :], in0=vn[gi][0:ntok, :],
                                                                in1=bct[0:ntok, (l * 2 + 1) * 256:(l * 2 + 2) * 256], op=ALU.add),
                         reads=[("vn", gi), "bct"], writes=[("vlnb", gi)], n=256)
                    if sample:
                        S.op("dve", lambda e, gi=gi: e.tensor_tensor(out=vlnf[0:ntok, :], in0=vn[gi][0:ntok, :],
                                                                    in1=bct[0:ntok, (l * 2 + 1) * 256:(l * 2 + 2) * 256], op=ALU.add),
                             reads=[("vn", gi), "bct"], writes=["vlnf"])
                        S.dma("sp", lambda e: e.dma_start(out=o_sgv[0][l], in_=vlnf[0:ntok, :]), "d_vlnf", reads=["vlnf"], is_output=True)
                    bg = next_bank()

                    def gfn(e, bg=bg, gi=gi):
                        ins = None
                        for g in range(4):
                            gp, gq = g // 2, g % 2
                            wo = (l * 4 + g) * 128
                            reg = lambda a, b_: banks[bg][gq * 64:(gq + 1) * 64, gp * 128 + a:gp * 128 + b_]
                            vcol = vlnb[gi]
                            if sample:
                                e.matmul(reg(0, 32), lhsT=ones_bf[0:1, 0:64], rhs=bsr[0:1, wo:wo + 32], start=True, stop=False)
                                ins = e.matmul(reg(0, 32), lhsT=vcol[0:32, g * 64:(g + 1) * 64], rhs=wmT[0:32, wo:wo + 32], start=False, stop=True)
                            else:
                                e.matmul(reg(0, 128), lhsT=ones_bf[0:1, 0:64], rhs=bsr[0:1, wo:wo + 128], start=True, stop=False)
                                e.matmul(reg(0, 64), lhsT=vcol[0:64, g * 64:(g + 1) * 64], rhs=wmT[0:64, wo:wo + 64], start=False, stop=False)
                                ins = e.matmul(reg(64, 128), lhsT=vcol[0:128, g * 64:(g + 1) * 64], rhs=wmT[0:128, wo + 64:wo + 128],
                                               start=False, stop=True)
                        return ins
                    S.op("pe", gfn, reads=[("vlnb", gi), "wmT", "bsr", "ones"], writes=[("ps", bg)], dur=0.8)
                    for gp in range(2):
                        S.op("dve", lambda e, bg=bg, gp=gp, tb=tb: e.tensor_tensor(out=mixT[:, 4 + gp, tb * 128:tb * 128 + ntok],
                                                                                 in0=banks[bg][:, gp * 128:gp * 128 + ntok],
                                                                                 in1=ubuf[:, gp, tb * 128:tb * 128 + ntok], op=ALU.mult),
                             reads=[("ps", bg), ("u", gp)], writes=[("mix", 4 + gp)], n=128)

                wvA, wkA = wload(w_in_p[l][:, 768:1024].rearrange("(k p) n -> p k n", p=128), (8, 256))
                wvB, wkB = wload(w_in_p[l][:, 1024:1280].rearrange("(k p) n -> p k n", p=128), (8, 256))
                wvC, wkC = wload(w_in_p[l][:, 1280:1536].rearrange("(k p) n -> p k n", p=128), (8, 256))
                for hf in range(2):
                    bh = proj_fm(wvC, wkC, hf, tw)
                    S.op("act", lambda e, bh=bh: e.activation(out=SC[5][:, 0:tw], in_=banks[bh][:, 0:tw], func=AF.Copy),
                         reads=[("ps", bh)], writes=["S5"])
                    bc_ = proj_fm(wvB, wkB, hf, tw)
                    S.op("dve", lambda e, bc_=bc_, hf=hf: e.tensor_tensor(out=prod[l][:, hf, 2:2 + tw], in0=banks[bc_][:, 0:tw],
                                                                         in1=SC[5][:, 0:tw], op=ALU.mult),
                         reads=[("ps", bc_), "S5"], writes=[("prod", l, hf)])
                    cwb = C_CONV + hf * 3
                    S.op("dve", lambda e, hf=hf, cwb=cwb: e.tensor_scalar(out=SC[6][:, 0:tw], in0=prod[l][:, hf, 2:2 + tw],
                                                                         scalar1=col(l, cwb, 2), scalar2=None, op0=ALU.mult),
                         reads=[("prod", l, hf), "colt"], writes=["S6"])
                    S.op("dve", lambda e, hf=hf, cwb=cwb: e.scalar_tensor_tensor(out=SC[6][:, 0:tw], in0=prod[l][:, hf, 1:1 + tw],
                                                                                scalar=col(l, cwb, 1), in1=SC[6][:, 0:tw],
                                                                                op0=ALU.mult, op1=ALU.add),
                         reads=[("prod", l, hf), "colt", "S6"], writes=["S6"])
                    S.op("dve", lambda e, hf=hf, cwb=cwb: e.scalar_tensor_tensor(out=SC[6][:, 0:tw], in0=prod[l][:, hf, 0:tw],
                                                                                scalar=col(l, cwb, 0), in1=SC[6][:, 0:tw],
                                                                                op0=ALU.mult, op1=ALU.add),
                         reads=[("prod", l, hf), "colt", "S6"], writes=["S6"])
                    bb = proj_fm(wvA, wkA, hf, tw)
                    S.op("dve", lambda e, bb=bb, hf=hf: e.tensor_tensor(out=mixT[:, 2 + hf, 0:tw], in0=banks[bb][:, 0:tw],
                                                                       in1=SC[6][:, 0:tw], op=ALU.mult),
                         reads=[("ps", bb), "S6"], writes=[("mix", 2 + hf)])
                if last:
                    S.op("dve", lambda e: e.tensor_copy(out=cstg[:].rearrange("p t h -> p h t"), in_=prod[l][:, :, tw:tw + 2]),
                         reads=[("prod", l, 0), ("prod", l, 1)], writes=["cstg"])
                    dst = (o_sconv if sample else o_pconv)[0 if sample else b_idx][l]
                    state_out_T(cstg[:].rearrange("p t h -> p (t h)"), 4, ["cstg"], dst)
                if not last:
                    S.op("dve", lambda e: e.tensor_copy(out=prod[l][:, :, 0:2], in_=prod[l][:, :, tw:tw + 2]),
                         reads=[("prod", l, 0), ("prod", l, 1)], writes=[("prod", l, 0), ("prod", l, 1)])

                chk('conv')
                chk('gu')
                wvE, wkE = wload(w_in_p[l][:, 1792:2048].rearrange("(k p) n -> p k n", p=128), (8, 256))
                for hf in range(2):
                    bp = proj_fm(wvE, wkE, hf, tw)
                    pk = ("pin", l, hf)
                    S.op("act", lambda e, bp=bp, hf=hf: e.activation(out=pin[l][:, hf, 15:15 + tw], in_=banks[bp][:, 0:tw], func=AF.Copy),
                         reads=[("ps", bp)], writes=[pk])
                    W = 15 + tw
                    P_ = pin[l]
                    s2, s4, s8, s16 = SC[7], SC[8], SC[9], SC[10]
                    S.op("dve", lambda e, hf=hf: e.tensor_tensor(out=s2[:, 1:W], in0=P_[:, hf, 1:W], in1=P_[:, hf, 0:W - 1], op=ALU.add),
                         reads=[pk], writes=["S7"])
                    S.op("dve", lambda e: e.tensor_tensor(out=s4[:, 3:W], in0=s2[:, 3:W], in1=s2[:, 1:W - 2], op=ALU.add),
                         reads=["S7"], writes=["S8"])
                    if hf == 1:
                        S.op("dve", lambda e: e.tensor_tensor(out=s8[:, 7:W], in0=s4[:, 7:W], in1=s4[:, 3:W - 4], op=ALU.add),
                             reads=["S8"], writes=["S9"])
                        S.op("dve", lambda e: e.tensor_tensor(out=s16[:, 15:W], in0=s8[:, 15:W], in1=s8[:, 7:W - 8], op=ALU.add),
                             reads=["S9"], writes=["S10"])
                        wlo, whi, klo, khi = s8, s16, "S9", "S10"
                    else:
                        wlo, whi, klo, khi = s2, s4, "S7", "S8"
                    pl = pooled[hf]
                    for (p0, p1, wsrc, wkey_) in ((0, 64, wlo, klo), (64, 128, whi, khi)):
                        S.op("dve", lambda e, p0=p0, p1=p1, wsrc=wsrc, hf=hf, pl=pl: e.scalar_tensor_tensor(
                            out=pl[p0:p1, 0:tw], in0=wsrc[p0:p1, 15:15 + tw], scalar=colt[p0:p1, C_INVW + hf:C_INVW + hf + 1],
                            in1=P_[p0:p1, hf, 15:15 + tw], op0=ALU.mult, op1=ALU.subtract),
                            reads=[wkey_, pk, "colt"], writes=[("pooled", hf)])
                        if first:
                            S.op("dve", lambda e, p0=p0, p1=p1, wsrc=wsrc, hf=hf: e.tensor_tensor(
                                out=ptmp[p0:p1, 0:16], in0=wsrc[p0:p1, 15:31], in1=colt[p0:p1, C_RC + hf * 16:C_RC + hf * 16 + 16], op=ALU.mult),
                                reads=[wkey_, "colt"], writes=["ptmp"])
                            S.op("dve", lambda e, p0=p0, p1=p1, hf=hf, pl=pl: e.tensor_tensor(
                                out=pl[p0:p1, 0:16], in0=ptmp[p0:p1, 0:16], in1=P_[p0:p1, hf, 15:31], op=ALU.subtract),
                                reads=["ptmp", pk], writes=[("pooled", hf)])
                    bpo = next_bank()
                    S.op("pe", lambda e, bpo=bpo, hf=hf, pl=pl: e.matmul(banks[bpo][:, 0:tw], lhsT=wpb[:, (l * 2 + hf) * 128:(l * 2 + hf + 1) * 128],
                                                               rhs=pl[:, 0:tw], start=True, stop=True),
                         reads=[("pooled", hf), "wpb"], writes=[("ps", bpo)])
                    S.op("act", lambda e, bpo=bpo, hf=hf: e.activation(out=mixT[:, 6 + hf, 0:tw], in_=banks[bpo][:, 0:tw], func=AF.Identity,
                                                                      scale=col(l, C_PSC, hf)),
                         reads=[("ps", bpo), "colt"], writes=[("mix", 6 + hf)])
                if last:
                    S.op("dve", lambda e: e.tensor_copy(out=pstg[:].rearrange("p t h -> p h t"), in_=pin[l][:, :, tw:tw + 15]),
                         reads=[("pin", l, 0), ("pin", l, 1)], writes=["pstg"])
                    dst = (o_spool if sample else o_ppool)[0 if sample else b_idx][l]
                    state_out_T(pstg[:].rearrange("p t h -> p (t h)"), 30, ["pstg"], dst)
                if not last:
                    S.op("dve", lambda e: e.tensor_copy(out=pin[l][:, :, 0:15], in_=pin[l][:, :, tw:tw + 15]),
                         reads=[("pin", l, 0), ("pin", l, 1)], writes=[("pin", l, 0), ("pin", l, 1)])

                chk('tokmajor')
                def attn_block(qb_):
                    nq = ntok
                    qc0 = qb_ * 128
                    if sample:
                        kbs = [(0, 0, 128, 0, 128, [(0, 128, 0, 32)]), (1, 128, 160, 1, 32, [(0, 32, 0, 32)])]
                    else:
                        kbs = []
                        if not (first and qb_ == 0):
                            kbs.append((0, qc0, qc0 + 128, qb_, 128, [(0, 128, 0, 64), (64, 128, 64, 128)]))
                        kbs.append((1, 128 + qc0, 256 + qc0, qb_ + 1, 128, [(0, 64, 0, 64), (0, 128, 64, 128)]))
                    for X in range(2):
                        pset = (qb_ * 2 + X) % 2
                        sb_ = [next_bank(), next_bank()]
                        for hh in range(2):
                            def sfn(e, hh=hh, sb_=sb_, X=X):
                                ins = None
                                for (kt, kc0, kc1, vb, nk, regs) in kbs:
                                    ins = e.matmul(banks[sb_[hh]][0:nk, kt * 128:kt * 128 + nq],
                                                   lhsT=kT[l][hh * 64:(hh + 1) * 64, kc0:kc1],
                                                   rhs=qab[X][hh * 64:(hh + 1) * 64, qc0:qc0 + nq], start=True, stop=True)
                                return ins
                            S.op("pe", sfn, reads=[("kT", l), ("q", X)], writes=[("ps", sb_[hh])], dur=0.15)
                            for (kt, kc0, kc1, vb, nk, regs) in kbs:
                                for (p0, p1, c0, c1) in regs:
                                    S.op("act", lambda e, hh=hh, kt=kt, p0=p0, p1=p1, c0=c0, c1=c1, sb_=sb_, pset=pset: e.activation(
                                        out=Pb[pset][hh][kt][p0:p1, c0:c1], in_=banks[sb_[hh]][p0:p1, kt * 128 + c0:kt * 128 + c1],
                                        func=AF.Exp, scale=0.125),
                                        reads=[("ps", sb_[hh])], writes=[("P", pset, hh, kt)], n=64, tset="lnexp")
                        ob = next_bank()

                        def ofn(e, ob=ob, pset=pset):
                            ins = None
                            for hh in range(2):
                                for i, (kt, kc0, kc1, vb, nk, regs) in enumerate(kbs):
                                    ins = e.matmul(banks[ob][hh * 64:(hh + 1) * 64, 0:nq], lhsT=vtm[l][0:nk, vb, hh * 64:(hh + 1) * 64],
                                                   rhs=Pb[pset][hh][kt][0:nk, 0:nq], start=(i == 0), stop=(i == len(kbs) - 1))
                                for i, (kt, kc0, kc1, vb, nk, regs) in enumerate(kbs):
                                    ins = e.matmul(banks[ob][hh * 64:(hh + 1) * 64, 128:128 + nq], lhsT=ones_bf[0:nk, 0:64],
                                                   rhs=Pb[pset][hh][kt][0:nk, 0:nq], start=(i == 0), stop=(i == len(kbs) - 1))
                            return ins
                        S.op("pe", ofn, reads=[("vtm", l), "ones"] + [("P", pset, hh, kt) for hh in range(2) for kt in range(2)],
                             writes=[("ps", ob)], dur=0.55)
                        ri = X
                        S.op("act", lambda e, ob=ob, ri=ri, X=X: e.activation(out=rec[ri][:, 0:nq], in_=banks[ob][:, 128:128 + nq], func=AF.Ln,
                                                                             bias=esink[:, 2 * l + X:2 * l + X + 1], scale=1.0),
                             reads=[("ps", ob), ("esink", l)], writes=[("rec", ri)], n=128, tset="lnexp")
                        S.op("act", lambda e, ri=ri: e.activation(out=rec2[ri][:, 0:nq], in_=rec[ri][:, 0:nq], func=AF.Exp, scale=-1.0),
                             reads=[("rec", ri)], writes=[("rec2", ri)], n=128, tset="lnexp")
                        S.op("dve", lambda e, ob=ob, ri=ri, X=X: e.tensor_tensor(out=mixT[:, X, qc0:qc0 + nq], in0=banks[ob][:, 0:nq],
                                                                                in1=rec2[ri][:, 0:nq], op=ALU.mult),
                             reads=[("ps", ob), ("rec2", ri)], writes=[("mix", X)], n=128)
                for qb__ in range(nblk):
                    attn_block(qb__)
                if not last:
                    S.op("act", lambda e: e.activation(out=kT[l][:, 0:128], in_=kT[l][:, tw:tw + 128], func=AF.Copy),
                         reads=[("kT", l)], writes=[("kT", l)])
                    S.op("dve", lambda e: e.tensor_copy(out=vtm[l][:, 0, :], in_=vtm[l][:, nblk, :]), reads=[("vtm", l)], writes=[("vtm", l)])

                chk('attn')
                MIX = [("mix", c) for c in range(8)]
                for n2 in range(4):
                    wvO, wkO = wload(w_out_p[l][:, n2 * 256:(n2 + 1) * 256].rearrange("(k p) n -> p k n", p=128), (8, 256))
                    for h2 in range(2):
                        n = n2 * 2 + h2
                        bo = next_bank()
                        mm_group(bo, lambda bo=bo: banks[bo][:, 0:tw], [wvO[:, k, h2 * 128:(h2 + 1) * 128] for k in range(8)],
                                 [mixT[:, k, 0:tw] for k in range(8)], reads=MIX + wkO)
                        S.op("dve", lambda e, bo=bo, n=n: e.tensor_tensor(out=xT[:, n, 0:tw], in0=banks[bo][:, 0:tw], in1=xT[:, n, 0:tw], op=ALU.add),
                             reads=[("ps", bo), XK(n)], writes=[XK(n)])

                chk('wout')
                rmsnorm(l, C_GFFN, 1, tw, xT, XK)
                chk('norm2')
                fcw = lo + C_FCW
                pending = []

                def ffn_A(j):
                    wvU, wkU = wload_up(l, j)
                    s_ = j % 2
                    Ub, cg, cvv, sg = ub2[s_], SC[6 + s_], SC[8 + s_], SC[10 + s_]
                    kU, kcg, kcv, ksg = f"U{s_}", f"S{6 + s_}", f"S{8 + s_}", f"S{10 + s_}"
                    hs = ((0, cg, kcg), (1, cvv, kcv))
                    bus = []
                    for (h, cb_, ck_) in hs:
                        bu = next_bank()
                        bus.append(bu)
                        if j < 2:
                            mm_split(bu, [wvU[:, h, k, :] for k in range(8)], tw, wkU)
                        else:
                            mm_group(bu, lambda bu=bu: banks[bu][:, 0:tw], [wvU[:, h, k, :] for k in range(8)],
                                     [hn[:, k, 0:tw] for k in range(8)], reads=HN + wkU)
                    S.op("act", lambda e: e.activation(out=Ub[:, :, 0:2], in_=fhalo[l][:, :, :, j].rearrange("p t h -> p h t"), func=AF.Copy),
                         reads=[("fhalo", l, j)], writes=[kU + "h"], n=4)
                    for (h, cb_, ck_), bu in zip(hs, bus):
                        jj = h * NJ + j
                        S.op("act", lambda e, h=h, bu=bu: e.activation(out=Ub[:, h, 2:2 + tw], in_=banks[bu][:, 0:tw], func=AF.Copy),
                             reads=[("ps", bu)], writes=[kU + str(h)])
                        S.op("act", lambda e, cb_=cb_, bu=bu, jj=jj: e.activation(out=cb_[:, 0:tw], in_=banks[bu][:, 0:tw], func=AF.Identity,
                                                                                scale=colt[:, fcw + jj * 3 + 2:fcw + jj * 3 + 3]),
                             reads=[("ps", bu), "colt"], writes=[ck_])
                    for tap in (1, 0):
                        for (h, cb_, ck_) in hs:
                            jj = h * NJ + j
                            S.op("dve", lambda e, h=h, cb_=cb_, jj=jj, tap=tap: e.scalar_tensor_tensor(
                                out=cb_[:, 0:tw], in0=Ub[:, h, tap:tap + tw], scalar=colt[:, fcw + jj * 3 + tap:fcw + jj * 3 + tap + 1],
                                in1=cb_[:, 0:tw], op0=ALU.mult, op1=ALU.add),
                                reads=[kU + str(h), kU + "h", ck_, "colt"], writes=[ck_])
                    S.op("dve", lambda e: e.tensor_copy(out=fhalo[l][:, :, :, j].rearrange("p t h -> p h t"), in_=Ub[:, :, tw:tw + 2]),
                         reads=[kU + "0", kU + "1"], writes=[("fhalo", l, j)], n=4)

                    def ffn_B():
                        S.op("act", lambda e: e.activation(out=sg[:, 0:tw], in_=cg[:, 0:tw], func=AF.Silu), reads=[kcg], writes=[ksg], tset="silu")
                        S.op("dve", lambda e: e.tensor_tensor(out=act[:, j, 0:tw], in0=sg[:, 0:tw], in1=cvv[:, 0:tw], op=ALU.mult),
                             reads=[ksg, kcv], writes=[("act", j)])
                    return ffn_B

                for j in range(NJ):
                    fb = ffn_A(j)
                    if pending:
                        pending.pop(0)()
                    pending.append(fb)
                while pending:
                    pending.pop(0)()
                if last:
                    dst = (o_sffn if sample else o_pffn)[0 if sample else b_idx][l]
                    state_out_T(fhalo[l][:].rearrange("p t h j -> p (t h j)"), 88, [("fhalo", l, jj_) for jj_ in range(NJ)], dst)
                chk('ffnup')
                ACTK = [("act", j) for j in range(NJ)]
                for n in range(8):
                    hk = NJ // 2
                    wvW0, wkW0 = wload(w_down[l][0:hk * 128, n * 128:(n + 1) * 128].rearrange("(k p) n -> p k n", p=128), (hk, 128))
                    wvW1, wkW1 = wload(w_down[l][hk * 128:DFF, n * 128:(n + 1) * 128].rearrange("(k p) n -> p k n", p=128), (hk, 128))
                    bd = next_bank()
                    lhs_all = [wvW0[:, k, :] for k in range(hk)] + [wvW1[:, k, :] for k in range(hk)]
                    for (k0, k1, wk_) in ((0, hk, wkW0), (hk, NJ, wkW1)):
                        def dfn(e, bd=bd, k0=k0, k1=k1, lhs_all=lhs_all):
                            ins = None
                            for k in range(k0, k1):
                                ins = e.matmul(banks[bd][:, 0:tw], lhsT=lhs_all[k], rhs=act[:, k, 0:tw], start=(k == 0), stop=(k == NJ - 1))
                            return ins
                        S.op("pe", dfn, reads=ACTK[k0:k1] + wk_, writes=[("ps", bd)], dur=(k1 - k0) * 0.218)
                    S.op("dve", lambda e, bd=bd, n=n: e.tensor_tensor(out=xT[:, n, 0:tw], in0=banks[bd][:, 0:tw], in1=xT[:, n, 0:tw], op=ALU.add),
                         reads=[("ps", bd), XK(n)], writes=[XK(n)])

            for l_ in range(DEPTH):
                run_layer(l_)
                chk('layer')

            for tb in range(nblk):
                bs2 = [next_bank(), next_bank()]
                fi = tb % 2
                for half in range(2):
                    def tfn2(e, half=half, bs2=bs2, tb=tb):
                        ins = None
                        for c4 in range(4):
                            c = half * 4 + c4
                            ins = e.transpose(out=banks[bs2[half]][0:ntok, c4 * 128:(c4 + 1) * 128], in_=xT[:, c, tb * 128:tb * 128 + ntok],
                                              identity=ident[:, :])
                        return ins
                    S.op("pe", tfn2, reads=[XK(half * 4 + c4) for c4 in range(4)] + ["ident"], writes=[("ps", bs2[half])], dur=0.5)
                    S.op("act", lambda e, half=half, bs2=bs2, fi=fi: e.activation(out=SC[half][0:ntok, 0:512], in_=banks[bs2[half]][0:ntok, 0:512],
                                                                                func=AF.Square, accum_out=fin_ss[fi][0:ntok, half:half + 1]),
                         reads=[("ps", bs2[half])], writes=[f"S{half}", ("fss", fi)])
                S.op("dve", lambda e, fi=fi: e.tensor_tensor(out=fin_ss[fi][0:ntok, 2:3], in0=fin_ss[fi][0:ntok, 0:1], in1=fin_ss[fi][0:ntok, 1:2], op=ALU.add),
                     reads=[("fss", fi)], writes=[("fss", fi)])
                S.op("act", lambda e, fi=fi: e.activation(out=fin_ss[fi][0:ntok, 3:4], in_=fin_ss[fi][0:ntok, 2:3], func=AF.Ln,
                                                         bias=colt[0:ntok, C_EPS:C_EPS + 1], scale=1.0 / D),
                     reads=[("fss", fi)], writes=[("fss", fi)], n=2, tset="lnexp")
                S.op("act", lambda e, fi=fi: e.activation(out=fin_r[fi][0:ntok, 0:1], in_=fin_ss[fi][0:ntok, 3:4], func=AF.Exp, scale=-0.5),
                     reads=[("fss", fi)], writes=[("finr", fi)], n=2, tset="lnexp")
                yi = state["yo"]
                state["yo"] = (yi + 1) % 2
                for half in range(2):
                    S.op("dve", lambda e, half=half, bs2=bs2, fi=fi, yi=yi: e.scalar_tensor_tensor(
                        out=yo[yi][0:ntok, half * 512:(half + 1) * 512], in0=banks[bs2[half]][0:ntok, 0:512], scalar=fin_r[fi][0:ntok, 0:1],
                        in1=bct[0:ntok, 1024 + half * 512:1024 + (half + 1) * 512], op0=ALU.mult, op1=ALU.mult),
                        reads=[("ps", bs2[half]), ("finr", fi), "bct"], writes=[("yo", yi)])
                S.dma("sp", lambda e, yi=yi, tb=tb: e.dma_start(out=ydst[0 if sample else b_idx][t0 + tb * 128:t0 + tb * 128 + ntok, :],
                                                              in_=yo[yi][0:ntok, :]),
                      f"d_yo{yi}", reads=[("yo", yi)], is_output=True)

        run_unit_inner = run_unit

        def run_unit(*a):
            run_unit_inner(*a)
            chk('unit')

        units = [("p", b, u) for b in range(2) for u in range(NUNIT)]
        try:
            chk("setup")
            for i, (k_, b_, u_) in enumerate(units):
                nxt = None
                if i + 1 < len(units):
                    nxt = (units[i + 1][1], units[i + 1][2])
                run_unit(k_, b_, u_, nxt)
            run_unit("s", 0, 0, None)
        except _Stop as ex:
            print("STOPPED at", ex)
        S.finish()
        _DBG["est"] = S.est_total
        _DBG["S"] = S
        _DBG["nops"] = len(S.ops)

        semkeys = set()
        for e_ in Sched.ENGS:
            for it in S.prog[e_]:
                if it[0] == "wait":
                    semkeys.add(it[1])
                else:
                    semkeys.add(it[2])
        sems = {}
        for k_ in sorted(semkeys):
            sems[k_] = es.enter_context(nc.semaphore(k_.replace("_", "")))

        def replay(eng_name):
            def body(e):
                for it in S.prog[eng_name]:
                    if it[0] == "wait":
                        e.wait_ge(sems[it[1]], it[2])
                    else:
                        ins = it[1](e)
                        ins.then_inc(sems[it[2]], it[3])
            return body

        with nc.Block() as block:
            block.sync(replay("sp"))
            block.tensor(replay("pe"))
            block.scalar(replay("act"))
            block.vector(replay("dve"))
            block.gpsimd(replay("pool"))
    return nc


_CACHE = {}


def kernel(**inp):
    inp = {k: np.asarray(v) for k, v in inp.items()}
    shared = _prep_shared(inp)
    if "nc" not in _CACHE:
        _CACHE["nc"] = build_program()
    nc = _CACHE["nc"]
    in_maps = []
    for i in range(NCORES):
        m = dict(shared)
        m["x_p"] = np.ascontiguousarray(inp["x_prompt"][2 * i:2 * i + 2])
        m["x_s"] = np.ascontiguousarray(inp["x_sample"][i:i + 1])
        m["ck"] = np.ascontiguousarray(inp["cache_attn_k"][i].reshape(DEPTH, 128, 128))
        m["cv"] = np.ascontiguousarray(inp["cache_attn_v"][i].reshape(DEPTH, 128, 128))
        m["st_conv"] = np.ascontiguousarray(inp["state_conv"][i])
        m["st_pool"] = np.ascontiguousarray(inp["state_pool"][i])
        m["st_ffn"] = np.ascontiguousarray(inp["state_ffn_conv"][i])
        in_maps.append(m)
    res = run_bass_kernel_spmd(nc, in_maps, core_ids=list(range(NCORES)))
    R = res.results

    def cat(name):
        return np.concatenate([np.asarray(r[name]) for r in R], axis=0)

    B, DB = 16, 8
    y_prompt = cat("y_p")
    y_sample = cat("y_s")
    pk = cat("o_pk").reshape(B, DEPTH, 128, 2, 64)
    pv = cat("o_pv").reshape(B, DEPTH, 128, 2, 64)
    pconv = cat("o_pconv").reshape(B, DEPTH, 2, 256)
    ppool = cat("o_ppool").reshape(B, DEPTH, 15, 256)
    pffn = cat("o_pffn").reshape(B, DEPTH, 2, 2 * DFF)
    sk = cat("o_sk").reshape(DB, DEPTH, DEC_SEQ, 2, 64)
    sv = cat("o_sv").reshape(DB, DEPTH, DEC_SEQ, 2, 64)
    sconv = cat("o_sconv").reshape(DB, DEPTH, 2, 256)
    spool = cat("o_spool").reshape(DB, DEPTH, 15, 256)
    sffn = cat("o_sffn").reshape(DB, DEPTH, 2, 2 * DFF)
    sgv = cat("o_sgv").reshape(DB, DEPTH, DEC_SEQ, 256)
    outs = (y_prompt, y_sample, pk, pv, pconv, ppool, pffn, sk, sv, sconv, spool, sffn, sgv)
    return tuple(np.ascontiguousarray(o, dtype=np.float32) for o in outs)
```
